# Optimizing a Trainium2 kernel written in Bass

```python
import math
import jax, jax.numpy as jnp
from jax import lax
import numpy as np

D_MODEL = 1024
BATCH = 8
SEQ = 2048
DEPTH = 4
DEC_BATCH = 128
DEC_SEQ = 4
PAST_LEN = 16384
PAGE_SIZE = 128

MIX_WIDTH = 2 * D_MODEL
CONV_WIDTH = 3 * MIX_WIDTH // 8
LRU_WIDTH = 3 * MIX_WIDTH // 8
MEM_WIDTH = MIX_WIDTH - CONV_WIDTH - LRU_WIDTH
LRU_HEADS = 8
LRU_HEAD_DIM = LRU_WIDTH // LRU_HEADS
MEM_HEADS = 4
MEM_HEAD_DIM = MEM_WIDTH // MEM_HEADS
N_MEM = 256
CONV_K = 31
LRU_CONV_K = 4
LRU_C = 8.0
EPS = 1e-6
IN_WIDTH = 3 * CONV_WIDTH + 2 * LRU_WIDTH + 2 * MEM_WIDTH

kernel_name = 'hybrid_conformer_rglru_memxattn_step'


def rmsnorm(x, g):
    xf = x.astype(jnp.float32)
    y = xf * lax.rsqrt(jnp.mean(xf * xf, axis=-1, keepdims=True) + EPS)
    return (y * g.astype(jnp.float32)).astype(x.dtype)


def layernorm(x, g, b):
    xf = x.astype(jnp.float32)
    mu = jnp.mean(xf, axis=-1, keepdims=True)
    var = jnp.mean(jnp.square(xf - mu), axis=-1, keepdims=True)
    y = (xf - mu) * lax.rsqrt(var + EPS)
    return (y * g.astype(jnp.float32) + b.astype(jnp.float32)).astype(x.dtype)


def depthwise_causal(xp, w, b):
    c = xp.shape[-1]
    y = lax.conv_general_dilated(xp, w[:, None, :].astype(xp.dtype), window_strides=(1,),
                                 padding='VALID', dimension_numbers=('NWC', 'WIO', 'NWC'),
                                 feature_group_count=c)
    return y + b


def linear_recurrence(a, bx, h0):
    bx = bx.at[:, 0].add(a[:, 0] * h0)
    def combine(l, r):
        return (l[0] * r[0], r[0] * l[1] + r[1])
    _, h = lax.associative_scan(combine, (a, bx), axis=1)
    return h


def memory_kv(mem, g, wk, wv):
    mn = rmsnorm(mem, g)
    b = mem.shape[0]
    k = (mn @ wk).reshape(b, N_MEM, MEM_HEADS, MEM_HEAD_DIM)
    v = (mn @ wv).reshape(b, N_MEM, MEM_HEADS, MEM_HEAD_DIM)
    return k, v


def mixer_layer(x, mem_k, mem_v, conv_buf, lru_buf, h0,
                g_pre, w_in, conv_w, conv_b, ln_g, ln_b,
                lru_conv_w, lru_conv_b, wa, ba, wx, bx, lam, w_out, g_post):
    b, t, _ = x.shape
    xn = rmsnorm(x, g_pre)
    proj = xn @ w_in
    s = np.cumsum([CONV_WIDTH, CONV_WIDTH, CONV_WIDTH, LRU_WIDTH, LRU_WIDTH, MEM_WIDTH])
    a_c, b_c, g_c, x_r, g_r, q, g_q = jnp.split(proj, [int(v) for v in s], axis=-1)

    u = a_c * jax.nn.sigmoid(b_c)
    up = jnp.concatenate([conv_buf.astype(u.dtype), u], axis=1)
    new_conv_buf = up[:, up.shape[1] - (CONV_K - 1):]
    c = layernorm(depthwise_causal(up, conv_w, conv_b), ln_g, ln_b)
    c = jax.nn.silu(c) * jax.nn.silu(g_c)

    xp = jnp.concatenate([lru_buf.astype(x_r.dtype), x_r], axis=1)
    new_lru_buf = xp[:, xp.shape[1] - (LRU_CONV_K - 1):]
    xc = depthwise_causal(xp, lru_conv_w, lru_conv_b)
    xh = xc.reshape(b, t, LRU_HEADS, LRU_HEAD_DIM)
    r = jax.nn.sigmoid(jnp.einsum('bthi,hij->bthj', xh, wa).reshape(b, t, LRU_WIDTH) + ba)
    ig = jax.nn.sigmoid(jnp.einsum('bthi,hij->bthj', xh, wx).reshape(b, t, LRU_WIDTH) + bx)
    log_a = -LRU_C * r.astype(jnp.float32) * jax.nn.softplus(-lam.astype(jnp.float32))
    a = jnp.exp(log_a)
    mult = jnp.sqrt(-jnp.expm1(2.0 * log_a))
    bx_t = mult * (ig * xc).astype(jnp.float32)
    h = linear_recurrence(a, bx_t, h0.astype(jnp.float32))
    new_h = h[:, -1]
    rr = h.astype(x.dtype) * jax.nn.silu(g_r)

    qh = q.reshape(b, t, MEM_HEADS, MEM_HEAD_DIM)
    sc = jnp.einsum('bthd,bmhd->bhtm', qh, mem_k).astype(jnp.float32) / math.sqrt(MEM_HEAD_DIM)
    p = jax.nn.softmax(sc, axis=-1).astype(x.dtype)
    o = jnp.einsum('bhtm,bmhd->bthd', p, mem_v).reshape(b, t, MEM_WIDTH) * jax.nn.silu(g_q)

    out = jnp.concatenate([c, rr, o], axis=-1) @ w_out
    y = x + rmsnorm(out, g_post)
    return y, new_conv_buf, new_lru_buf, new_h.astype(h0.dtype)


def setup_inputs(seed: int = 0) -> dict:
    key = jax.random.key(seed)
    ks = jax.random.split(key, 32)
    f = jnp.float32

    def nrm(k, shape, s):
        return jax.random.normal(k, shape, f) * s

    a0 = jax.random.uniform(ks[20], (DEPTH, LRU_WIDTH), f, 0.9, 0.999)
    return {
        'x_prompt': nrm(ks[0], (BATCH, SEQ, D_MODEL), 1.0),
        'x_sample': nrm(ks[1], (DEC_BATCH, DEC_SEQ, D_MODEL), 1.0),
        'mem_prompt': nrm(ks[2], (BATCH, N_MEM, D_MODEL), 1.0),
        'cache_conv': nrm(ks[3], (DEPTH, DEC_BATCH, CONV_K - 1, CONV_WIDTH), 0.5),
        'cache_lru_conv': nrm(ks[4], (DEPTH, DEC_BATCH, LRU_CONV_K - 1, LRU_WIDTH), 0.5),
        'state_lru_h': nrm(ks[5], (DEPTH, DEC_BATCH, LRU_WIDTH), 0.5),
        'cache_mem_k': nrm(ks[6], (DEPTH, DEC_BATCH, N_MEM, MEM_HEADS, MEM_HEAD_DIM), 1.0),
        'cache_mem_v': nrm(ks[7], (DEPTH, DEC_BATCH, N_MEM, MEM_HEADS, MEM_HEAD_DIM), 1.0),
        'norm_pre_g': 1.0 + nrm(ks[8], (DEPTH, D_MODEL), 0.05),
        'w_in': nrm(ks[9], (DEPTH, D_MODEL, IN_WIDTH), D_MODEL ** -0.5),
        'conv_w': nrm(ks[10], (DEPTH, CONV_K, CONV_WIDTH), CONV_K ** -0.5),
        'conv_b': nrm(ks[11], (DEPTH, CONV_WIDTH), 0.02),
        'conv_ln_g': 1.0 + nrm(ks[12], (DEPTH, CONV_WIDTH), 0.05),
        'conv_ln_b': nrm(ks[13], (DEPTH, CONV_WIDTH), 0.02),
        'lru_conv_w': nrm(ks[14], (DEPTH, LRU_CONV_K, LRU_WIDTH), LRU_CONV_K ** -0.5),
        'lru_conv_b': nrm(ks[15], (DEPTH, LRU_WIDTH), 0.02),
        'lru_wa': nrm(ks[16], (DEPTH, LRU_HEADS, LRU_HEAD_DIM, LRU_HEAD_DIM), LRU_HEAD_DIM ** -0.5),
        'lru_ba': nrm(ks[17], (DEPTH, LRU_WIDTH), 0.02),
        'lru_wx': nrm(ks[18], (DEPTH, LRU_HEADS, LRU_HEAD_DIM, LRU_HEAD_DIM), LRU_HEAD_DIM ** -0.5),
        'lru_bx': nrm(ks[19], (DEPTH, LRU_WIDTH), 0.02),
        'lru_lambda': jnp.log(a0) - jnp.log1p(-a0),
        'mem_norm_g': 1.0 + nrm(ks[21], (DEPTH, D_MODEL), 0.05),
        'w_mem_k': nrm(ks[22], (DEPTH, D_MODEL, MEM_WIDTH), D_MODEL ** -0.5),
        'w_mem_v': nrm(ks[23], (DEPTH, D_MODEL, MEM_WIDTH), D_MODEL ** -0.5),
        'w_out': nrm(ks[24], (DEPTH, MIX_WIDTH, D_MODEL), MIX_WIDTH ** -0.5),
        'norm_post_g': 1.0 + nrm(ks[25], (DEPTH, D_MODEL), 0.05),
    }


def reference(x_prompt, x_sample, mem_prompt, cache_conv, cache_lru_conv, state_lru_h,
              cache_mem_k, cache_mem_v, norm_pre_g, w_in, conv_w, conv_b, conv_ln_g, conv_ln_b,
              lru_conv_w, lru_conv_b, lru_wa, lru_ba, lru_wx, lru_bx, lru_lambda,
              mem_norm_g, w_mem_k, w_mem_v, w_out, norm_post_g):
    bp = x_prompt.shape[0]
    dt = x_prompt.dtype
    xp, xs = x_prompt, x_sample
    p_conv, p_lconv, p_h, p_mk, p_mv = [], [], [], [], []
    s_conv, s_lconv, s_h = [], [], []
    for l in range(DEPTH):
        w = (norm_pre_g[l], w_in[l], conv_w[l], conv_b[l], conv_ln_g[l], conv_ln_b[l],
             lru_conv_w[l], lru_conv_b[l], lru_wa[l], lru_ba[l], lru_wx[l], lru_bx[l],
             lru_lambda[l], w_out[l], norm_post_g[l])
        mk, mv = memory_kv(mem_prompt, mem_norm_g[l], w_mem_k[l], w_mem_v[l])
        xp, cb, lb, hh = mixer_layer(
            xp, mk, mv,
            jnp.zeros((bp, CONV_K - 1, CONV_WIDTH), dt),
            jnp.zeros((bp, LRU_CONV_K - 1, LRU_WIDTH), dt),
            jnp.zeros((bp, LRU_WIDTH), dt), *w)
        p_conv.append(cb); p_lconv.append(lb); p_h.append(hh); p_mk.append(mk); p_mv.append(mv)
        xs, cb2, lb2, hh2 = mixer_layer(
            xs, cache_mem_k[l], cache_mem_v[l], cache_conv[l], cache_lru_conv[l],
            state_lru_h[l], *w)
        s_conv.append(cb2); s_lconv.append(lb2); s_h.append(hh2)
    return (xp, xs,
            jnp.stack(p_conv), jnp.stack(p_lconv), jnp.stack(p_h), jnp.stack(p_mk), jnp.stack(p_mv),
            jnp.stack(s_conv), jnp.stack(s_lconv), jnp.stack(s_h))
```

```python
import math
import numpy as np
import concourse.bass as bass
import concourse.mybir as mybir
from concourse.bass_utils import run_bass_kernel_spmd

F32 = mybir.dt.float32
BF16 = mybir.dt.bfloat16
AF = mybir.ActivationFunctionType
ALU = mybir.AluOpType

D = 1024
KD = 8
CW = 768
LW = 768
MW = 512
NH = 4
NMEM = 256
CK = 31
HK = CK - 1
LK = 4
LH = LK - 1
INW = 4864
EPS = 1e-6
SEGS = [("a", 0, 6), ("b", 768, 6), ("gc", 1536, 6), ("xr", 2304, 6), ("gr", 3072, 6), ("q", 3840, 4), ("gq", 4352, 4)]
NP7 = 42
R_CW, R_CB, R_LNG, R_LNB, R_LCW, R_LCB, R_BA, R_BX, R_LAM = 0, 31, 32, 33, 34, 38, 39, 40, 41
TS = 4


def gate_blocks():
    out = []
    for mo in range(6):
        hlo = (128 * mo) // 96
        hhi = (128 * mo + 127) // 96
        klo = (96 * hlo) // 128
        khi = (96 * hhi + 95) // 128
        for kc in range(klo, khi + 1):
            out.append((mo, kc))
    return out


GBLK = gate_blocks()
NBLK = len(GBLK)


class StopEmit(Exception):
    pass


class Buf:
    __slots__ = ("w", "r", "name", "excl")

    def __init__(self, name="", excl=False):
        self.w = None
        self.r = []
        self.name = name
        self.excl = excl


class Eng:
    def __init__(self, nc, e, name):
        self.e = e
        self.name = name
        self.sem = nc.alloc_semaphore("es_" + name)
        self.cnt = 0
        self.seen = {}


class Tracker:
    def __init__(self, nc, ndma=56):
        self.nc = nc
        self.pe = Eng(nc, nc.tensor, "pe")
        self.act = Eng(nc, nc.scalar, "act")
        self.dve = Eng(nc, nc.vector, "dve")
        self.pool = Eng(nc, nc.gpsimd, "pool")
        self.sp = Eng(nc, nc.sync, "sp")
        self.dpools = {}
        for nm, cnt in (("sp", ndma // 2), ("pool", ndma // 2)):
            self.dpools[nm] = {"sems": [nc.alloc_semaphore(f"ds_{nm}{i}") for i in range(cnt)], "cnt": [0] * cnt, "next": 0}
        self.out_events = []
        import os
        self.nops = 0
        self.maxops = int(os.environ.get("STOPN", "100000000"))
        self.log = []

    def _waits(self, E, reads, writes):
        self.nops += 1
        if self.nops > self.maxops:
            raise StopEmit()
        need = {}

        def add(ev, raw):
            sem, val, eng = ev
            if eng is E and E is not self.pool:
                if not raw or E is self.pe:
                    return
            k = id(sem)
            if k not in need or need[k][1] < val:
                need[k] = (sem, val)

        for b in reads:
            if b.w is not None:
                add(b.w, True)
            if b.excl:
                for ev in b.r:
                    add(ev, False)
        for b in writes:
            if b.w is not None:
                add(b.w, False)
            for ev in b.r:
                add(ev, False)
        for k, (sem, val) in need.items():
            if E.seen.get(k, 0) < val:
                E.e.wait_ge(sem, val)
                E.seen[k] = val

    def _commit(self, ev, reads, writes):
        for b in writes:
            b.w = ev
            b.r = []
        for b in reads:
            b.r = [x for x in b.r if x[0] is not ev[0]]
            b.r.append(ev)

    def op(self, E, fn, reads=(), writes=()):
        self._waits(E, reads, writes)
        ins = fn(E.e)
        E.cnt += 1
        ins.then_inc(E.sem, 1)
        self._commit((E.sem, E.cnt, E), reads, writes)

    def group(self, E, fns, reads=(), writes=()):
        self._waits(E, reads, writes)
        ins = None
        for fn in fns:
            ins = fn(E.e)
        E.cnt += 1
        ins.then_inc(E.sem, 1)
        self._commit((E.sem, E.cnt, E), reads, writes)

    def dma(self, Q, out, in_, reads=(), writes=(), is_output=False, slow=False):
        self._waits(Q, reads, writes)
        dp = self.dpools[Q.name]
        i = dp["next"]
        dp["next"] = (i + 1) % len(dp["sems"])
        if dp["cnt"][i] > 0 and Q.seen.get(id(dp["sems"][i]), 0) < dp["cnt"][i]:
            Q.e.wait_ge(dp["sems"][i], dp["cnt"][i])
            Q.seen[id(dp["sems"][i])] = dp["cnt"][i]
        if slow:
            Q.e.dma_start(out=out, in_=in_, allow_slow_non_contiguous=True).then_inc(dp["sems"][i], 16)
        else:
            Q.e.dma_start(out=out, in_=in_).then_inc(dp["sems"][i], 16)
        dp["cnt"][i] += 16
        ev = (dp["sems"][i], dp["cnt"][i], None)
        self._commit(ev, reads, writes)
        if is_output:
            self.out_events.append(ev)

    def finish(self):
        E = self.sp
        for dp in self.dpools.values():
            for sem, val in zip(dp["sems"], dp["cnt"]):
                if val > 0:
                    E.e.wait_ge(sem, val)
        for G in (self.pe, self.act, self.dve, self.pool):
            if G.cnt > 0:
                E.e.wait_ge(G.sem, G.cnt)


class Cfg:
    def __init__(self, L=4, SEQ=2048, NB=16, T=256, NPE=0, upto=99):
        self.L, self.SEQ, self.NB, self.T, self.NPE = L, SEQ, NB, T, NPE
        self.upto = upto
        self.ndma = 56
        self.NS = NB * TS
        assert SEQ % T == 0 and T >= HK and self.NS <= T


def build(cfg):
    L, SEQ, NB, T = cfg.L, cfg.SEQ, cfg.NB, cfg.T
    NS = cfg.NS
    SL = T
    nc = bass.Bass("TRN2", target_bir_lowering=False)

    def din(name, shape):
        return nc.dram_tensor(name, list(shape), F32, kind="ExternalInput").ap()

    def dout(name, shape):
        return nc.dram_tensor(name, list(shape), F32, kind="ExternalOutput").ap()

    xpT = din("xpT", [D, SEQ]); xsT = din("xsT", [D, NS]); memT = din("memT", [D, NMEM])
    cconvT = din("cconvT", [L, CW, NB, HK]); clruT = din("clruT", [L, LW, NB, LH]); h0T = din("h0T", [L, LW, NB])
    ckT = din("ckT", [L, NB, MW, NMEM]); cv = din("cv", [L, NB, NMEM, MW])
    w_in = din("w_in", [L, D, INW]); w_out = din("w_out", [L, 2 * D, D])
    wk = din("wk", [L, D, MW]); wv = din("wv", [L, D, MW])
    wab = din("wab", [L, 128, NBLK, 128]); wxb = din("wxb", [L, 128, NBLK, 128])
    p768 = din("p768", [L, CW, NP7]); p1024 = din("p1024", [L, D, 3])
    ident_d = din("ident", [128, 128])
    DGd = nc.dram_tensor("dgscr", [L, 6, 128, CK * 128], BF16, kind="Internal").ap()
    DGb = [[Buf(f"dg{l}_{j}") for j in range(6)] for l in range(L)]
    DGLd = nc.dram_tensor("dglscr", [L, 128, 6 * LK * 128], BF16, kind="Internal").ap()
    DGLb = [Buf(f"dgl{l}") for l in range(L)]

    ypT = dout("ypT", [D, SEQ]); ysT = dout("ysT", [D, NS])
    pconvT = dout("pconvT", [L, CW, HK]); plconvT = dout("plconvT", [L, LW, LH]); phT = dout("phT", [L, LW, 1])
    pmkT = dout("pmkT", [L, MW, NMEM]); pmv = dout("pmv", [L, NMEM, MW])
    sconvT = dout("sconvT", [L, CW, NB, HK]); slconvT = dout("slconvT", [L, LW, NB, LH]); shT = dout("shT", [L, LW, NB])
    xscr = [nc.dram_tensor(f"xscr{i}", [D, SEQ + NS], F32, kind="Internal").ap() for i in range(2)]

    tk = Tracker(nc, ndma=cfg.ndma)
    PE, ACT, DVE, POOL, SP = tk.pe, tk.act, tk.dve, tk.pool, tk.sp

    def sb(name, shape, dt):
        return nc.alloc_sbuf_tensor(name, list(shape), dt)

    Wt = {}; Wb = {}
    for (nm, c0, ncs) in SEGS:
        Wt[nm] = sb("W_" + nm, [128, KD, ncs * 128], BF16); Wb[nm] = Buf("W_" + nm)
    WORt = [sb(f"WOR{i}", [128, 16, 256], BF16) for i in range(3)]
    WORb = [Buf(f"WOR{i}") for i in range(3)]
    wor_state = {"i": 0}
    WAt = sb("WA", [128, NBLK, 128], BF16); WAb = Buf("WA")
    WXt = sb("WX", [128, NBLK, 128], BF16); WXb = Buf("WX")
    P7t = [sb(f"P7_{i}", [128, 6, NP7], F32) for i in range(2)]; P7b = [Buf(), Buf()]
    P10t = [sb(f"P10_{i}", [128, KD, 3], F32) for i in range(2)]; P10b = [Buf(), Buf()]
    CVt = sb("CV", [128, 6], F32); CVb = Buf("CV")
    CVtmp = sb("CVtmp", [128, 6], F32); CVtmpb = Buf("CVtmp")
    NEGt = sb("NEGP", [128, 6, 4], F32); NEGb = Buf("NEGP")
    ONESt = sb("ONES", [128, 128], BF16); ONESb = Buf("ONES")
    EPSt = sb("EPSc", [128, 1], F32); ONEt = sb("ONEc", [128, 1], F32); CONSTb = Buf("const")
    KTPt = sb("KTP", [128, NH, NMEM], BF16); KTPb = Buf("KTP")
    VPt = sb("VP", [128, 2, MW], BF16); VPb = Buf("VP")
    HISTCt = sb("HISTC", [128, 6, HK], F32); HISTCb = [Buf() for _ in range(3)]
    HISTLt = sb("HISTL", [128, 6, LH], F32); HISTLb = [Buf() for _ in range(3)]
    HSTt = sb("HST", [128, 6], F32); HSTb = [Buf() for _ in range(6)]
    XBt = [sb(f"XB{i}", [128, KD, T], F32) for i in range(2)]; XBb = [Buf(), Buf()]
    XNt = sb("XN", [128, KD, T], BF16); XNb = Buf("XN")
    CATt = sb("CAT", [128, 16, T], BF16); CATb = [Buf(f"cat{i}") for i in range(16)]
    IDENTt = sb("IDENT", [128, 128], BF16); IDENTb = Buf("IDENT")
    DGRt = [sb(f"DGR{i}", [128, CK, 128], BF16) for i in range(2)]; DGRb = [Buf("dgr0"), Buf("dgr1")]
    dgr_state = {"i": 0}
    NSLOT = 49
    TMt = sb("TM", [128, NSLOT * SL], F32)
    TMb = [Buf(f"tm{i}") for i in range(NSLOT)]
    PBt = [nc.alloc_psum_tensor(f"PB{i}", [128, 512], F32) for i in range(8)]
    PBb = [Buf(f"pb{i}", excl=True) for i in range(8)]
    pstate = {"i": 0}

    def bank():
        i = pstate["i"]
        pstate["i"] = (i + 1) % 7
        return PBt[i], [PBb[i]]

    class TV:
        def __init__(self, s0, nel_f32, eoff=0):
            self.e0 = s0 * SL + eoff
            self.nel = nel_f32
            sa = self.e0 // SL
            s1 = (self.e0 + nel_f32 + SL - 1) // SL
            assert s1 <= NSLOT, (s0, nel_f32)
            self.b = TMb[sa:s1]

        def f(self):
            return TMt[:, self.e0:self.e0 + self.nel]

        def h(self):
            return TMt[:, self.e0:self.e0 + self.nel].bitcast(BF16)

    tk.op(POOL, lambda e: e.memset(ONESt[:], 1.0), [], [ONESb])
    tk.op(POOL, lambda e: e.memset(EPSt[:], EPS), [], [CONSTb])
    tk.op(POOL, lambda e: e.memset(ONEt[:], 1.0), [], [CONSTb])
    tk.dma(POOL, IDENTt[:], ident_d, [], [IDENTb])

    def load_weights(l):
        for (nm, c0, ncs) in SEGS:
            tk.dma(POOL, Wt[nm][:], w_in[l, :, c0:c0 + ncs * 128].rearrange("(k p) c -> p k c", p=128), [], [Wb[nm]])
        tk.dma(POOL, WAt[:], wab[l], [], [WAb])
        tk.dma(POOL, WXt[:], wxb[l], [], [WXb])

    def load_params(l):
        pi = l % 2
        tk.dma(SP, P7t[pi][:], p768[l].rearrange("(j p) r -> p j r", p=128), [], [P7b[pi]])
        tk.dma(SP, P10t[pi][:], p1024[l].rearrange("(j p) r -> p j r", p=128), [], [P10b[pi]])

    def layer_consts(l):
        pi = l % 2
        P7 = P7t[pi]
        tk.op(ACT, lambda e: e.activation(out=CVtmp[:], in_=P7[:, :, R_LAM], func=AF.Exp, scale=-1.0), [P7b[pi]], [CVtmpb])
        tk.op(ACT, lambda e: e.activation(out=CVtmp[:], in_=CVtmp[:], func=AF.Ln, bias=ONEt[:], scale=1.0), [CVtmpb, CONSTb], [CVtmpb])
        tk.op(DVE, lambda e: e.tensor_scalar(out=CVt[:], in0=CVtmp[:], scalar1=-8.0, scalar2=None, op0=ALU.mult), [CVtmpb], [CVb])
        tk.op(DVE, lambda e: e.tensor_scalar(out=NEGt[:, :, 0:2], in0=P7[:, :, R_BA:R_BA + 2], scalar1=-1.0, scalar2=None, op0=ALU.mult), [P7b[pi]], [NEGb])
        tk.op(DVE, lambda e: e.tensor_scalar(out=NEGt[:, :, 2:4], in0=P7[:, :, R_LNG:R_LNG + 2], scalar1=-1.0, scalar2=None, op0=ALU.mult), [P7b[pi]], [NEGb])

    def diag_chunks(l, js):
        pi = l % 2
        for j in js:
            r = dgr_state["i"]; dgr_state["i"] = 1 - r
            tk.op(POOL, lambda e, j=j, r=r: e.tensor_tensor(out=DGRt[r][:], in0=IDENTt[:].unsqueeze(1).broadcast_to([128, CK, 128]),
                                                          in1=P7t[pi][:, j, R_CW:R_CW + CK].unsqueeze(2).broadcast_to([128, CK, 128]), op=ALU.mult),
                  [IDENTb, P7b[pi]], [DGRb[r]])
            tk.dma(SP, DGd[l, j], DGRt[r][:].rearrange("p k c -> p (k c)"), [DGRb[r]], [DGb[l][j]])
        if 5 in js:
            r = dgr_state["i"]; dgr_state["i"] = 1 - r
            for j in range(6):
                tk.op(POOL, lambda e, j=j, r=r: e.tensor_tensor(out=DGRt[r][:, j * LK:(j + 1) * LK, :], in0=IDENTt[:].unsqueeze(1).broadcast_to([128, LK, 128]),
                                                              in1=P7t[pi][:, j, R_LCW:R_LCW + LK].unsqueeze(2).broadcast_to([128, LK, 128]), op=ALU.mult),
                      [IDENTb, P7b[pi]], [DGRb[r]])
            tk.dma(SP, DGLd[l], DGRt[r][:, 0:6 * LK, :].rearrange("p k c -> p (k c)"), [DGRb[r]], [DGLb[l]])

    def mem_phase(l):
        pi = l % 2
        P10 = P10t[pi]
        MEM = TV(0, 8 * SL); MSQ = TV(8, 4 * SL); MN = TV(8, 4 * SL); RM = TV(12, NMEM)
        WKv = TV(13, 8 * SL); WVv = TV(21, 8 * SL); OF = [TV(29, 2 * SL), TV(31, 2 * SL)]
        assert NMEM == SL
        memf = MEM.f().rearrange("p (k m) -> p k m", k=KD)
        msq = MSQ.h().rearrange("p (k m) -> p k m", k=KD)
        mn = MN.h().rearrange("p (k m) -> p k m", k=KD)
        wkv = WKv.h().rearrange("p (k c) -> p k c", k=KD)
        wvv = WVv.h().rearrange("p (k c) -> p k c", k=KD)
        tk.dma(SP, memf, memT.rearrange("(k p) m -> p k m", p=128), [], MEM.b)
        tk.dma(POOL, wkv, wk[l].rearrange("(k p) c -> p k c", p=128), [], WKv.b)
        tk.dma(POOL, wvv, wv[l].rearrange("(k p) c -> p k c", p=128), [], WVv.b)
        tk.op(ACT, lambda e: e.activation(out=msq, in_=memf, func=AF.Square), MEM.b, MSQ.b)
        pb, pbb = bank()
        tk.group(PE, [(lambda e, kc=kc: e.matmul(pb[:, 0:NMEM], lhsT=ONESt[:], rhs=msq[:, kc, :], start=(kc == 0), stop=(kc == KD - 1)))
                      for kc in range(KD)], MSQ.b + [ONESb], pbb)
        tk.op(ACT, lambda e: e.activation(out=RM.f(), in_=pb[:, 0:NMEM], func=AF.Ln, bias=EPSt[:], scale=1.0 / D), pbb + [CONSTb], RM.b)
        tk.op(ACT, lambda e: e.activation(out=RM.f(), in_=RM.f(), func=AF.Exp, scale=-0.5), RM.b, RM.b)
        for kc in range(KD):
            tk.op(DVE, lambda e, kc=kc: e.scalar_tensor_tensor(out=mn[:, kc, :], in0=memf[:, kc, :], scalar=P10[:, kc, 2:3], in1=RM.f(),
                                                             op0=ALU.mult, op1=ALU.mult), MEM.b + RM.b + [P10b[pi]], MN.b)
        for hp in range(2):
            pb, pbb = bank()
            for s in range(2):
                h = 2 * hp + s
                tk.group(PE, [(lambda e, kc=kc, h=h, s=s: e.matmul(pb[:, s * NMEM:(s + 1) * NMEM], lhsT=wkv[:, kc, h * 128:(h + 1) * 128],
                                                                rhs=mn[:, kc, :], start=(kc == 0), stop=(kc == KD - 1))) for kc in range(KD)],
                         WKv.b + MN.b, pbb)
            tk.op(ACT, lambda e, hp=hp: e.activation(out=KTPt[:, 2 * hp:2 * hp + 2, :], in_=pb[:, :].rearrange("p (s m) -> p s m", s=2), func=AF.Copy),
                  pbb, [KTPb])
            of = OF[hp % 2]
            import os
            dbg = os.environ.get("DBG", "")
            if "A" in dbg:
                tk.op(ACT, lambda e: e.activation(out=of.f(), in_=pb[:, :], func=AF.Copy), pbb, of.b)
            elif "S" in dbg:
                tk.op(DVE, lambda e: e.tensor_copy(out=of.f()[:, 0:256], in_=RM.f()), pbb + RM.b, of.b)
            elif "N" in dbg:
                tk.op(DVE, lambda e: e.tensor_copy(out=of.f(), in_=pb[:, :]), [], of.b)
            elif "T" in dbg:
                tk.op(DVE, lambda e: e.tensor_scalar(out=of.f(), in0=pb[:, :], scalar1=1.0, scalar2=None, op0=ALU.mult), pbb, of.b)
            elif "H" in dbg:
                tk.op(DVE, lambda e: e.tensor_copy(out=of.f()[:, 0:256], in_=pb[:, 0:256]), pbb, of.b)
                tk.op(DVE, lambda e: e.tensor_copy(out=of.f()[:, 256:512], in_=pb[:, 256:512]), pbb, of.b)
            else:
                tk.op(DVE, lambda e: e.tensor_copy(out=of.f(), in_=pb[:, :]), pbb, of.b)
            tk.dma(SP, pmkT[l, hp * 256:(hp + 1) * 256, :].rearrange("(s p) m -> p s m", p=128), of.f().rearrange("p (s m) -> p s m", s=2),
                   of.b, [], is_output=True)
        for mc in range(2):
            pb, pbb = bank()
            tk.group(PE, [(lambda e, kc=kc, mc=mc: e.matmul(pb[:, :], lhsT=mn[:, kc, mc * 128:(mc + 1) * 128], rhs=wvv[:, kc, :],
                                                          start=(kc == 0), stop=(kc == KD - 1))) for kc in range(KD)], WVv.b + MN.b, pbb)
            tk.op(ACT, lambda e, mc=mc: e.activation(out=VPt[:, mc, :], in_=pb[:, :], func=AF.Copy), pbb, [VPb])
            of = OF[mc % 2]
            tk.op(DVE, lambda e: e.tensor_copy(out=of.f(), in_=pb[:, :]), pbb, of.b)
            tk.dma(SP, pmv[l, mc * 128:(mc + 1) * 128, :], of.f(), of.b, [], is_output=True)

    class Grp:
        pass

    def make_groups():
        gs = []
        for ti in range(SEQ // T):
            g = Grp(); g.kind = "p"; g.n = T; g.nseq = 1; g.tlen = T; g.c0 = ti * T
            g.first = (ti == 0); g.last = (ti == SEQ // T - 1); g.idx = ti
            gs.append(g)
        g = Grp(); g.kind = "s"; g.n = NS; g.nseq = NB; g.tlen = TS; g.c0 = SEQ; g.first = True; g.last = True; g.idx = SEQ // T
        gs.append(g)
        return gs

    groups = make_groups()
    NG = len(groups)
    scrb = [[Buf() for _ in range(NG)] for _ in range(2)]
    xslot = {"i": 0}

    def proj_pair(seg, j0, n, npair=2):
        pb, pbb = bank()
        for s in range(npair):
            j = j0 + s
            tk.group(PE, [(lambda e, kc=kc, j=j, s=s: e.matmul(pb[:, s * n:(s + 1) * n], lhsT=Wt[seg][:, kc, j * 128:(j + 1) * 128],
                                                             rhs=XNt[:, kc, 0:n], start=(kc == 0), stop=(kc == KD - 1))) for kc in range(KD)],
                     [Wb[seg], XNb], pbb)
        return pb, pbb

    def rsqrt_act(out_tv, in_ap, rd, scale):
        tk.op(ACT, lambda e: e.activation(out=out_tv.f(), in_=in_ap, func=AF.Ln, bias=EPSt[:], scale=scale), rd + [CONSTb], out_tv.b)
        tk.op(ACT, lambda e: e.activation(out=out_tv.f(), in_=out_tv.f(), func=AF.Exp, scale=-0.5), out_tv.b, out_tv.b)

    def sigmoid_act(out_ap, out_b, in_ap, rd, nscale=-1.0, nbias=None):
        if nbias is None:
            tk.op(ACT, lambda e: e.activation(out=out_ap, in_=in_ap, func=AF.Exp, scale=nscale), rd, out_b)
        else:
            tk.op(ACT, lambda e: e.activation(out=out_ap, in_=in_ap, func=AF.Exp, bias=nbias, scale=nscale), rd, out_b)
        tk.op(ACT, lambda e: e.activation(out=out_ap, in_=out_ap, func=AF.Ln, bias=ONEt[:], scale=1.0), out_b + [CONSTb], out_b)
        tk.op(ACT, lambda e: e.activation(out=out_ap, in_=out_ap, func=AF.Exp, scale=-1.0), out_b, out_b)

    def stage_A(l, g):
        pi = l % 2
        n = g.n
        xi = xslot["i"]; xslot["i"] = 1 - xi
        g.xi = xi
        X = XBt[xi]
        if l == 0:
            src = (xpT[:, g.c0:g.c0 + n] if g.kind == "p" else xsT[:, 0:n])
            rd = []
        else:
            src = xscr[(l - 1) % 2][:, g.c0:g.c0 + n]
            rd = [scrb[(l - 1) % 2][g.idx]]
        tk.dma(SP, X[:, :, 0:n], src.rearrange("(k p) t -> p k t", p=128), rd, [XBb[xi]])
        SQ = TV(0, 4 * SL); RT = TV(4, n)
        sq = SQ.h().rearrange("p (k t) -> p k t", k=KD)
        tk.op(ACT, lambda e: e.activation(out=sq[:, :, 0:n], in_=X[:, :, 0:n], func=AF.Square), [XBb[xi]], SQ.b)
        pb, pbb = bank()
        tk.group(PE, [(lambda e, kc=kc: e.matmul(pb[:, 0:n], lhsT=ONESt[:], rhs=sq[:, kc, 0:n], start=(kc == 0), stop=(kc == KD - 1)))
                      for kc in range(KD)], SQ.b + [ONESb], pbb)
        rsqrt_act(RT, pb[:, 0:n], pbb, 1.0 / D)
        for kc in range(KD):
            tk.op(DVE, lambda e, kc=kc: e.scalar_tensor_tensor(out=XNt[:, kc, 0:n], in0=X[:, kc, 0:n], scalar=P10t[pi][:, kc, 0:1], in1=RT.f(),
                                                             op0=ALU.mult, op1=ALU.mult), [XBb[xi], P10b[pi]] + RT.b, [XNb])

    def stage_B(l, g, ck=lambda lv: None, after_conv=None, after_lru=None):
        pi = l % 2
        g.wo = {}
        for dp_ in range(3):
            wo_fetch(l, g, dp_)
        P7 = P7t[pi]; P7B = P7b[pi]
        n, nseq, tlen = g.n, g.nseq, g.tlen
        isp = (g.kind == "p")

        def v3(ap2):
            return ap2.rearrange("p (s t) -> p s t", s=nseq)

        ulen = HK + tlen
        SIGs = [TV(5 + 2 * i, 2 * n) for i in range(3)]
        UPBs = [TV(11 + 3 * i, nseq * ulen) for i in range(3)]
        UP = TV(20, 2 * nseq * ulen)
        CS = TV(25, 6 * n)
        CSB = TV(31, n); CSQ = TV(32, n)
        MEAN = TV(5, n); MSQv = TV(6, n); VAR = TV(7, n)
        cs = CS.f().rearrange("p (j t) -> p j t", j=6)
        up = UP.f().rearrange("p (j s u) -> p j s u", j=2, s=nseq)
        csb = CSB.h().rearrange("p (j t) -> p j t", j=2)
        csq = CSQ.h().rearrange("p (j t) -> p j t", j=2)
        upbs = [u_.h().rearrange("p (j s u) -> p j s u", j=2, s=nseq) for u_ in UPBs]
        for jp in range(3):
            j0 = 2 * jp
            SIG = SIGs[jp]; UPB = UPBs[jp]; upb = upbs[jp]
            pbB, pbBb = proj_pair("b", j0, n)
            sigmoid_act(SIG.f(), SIG.b, pbB[:, 0:2 * n], pbBb)
            pbA, pbAb = proj_pair("a", j0, n)
            if isp:
                if g.first:
                    tk.op(POOL, lambda e: e.memset(up[:, :, :, 0:HK], 0.0), [], UP.b)
                else:
                    tk.op(POOL, lambda e: e.tensor_copy(out=up[:, :, 0, 0:HK], in_=HISTCt[:, j0:j0 + 2, :]), [HISTCb[jp]], UP.b)
            else:
                STG = TV(27, 2 * NB * HK)
                stg = STG.f().rearrange("p (j s r) -> p j s r", j=2, s=nseq)
                tk.dma(SP, STG.f().rearrange("p (j q) -> p j q", j=2),
                       cconvT[l, j0 * 128:(j0 + 2) * 128].rearrange("(j p) b r -> p j (b r)", p=128), [], STG.b)
                tk.op(POOL, lambda e, stg=stg: e.tensor_copy(out=up[:, :, :, 0:HK], in_=stg), STG.b, UP.b)
            tk.op(POOL, lambda e, upb=upb: e.tensor_copy(out=upb[:, :, :, 0:HK], in_=up[:, :, :, 0:HK]), UP.b, UPB.b)
            a4 = pbA[:, 0:2 * n].rearrange("p (j s t) -> p j s t", j=2, s=nseq)
            s4 = SIG.f().rearrange("p (j s t) -> p j s t", j=2, s=nseq)
            tk.op(DVE, lambda e, a4=a4, s4=s4: e.tensor_tensor(out=up[:, :, :, HK:HK + tlen], in0=a4, in1=s4, op=ALU.mult), pbAb + SIG.b, UP.b)
            tk.op(DVE, lambda e, a4=a4, s4=s4, upb=upb: e.tensor_tensor(out=upb[:, :, :, HK:HK + tlen], in0=a4, in1=s4, op=ALU.mult), pbAb + SIG.b, UPB.b)
            if isp:
                if g.last:
                    tk.dma(SP, pconvT[l, j0 * 128:(j0 + 2) * 128, :].rearrange("(j p) r -> p j r", p=128), up[:, :, 0, tlen:tlen + HK], UP.b, [], is_output=True)
                else:
                    tk.op(POOL, lambda e: e.tensor_copy(out=HISTCt[:, j0:j0 + 2, :], in_=up[:, :, 0, tlen:tlen + HK]), UP.b, [HISTCb[jp]])
            else:
                tk.op(POOL, lambda e, stg=stg: e.tensor_copy(out=stg, in_=up[:, :, :, tlen:tlen + HK]), UP.b, STG.b)
                tk.dma(SP, sconvT[l, j0 * 128:(j0 + 2) * 128].rearrange("(j p) b r -> p j (b r)", p=128),
                       STG.f().rearrange("p (j q) -> p j q", j=2), STG.b, [], is_output=True)
        if isp:
            SGcs = [TV(0, 2 * n), TV(2, 2 * n), TV(23, 2 * n)]
        else:
            SGcs = [TV(0, 2 * n), TV(2, 2 * n), TV(4, 2 * n)]
        for jp in range(3):
            pbG, pbGb = proj_pair("gc", 2 * jp, n)
            sigmoid_act(SGcs[jp].f(), SGcs[jp].b, pbG[:, 0:2 * n], pbGb)
            tk.op(DVE, lambda e, jp=jp, pbG=pbG: e.tensor_tensor(out=SGcs[jp].f(), in0=SGcs[jp].f(), in1=pbG[:, 0:2 * n], op=ALU.mult),
                  SGcs[jp].b + pbGb, SGcs[jp].b)
        pst, pstb = PBt[7], [PBb[7]]
        for jp in range(3):
            j0 = 2 * jp
            upb = upbs[jp]; UPB = UPBs[jp]
            pbc, pbcb = bank()
            for s in range(2):
                j = j0 + s
                r = dgr_state["i"]; dgr_state["i"] = 1 - r
                tk.dma(SP, DGRt[r][:].rearrange("p k c -> p (k c)"), DGd[l, j], [DGb[l][j]], [DGRb[r]])
                tk.group(PE, [(lambda e, k=k, s=s, r=r, upb=upb: e.matmul(pbc[:, s * n:(s + 1) * n], lhsT=DGRt[r][:, k, :], rhs=upb[:, s, :, k:k + tlen],
                                                                        start=(k == 0), stop=(k == CK - 1))) for k in range(CK)],
                         [DGRb[r]] + UPB.b, pbcb)
            for s in range(2):
                j = j0 + s
                tk.op(ACT, lambda e, s=s, j=j: e.activation(out=cs[:, j, :], in_=pbc[:, s * n:(s + 1) * n], func=AF.Identity,
                                                           bias=P7[:, j, R_CB:R_CB + 1], scale=1.0), pbcb + [P7B], CS.b)
            tk.op(ACT, lambda e, j0=j0: e.activation(out=csb, in_=cs[:, j0:j0 + 2, :], func=AF.Copy), CS.b, CSB.b)
            tk.op(ACT, lambda e, j0=j0: e.activation(out=csq, in_=cs[:, j0:j0 + 2, :], func=AF.Square), CS.b, CSQ.b)
            tk.group(PE, [(lambda e, s=s: e.matmul(pst[:, 0:n], lhsT=ONESt[:], rhs=csb[:, s, :], start=(jp == 0 and s == 0), stop=(jp == 2 and s == 1),
                                                   skip_group_check=True)) for s in range(2)], CSB.b + [ONESb], pstb)
            tk.group(PE, [(lambda e, s=s: e.matmul(pst[:, n:2 * n], lhsT=ONESt[:], rhs=csq[:, s, :], start=False, stop=(jp == 2 and s == 1),
                                                   skip_group_check=True)) for s in range(2)], CSQ.b + [ONESb], pstb)
        if after_conv is not None:
            after_conv()
        B0 = 5
        xlen = LH + tlen
        XRP = TV(33, 2 * nseq * xlen)
        XC = TV(39, 6 * n); XCB = TV(45, 3 * n)
        T0 = TV(4, nseq)
        H0v = TV(3, 6 * NB); SHOv = TV(7, 6 * NB)
        H0t = H0v.f().rearrange("p (j b) -> p j b", j=6); SHOt = SHOv.f().rearrange("p (j b) -> p j b", j=6)
        H0b = None; SHOb = None
        xc = XC.f().rearrange("p (j t) -> p j t", j=6)
        xcb = XCB.h().rearrange("p (j t) -> p j t", j=6)
        xrp = XRP.f().rearrange("p (j s u) -> p j s u", j=2, s=nseq)
        if not isp:
            tk.dma(SP, H0t, h0T[l].rearrange("(j p) b -> p j b", p=128), [], H0v.b)
        XRPB = TV(36, nseq * xlen)
        xrpb = XRPB.h().rearrange("p (j s u) -> p j s u", j=2, s=nseq)
        rl = dgr_state["i"]; dgr_state["i"] = 1 - rl
        tk.dma(SP, DGRt[rl][:, 0:6 * LK, :].rearrange("p k c -> p (k c)"), DGLd[l], [DGLb[l]], [DGRb[rl]])
        for jp in range(3):
            j0 = 2 * jp
            pbX, pbXb = proj_pair("xr", j0, n)
            if isp:
                if g.first:
                    tk.op(POOL, lambda e: e.memset(xrp[:, :, :, 0:LH], 0.0), [], XRP.b)
                else:
                    tk.op(POOL, lambda e: e.tensor_copy(out=xrp[:, :, 0, 0:LH], in_=HISTLt[:, j0:j0 + 2, :]), [HISTLb[jp]], XRP.b)
            else:
                STL = TV(48, 2 * NB * LH)
                stl = STL.f().rearrange("p (j s r) -> p j s r", j=2, s=nseq)
                tk.dma(SP, STL.f().rearrange("p (j q) -> p j q", j=2),
                       clruT[l, j0 * 128:(j0 + 2) * 128].rearrange("(j p) b r -> p j (b r)", p=128), [], STL.b)
                tk.op(POOL, lambda e, stl=stl: e.tensor_copy(out=xrp[:, :, :, 0:LH], in_=stl), STL.b, XRP.b)
            tk.op(POOL, lambda e: e.tensor_copy(out=xrpb[:, :, :, 0:LH], in_=xrp[:, :, :, 0:LH]), XRP.b, XRPB.b)
            x4 = pbX[:, 0:2 * n].rearrange("p (j s t) -> p j s t", j=2, s=nseq)
            tk.op(ACT, lambda e, x4=x4: e.activation(out=xrp[:, :, :, LH:LH + tlen], in_=x4, func=AF.Copy), pbXb, XRP.b)
            tk.op(ACT, lambda e, x4=x4: e.activation(out=xrpb[:, :, :, LH:LH + tlen], in_=x4, func=AF.Copy), pbXb, XRPB.b)
            if isp:
                if g.last:
                    tk.dma(SP, plconvT[l, j0 * 128:(j0 + 2) * 128, :].rearrange("(j p) r -> p j r", p=128), xrp[:, :, 0, tlen:tlen + LH], XRP.b, [], is_output=True)
                else:
                    tk.op(POOL, lambda e: e.tensor_copy(out=HISTLt[:, j0:j0 + 2, :], in_=xrp[:, :, 0, tlen:tlen + LH]), XRP.b, [HISTLb[jp]])
            else:
                tk.op(POOL, lambda e, stl=stl: e.tensor_copy(out=stl, in_=xrp[:, :, :, tlen:tlen + LH]), XRP.b, STL.b)
                tk.dma(SP, slconvT[l, j0 * 128:(j0 + 2) * 128].rearrange("(j p) b r -> p j (b r)", p=128),
                       STL.f().rearrange("p (j q) -> p j q", j=2), STL.b, [], is_output=True)
            pbx, pbxb = bank()
            for s in range(2):
                j = j0 + s
                tk.group(PE, [(lambda e, k=k, s=s, j=j: e.matmul(pbx[:, s * n:(s + 1) * n], lhsT=DGRt[rl][:, j * LK + k, :], rhs=xrpb[:, s, :, k:k + tlen],
                                                               start=(k == 0), stop=(k == LK - 1))) for k in range(LK)],
                         [DGRb[rl]] + XRPB.b, pbxb)
            for s in range(2):
                j = j0 + s
                tk.op(ACT, lambda e, s=s, j=j, pbx=pbx: e.activation(out=xc[:, j, :], in_=pbx[:, s * n:(s + 1) * n], func=AF.Identity,
                                                                    bias=P7[:, j, R_LCB:R_LCB + 1], scale=1.0), pbxb + [P7B], XC.b)
        tk.op(ACT, lambda e: e.activation(out=xcb, in_=xc, func=AF.Copy), XC.b, XCB.b)
        tk.op(DVE, lambda e: e.tensor_scalar(out=MEAN.f(), in0=pst[:, 0:n], scalar1=1.0 / CW, scalar2=None, op0=ALU.mult), pstb, MEAN.b)
        tk.op(DVE, lambda e: e.tensor_tensor(out=MSQv.f(), in0=MEAN.f(), in1=MEAN.f(), op=ALU.mult), MEAN.b, MSQv.b)
        tk.op(DVE, lambda e: e.scalar_tensor_tensor(out=VAR.f(), in0=pst[:, n:2 * n], scalar=1.0 / CW, in1=MSQv.f(), op0=ALU.mult, op1=ALU.subtract),
              pstb + MSQv.b, VAR.b)
        rsqrt_act(VAR, VAR.f(), VAR.b, 1.0)
        TT = TV(8, 6 * n); ZZ = TV(14, 6 * n)
        tt = TT.f().rearrange("p (j t) -> p j t", j=6)
        zz = ZZ.f().rearrange("p (j t) -> p j t", j=6)
        mean_b = MEAN.f().unsqueeze(1).broadcast_to([128, 6, n])
        rs_b = VAR.f().unsqueeze(1).broadcast_to([128, 6, n])
        tk.op(DVE, lambda e: e.tensor_tensor(out=tt, in0=cs, in1=mean_b, op=ALU.subtract), CS.b + MEAN.b, TT.b)
        tk.op(DVE, lambda e: e.tensor_tensor(out=tt, in0=tt, in1=rs_b, op=ALU.mult), TT.b + VAR.b, TT.b)
        for j in range(6):
            tk.op(DVE, lambda e, j=j: e.tensor_scalar(out=zz[:, j, :], in0=tt[:, j, :], scalar1=P7[:, j, R_LNG:R_LNG + 1], scalar2=P7[:, j, R_LNB:R_LNB + 1],
                                                     op0=ALU.mult, op1=ALU.add), TT.b + [P7B], ZZ.b)
        EE = TV(25, 6 * n)
        ee = EE.f().rearrange("p (j t) -> p j t", j=6)
        for j in range(6):
            tk.op(ACT, lambda e, j=j: e.activation(out=ee[:, j, :], in_=tt[:, j, :], func=AF.Exp, bias=NEGt[:, j, 3:4], scale=NEGt[:, j, 2:3]),
                  TT.b + [NEGb], EE.b)
        tk.op(ACT, lambda e: e.activation(out=EE.f(), in_=EE.f(), func=AF.Ln, bias=ONEt[:], scale=1.0), EE.b + [CONSTb], EE.b)
        tk.op(ACT, lambda e: e.activation(out=EE.f(), in_=EE.f(), func=AF.Exp, scale=-1.0), EE.b, EE.b)
        tk.op(DVE, lambda e: e.tensor_tensor(out=ZZ.f(), in0=ZZ.f(), in1=EE.f(), op=ALU.mult), ZZ.b + EE.b, ZZ.b)
        for jp in range(3):
            j0 = 2 * jp
            tk.op(DVE, lambda e, j0=j0, jp=jp: e.tensor_tensor(out=CATt[:, j0:j0 + 2, 0:n], in0=zz[:, j0:j0 + 2, :],
                                                              in1=SGcs[jp].f().rearrange("p (j t) -> p j t", j=2), op=ALU.mult),
                  ZZ.b + SGcs[jp].b, CATb[j0:j0 + 2])

        ck(5)
        blk_of = {}
        for bi, (mo, kc) in enumerate(GBLK):
            blk_of.setdefault(mo, []).append((bi, kc))
        SG2s = [TV(0, 2 * n), TV(2, 2 * n), TV(5, 2 * n)]
        for jp in range(3):
            pbG, pbGb = proj_pair("gr", 2 * jp, n)
            sigmoid_act(SG2s[jp].f(), SG2s[jp].b, pbG[:, 0:2 * n], pbGb)
            tk.op(DVE, lambda e, jp=jp: e.tensor_tensor(out=SG2s[jp].f(), in0=SG2s[jp].f(), in1=pbG[:, 0:2 * n], op=ALU.mult), SG2s[jp].b + pbGb, SG2s[jp].b)
        QT = TV(33, 2 * n); SGQ = TV(35, 4 * n)
        HB = max(1, min(NH, 512 // (2 * n)))
        qt = QT.h().rearrange("p (h t) -> p h t", h=NH)
        sgq = SGQ.f().rearrange("p (h t) -> p h t", h=NH)
        for jp in range(2):
            pbQ, pbQb = proj_pair("q", 2 * jp, n)
            tk.op(ACT, lambda e, jp=jp: e.activation(out=qt[:, 2 * jp:2 * jp + 2, :], in_=pbQ[:, 0:2 * n].rearrange("p (h t) -> p h t", h=2), func=AF.Copy),
                  pbQb, QT.b)
            pbG, pbGb = proj_pair("gq", 2 * jp, n)
            g3 = pbG[:, 0:2 * n].rearrange("p (h t) -> p h t", h=2)
            sigmoid_act(sgq[:, 2 * jp:2 * jp + 2, :], SGQ.b, g3, pbGb)
            tk.op(DVE, lambda e, jp=jp, g3=g3: e.tensor_tensor(out=sgq[:, 2 * jp:2 * jp + 2, :], in0=sgq[:, 2 * jp:2 * jp + 2, :], in1=g3, op=ALU.mult),
                  SGQ.b + pbGb, SGQ.b)
        sets = []
        for si in range(2):
            base = 17 + 8 * si
            sets.append((TV(base, 2 * n), TV(base, 2 * n, eoff=2 * n), TV(base + 4, 2 * n), TV(base + 6, 2 * n)))
        for bt in range(3):
            m0 = 2 * bt
            RGv, IGv, A2v, HHv = sets[bt % 2]
            rg = RGv.f().rearrange("p (j t) -> p j t", j=2); ig = IGv.f().rearrange("p (j t) -> p j t", j=2)
            hhv = HHv.f().rearrange("p (j t) -> p j t", j=2)
            for q in range(2):
                mo = m0 + q
                pb, pbb = bank()
                lst = blk_of[mo]
                tk.group(PE, [(lambda e, bi=bi, kc=kc, i=i: e.matmul(pb[:, 0:n], lhsT=WAt[:, bi, :], rhs=xcb[:, kc, :], start=(i == 0), stop=(i == len(lst) - 1)))
                              for i, (bi, kc) in enumerate(lst)], [WAb] + XCB.b, pbb)
                tk.group(PE, [(lambda e, bi=bi, kc=kc, i=i: e.matmul(pb[:, n:2 * n], lhsT=WXt[:, bi, :], rhs=xcb[:, kc, :], start=(i == 0), stop=(i == len(lst) - 1)))
                              for i, (bi, kc) in enumerate(lst)], [WXb] + XCB.b, pbb)
                tk.op(ACT, lambda e, mo=mo, q=q, pb=pb: e.activation(out=rg[:, q, :], in_=pb[:, 0:n], func=AF.Exp, bias=NEGt[:, mo, 0:1], scale=-1.0), pbb + [NEGb], RGv.b)
                tk.op(ACT, lambda e, mo=mo, q=q, pb=pb: e.activation(out=ig[:, q, :], in_=pb[:, n:2 * n], func=AF.Exp, bias=NEGt[:, mo, 1:2], scale=-1.0), pbb + [NEGb], IGv.b)
            RI = TV(17 + 8 * (bt % 2), 4 * n)
            tk.op(ACT, lambda e, RI=RI: e.activation(out=RI.f(), in_=RI.f(), func=AF.Ln, bias=ONEt[:], scale=1.0), RI.b + [CONSTb], RI.b)
            tk.op(ACT, lambda e, RI=RI: e.activation(out=RI.f(), in_=RI.f(), func=AF.Exp, scale=-1.0), RI.b, RI.b)
            for q in range(2):
                mo = m0 + q
                tk.op(ACT, lambda e, mo=mo, q=q: e.activation(out=rg[:, q, :], in_=rg[:, q, :], func=AF.Exp, scale=CVt[:, mo:mo + 1]), RGv.b + [CVb], RGv.b)
            tk.op(DVE, lambda e: e.tensor_tensor(out=A2v.f(), in0=RGv.f(), in1=RGv.f(), op=ALU.mult), RGv.b, A2v.b)
            tk.op(ACT, lambda e: e.activation(out=A2v.f(), in_=A2v.f(), func=AF.Ln, bias=ONEt[:], scale=-1.0), A2v.b + [CONSTb], A2v.b)
            tk.op(ACT, lambda e: e.activation(out=A2v.f(), in_=A2v.f(), func=AF.Exp, scale=0.5), A2v.b, A2v.b)
            tk.op(DVE, lambda e, m0=m0: e.tensor_tensor(out=ig, in0=xc[:, m0:m0 + 2, :], in1=ig, op=ALU.mult), IGv.b + XC.b, IGv.b)
            tk.op(DVE, lambda e: e.tensor_tensor(out=IGv.f(), in0=IGv.f(), in1=A2v.f(), op=ALU.mult), IGv.b + A2v.b, IGv.b)
            for q in range(2):
                mo = m0 + q
                aq = rg[:, q, :]; bq = ig[:, q, :]; hq = hhv[:, q, :]
                if isp:
                    if g.first:
                        init = 0.0; rdi = []
                    else:
                        init = HSTt[:, mo:mo + 1]; rdi = [HSTb[mo]]
                else:
                    aa3 = v3(aq); bx3 = v3(bq)
                    tk.op(DVE, lambda e, mo=mo, aa3=aa3: e.tensor_tensor(out=T0.f(), in0=aa3[:, :, 0], in1=H0t[:, mo, :], op=ALU.mult), RGv.b + H0v.b, T0.b)
                    tk.op(DVE, lambda e, bx3=bx3: e.tensor_tensor(out=bx3[:, :, 0], in0=bx3[:, :, 0], in1=T0.f(), op=ALU.add), IGv.b + T0.b, IGv.b)
                    tk.op(DVE, lambda e, aa3=aa3: e.memset(aa3[:, :, 0], 0.0), RGv.b, RGv.b)
                    init = 0.0; rdi = []
                tk.op(DVE, lambda e, init=init, aq=aq, bq=bq, hq=hq: e.tensor_tensor_scan(out=hq, data0=aq, data1=bq, initial=init, op0=ALU.mult, op1=ALU.add),
                      RGv.b + IGv.b + rdi, HHv.b)
                if isp:
                    tk.op(POOL, lambda e, mo=mo, hq=hq: e.tensor_copy(out=HSTt[:, mo:mo + 1], in_=hq[:, n - 1:n]), HHv.b, [HSTb[mo]])
                else:
                    tk.op(POOL, lambda e, mo=mo, hq=hq: e.tensor_copy(out=SHOt[:, mo, :], in_=v3(hq)[:, :, tlen - 1]), HHv.b, SHOv.b)
            sgs = SG2s[bt]
            tk.op(DVE, lambda e, m0=m0, sgs=sgs: e.tensor_tensor(out=CATt[:, 6 + m0:6 + m0 + 2, 0:n], in0=hhv, in1=sgs.f().rearrange("p (j t) -> p j t", j=2), op=ALU.mult),
                  HHv.b + sgs.b, CATb[6 + m0:6 + m0 + 2])
        if isp:
            if g.last:
                tk.dma(SP, phT[l].rearrange("(j p) o -> p (j o)", p=128), HSTt[:], HSTb, [], is_output=True, slow=True)
        else:
            tk.dma(SP, shT[l].rearrange("(j p) b -> p j b", p=128), SHOt, SHOv.b, [], is_output=True)

        if after_lru is not None:
            after_lru()
        ck(6)
        B0 = 5
        HB = max(1, min(NH, 512 // (2 * n)))
        PT = [TV(B0 + 6, HB * n), TV(B0 + 7, HB * n)]
        RD = TV(B0 + 8, HB * n); OT = TV(B0 + 9, HB * n)
        sc = 1.0 / math.sqrt(128.0)
        if not isp:
            kring = []
            vring = []
            for i in range(4):
                kv_ = TV(15 + 2 * i, NH * NMEM // 2)
                kring.append((kv_.h().rearrange("p (h m) -> p h m", h=NH), kv_.b))
                vv_ = TV(23 + 2 * i, 2 * MW // 2)
                vring.append((vv_.h().rearrange("p (c d) -> p c d", c=2), vv_.b))
            NR = len(kring)
        for hg in range(NH // HB):
            h0 = hg * HB
            pbS, pbSb = bank()
            pbO, pbOb = bank()
            sS = pbS[:, 0:HB * 2 * n].rearrange("p (h c t) -> p h c t", h=HB, c=2)
            sO = pbO[:, 0:HB * 2 * n].rearrange("p (h c t) -> p h c t", h=HB, c=2)
            pt = PT[hg % 2]
            ptv = pt.h().rearrange("p (h c t) -> p h c t", h=HB, c=2)
            if isp:
                for hh in range(HB):
                    h = h0 + hh
                    for mc in range(2):
                        tk.group(PE, [lambda e, hh=hh, h=h, mc=mc: e.matmul(sS[:, hh, mc, 0:n], lhsT=KTPt[:, h, mc * 128:(mc + 1) * 128], rhs=qt[:, h, 0:n],
                                                                          start=True, stop=True)], [KTPb] + QT.b, pbSb)
            else:
                for b in range(NB):
                    kap, kb_ = kring[b % NR]
                    tk.dma(POOL, kap, ckT[l, b].rearrange("(h d) m -> d h m", d=128), [], kb_)
                    c0, c1 = b * TS, (b + 1) * TS
                    fns = []
                    for hh in range(HB):
                        h = h0 + hh
                        for mc in range(2):
                            fns.append(lambda e, hh=hh, h=h, mc=mc, kap=kap, c0=c0, c1=c1: e.matmul(sS[:, hh, mc, c0:c1], lhsT=kap[:, h, mc * 128:(mc + 1) * 128],
                                                                                                  rhs=qt[:, h, c0:c1], start=True, stop=True))
                    tk.group(PE, fns, kb_ + QT.b, pbSb)
            tk.op(ACT, lambda e: e.activation(out=ptv, in_=sS, func=AF.Exp, scale=sc), pbSb, pt.b)
            for hh in range(HB):
                tk.group(PE, [(lambda e, hh=hh, mc=mc: e.matmul(sO[:, hh, 0, :], lhsT=ONESt[:], rhs=ptv[:, hh, mc, :], start=(mc == 0), stop=(mc == 1)))
                              for mc in range(2)], pt.b + [ONESb], pbOb)
            if isp:
                for hh in range(HB):
                    h = h0 + hh
                    tk.group(PE, [(lambda e, hh=hh, h=h, mc=mc: e.matmul(sO[:, hh, 1, 0:n], lhsT=VPt[:, mc, h * 128:(h + 1) * 128], rhs=ptv[:, hh, mc, 0:n],
                                                                       start=(mc == 0), stop=(mc == 1))) for mc in range(2)], [VPb] + pt.b, pbOb)
            else:
                for b in range(NB):
                    vap, vb_ = vring[b % NR]
                    tk.dma(POOL, vap, cv[l, b].rearrange("(c p) d -> p c d", p=128), [], vb_)
                    c0, c1 = b * TS, (b + 1) * TS
                    for hh in range(HB):
                        h = h0 + hh
                        tk.group(PE, [(lambda e, hh=hh, h=h, mc=mc, vap=vap, c0=c0, c1=c1: e.matmul(sO[:, hh, 1, c0:c1], lhsT=vap[:, mc, h * 128:(h + 1) * 128],
                                                                                                  rhs=ptv[:, hh, mc, c0:c1], start=(mc == 0), stop=(mc == 1)))
                                      for mc in range(2)], vb_ + pt.b, pbOb)
            rd = RD.f().rearrange("p (h t) -> p h t", h=HB)
            ot = OT.f().rearrange("p (h t) -> p h t", h=HB)
            tk.op(DVE, lambda e: e.reciprocal(out=rd, in_=sO[:, :, 0, :]), pbOb, RD.b)
            tk.op(DVE, lambda e: e.tensor_tensor(out=ot, in0=sO[:, :, 1, :], in1=rd, op=ALU.mult), pbOb + RD.b, OT.b)
            tk.op(DVE, lambda e, h0=h0: e.tensor_tensor(out=CATt[:, 12 + h0:12 + h0 + HB, 0:n], in0=ot, in1=sgq[:, h0:h0 + HB, :], op=ALU.mult),
                  OT.b + SGQ.b, CATb[12 + h0:12 + h0 + HB])

    def wo_fetch(l, g, dp):
        r = wor_state["i"]; wor_state["i"] = (r + 1) % 3
        g.wo[dp] = r
        tk.dma(POOL, WORt[r][:], w_out[l, :, dp * 256:(dp + 1) * 256].rearrange("(k p) c -> p k c", p=128), [], [WORb[r]])

    def stage_C(l, g):
        pi = l % 2
        n = g.n
        xi = g.xi
        X = XBt[xi]
        O32 = TV(25, 8 * SL); OSQ = TV(21, 4 * SL); R2 = TV(20, n)
        o32 = O32.f().rearrange("p (k t) -> p k t", k=KD)
        osq = OSQ.h().rearrange("p (k t) -> p k t", k=KD)
        for dp in range(4):
            pb, pbb = bank()
            for s in range(2):
                d = 2 * dp + s
                fns = []
                r = g.wo[dp]
                for kc in range(16):
                    fns.append(lambda e, kc=kc, r=r, s=s: e.matmul(pb[:, s * n:(s + 1) * n], lhsT=WORt[r][:, kc, s * 128:(s + 1) * 128],
                                                                 rhs=CATt[:, kc, 0:n], start=(kc == 0), stop=(kc == 15)))
                tk.group(PE, fns, [WORb[r]] + CATb, pbb)
            if dp == 0:
                wo_fetch(l, g, 3)
            pv = pb[:, 0:2 * n].rearrange("p (s t) -> p s t", s=2)
            tk.op(ACT, lambda e, dp=dp, pv=pv: e.activation(out=o32[:, 2 * dp:2 * dp + 2, 0:n], in_=pv, func=AF.Copy), pbb, O32.b)
            tk.op(ACT, lambda e, dp=dp, pv=pv: e.activation(out=osq[:, 2 * dp:2 * dp + 2, 0:n], in_=pv, func=AF.Square), pbb, OSQ.b)
        pb, pbb = bank()
        tk.group(PE, [(lambda e, kc=kc: e.matmul(pb[:, 0:n], lhsT=ONESt[:], rhs=osq[:, kc, 0:n], start=(kc == 0), stop=(kc == KD - 1)))
                      for kc in range(KD)], OSQ.b + [ONESb], pbb)
        rsqrt_act(R2, pb[:, 0:n], pbb, 1.0 / D)
        tk.op(DVE, lambda e: e.tensor_tensor(out=o32[:, :, 0:n], in0=o32[:, :, 0:n], in1=R2.f().unsqueeze(1).broadcast_to([128, KD, n]), op=ALU.mult),
              O32.b + R2.b, O32.b)
        for d in range(KD):
            tk.op(DVE, lambda e, d=d: e.scalar_tensor_tensor(out=X[:, d, 0:n], in0=o32[:, d, 0:n], scalar=P10t[pi][:, d, 1:2], in1=X[:, d, 0:n],
                                                           op0=ALU.mult, op1=ALU.add), O32.b + [XBb[xi], P10b[pi]], [XBb[xi]])
        if l == L - 1:
            dst = (ypT[:, g.c0:g.c0 + n] if g.kind == "p" else ysT[:, 0:n])
            tk.dma(SP, dst.rearrange("(k p) t -> p k t", p=128), X[:, :, 0:n], [XBb[xi]], [], is_output=True)
        else:
            tk.dma(SP, xscr[l % 2][:, g.c0:g.c0 + n].rearrange("(k p) t -> p k t", p=128), X[:, :, 0:n], [XBb[xi]], [scrb[l % 2][g.idx]])

    def ck(level):
        if cfg.upto < level:
            raise StopEmit()

    def emit_all():
        load_params(0)
        load_weights(0)
        ck(1)
        diag_chunks(0, range(6))
        order = [groups[-1]] + groups[:-1]

        def reload(l, names):
            for nm_ in names:
                if nm_ == "WA":
                    tk.dma(POOL, WAt[:], wab[l], [], [WAb])
                elif nm_ == "WX":
                    tk.dma(POOL, WXt[:], wxb[l], [], [WXb])
                elif nm_ == "WO":
                    pass
                else:
                    (nm, c0, ncs) = [x for x in SEGS if x[0] == nm_][0]
                    tk.dma(POOL, Wt[nm][:], w_in[l, :, c0:c0 + ncs * 128].rearrange("(k p) c -> p k c", p=128), [], [Wb[nm]])

        for l in range(L):
            layer_consts(l)
            if l + 1 < L:
                load_params(l + 1)
            ck(2)
            mem_phase(l)
            if l > 0:
                reload(l, ["gc", "xr", "gr", "WA", "WX", "q", "gq"])
            ck(3)
            stage_A(l, order[0])
            for gi, g in enumerate(order):
                ck(4)
                ac = None; al = None
                last = (gi == NG - 1 and l + 1 < L)
                ngen = min(3, NG - 1)
                per = (6 + ngen - 1) // ngen
                dg = None
                if l + 1 < L and 1 <= gi <= ngen:
                    dg = (lambda gi=gi: diag_chunks(l + 1, range(per * (gi - 1), min(6, per * gi))))
                if last:
                    def ac(dg=dg):
                        if dg is not None:
                            dg()
                        reload(l + 1, ["b", "a"])
                    al = None
                else:
                    ac = dg
                stage_B(l, g, ck, after_conv=ac, after_lru=al)
                if gi == 0 and l > 0:
                    reload(l, ["WO"])
                if gi + 1 < NG:
                    stage_A(l, order[gi + 1])
                ck(8)
                stage_C(l, g)

    try:
        emit_all()
    except StopEmit:
        pass
    tk.finish()
    return nc


def host_inputs(inp, cfg, ncores):
    L, SEQ, NB = cfg.L, cfg.SEQ, cfg.NB
    f = lambda a: np.ascontiguousarray(np.asarray(a, dtype=np.float32))
    p768 = np.concatenate([
        np.asarray(inp["conv_w"])[:L], np.asarray(inp["conv_b"])[:L, None], np.asarray(inp["conv_ln_g"])[:L, None],
        np.asarray(inp["conv_ln_b"])[:L, None], np.asarray(inp["lru_conv_w"])[:L], np.asarray(inp["lru_conv_b"])[:L, None],
        np.asarray(inp["lru_ba"])[:L, None], np.asarray(inp["lru_bx"])[:L, None], np.asarray(inp["lru_lambda"])[:L, None]], axis=1)
    assert p768.shape[1] == NP7
    p768 = f(p768.transpose(0, 2, 1))
    p1024 = f(np.stack([np.asarray(inp["norm_pre_g"])[:L], np.asarray(inp["norm_post_g"])[:L], np.asarray(inp["mem_norm_g"])[:L]], axis=2))

    def blocks(w):
        w = np.asarray(w)[:L]
        bd = np.zeros((L, LW, LW), np.float32)
        for h in range(8):
            bd[:, 96 * h:96 * h + 96, 96 * h:96 * h + 96] = w[:, h]
        out = np.zeros((L, 128, NBLK, 128), np.float32)
        for bi, (mo, kc) in enumerate(GBLK):
            out[:, :, bi, :] = bd[:, kc * 128:(kc + 1) * 128, mo * 128:(mo + 1) * 128]
        return out

    shared = {
        "w_in": f(np.asarray(inp["w_in"])[:L]), "w_out": f(np.asarray(inp["w_out"])[:L]),
        "wk": f(np.asarray(inp["w_mem_k"])[:L]), "wv": f(np.asarray(inp["w_mem_v"])[:L]),
        "wab": blocks(inp["lru_wa"]), "wxb": blocks(inp["lru_wx"]), "p768": p768, "p1024": p1024,
        "ident": np.eye(128, dtype=np.float32),
    }
    xp = np.asarray(inp["x_prompt"]); xs = np.asarray(inp["x_sample"]); mem = np.asarray(inp["mem_prompt"])
    cc = np.asarray(inp["cache_conv"]); cl = np.asarray(inp["cache_lru_conv"]); h0 = np.asarray(inp["state_lru_h"])
    ck = np.asarray(inp["cache_mem_k"]); cvv = np.asarray(inp["cache_mem_v"])
    maps = []
    for i in range(ncores):
        sl = slice(i * NB, (i + 1) * NB)
        m = dict(shared)
        m["xpT"] = f(xp[i, :SEQ].T)
        m["xsT"] = f(xs[sl].reshape(NB * TS, D).T)
        m["memT"] = f(mem[i].T)
        m["cconvT"] = f(cc[:L, sl].transpose(0, 3, 1, 2))
        m["clruT"] = f(cl[:L, sl].transpose(0, 3, 1, 2))
        m["h0T"] = f(h0[:L, sl].transpose(0, 2, 1))
        m["ckT"] = f(ck[:L, sl].reshape(L, NB, NMEM, MW).transpose(0, 1, 3, 2))
        m["cv"] = f(cvv[:L, sl].reshape(L, NB, NMEM, MW))
        maps.append(m)
    return maps


def host_outputs(results, cfg, ncores):
    L, SEQ, NB = cfg.L, cfg.SEQ, cfg.NB
    yp = np.stack([r["ypT"].T for r in results])
    ys = np.concatenate([r["ysT"].T.reshape(NB, TS, D) for r in results])
    pconv = np.stack([r["pconvT"].transpose(0, 2, 1) for r in results], axis=1)
    plconv = np.stack([r["plconvT"].transpose(0, 2, 1) for r in results], axis=1)
    ph = np.stack([r["phT"][:, :, 0] for r in results], axis=1)
    pmk = np.stack([r["pmkT"].transpose(0, 2, 1).reshape(L, NMEM, NH, 128) for r in results], axis=1)
    pmv = np.stack([r["pmv"].reshape(L, NMEM, NH, 128) for r in results], axis=1)
    sconv = np.concatenate([r["sconvT"].transpose(0, 2, 3, 1) for r in results], axis=1)
    slconv = np.concatenate([r["slconvT"].transpose(0, 2, 3, 1) for r in results], axis=1)
    sh = np.concatenate([r["shT"].transpose(0, 2, 1) for r in results], axis=1)
    outs = (yp, ys, pconv, plconv, ph, pmk, pmv, sconv, slconv, sh)
    return tuple(np.ascontiguousarray(o.astype(np.float32)) for o in outs)


_CACHE = {}


def kernel(**inputs):
    cfg = Cfg()
    ncores = 8
    if "nc" not in _CACHE:
        _CACHE["nc"] = build(cfg)
    nc = _CACHE["nc"]
    maps = host_inputs(inputs, cfg, ncores)
    res = run_bass_kernel_spmd(nc, maps, core_ids=list(range(ncores)))
    return host_outputs(res.results, cfg, ncores)
```

```python
import math
import numpy as np
import concourse.bass as bass
import concourse.mybir as mybir
from concourse.bass_utils import run_bass_kernel_spmd

F32 = mybir.dt.float32
BF16 = mybir.dt.bfloat16
AF = mybir.ActivationFunctionType
ALU = mybir.AluOpType

D = 1024
KD = 8
CW = 768
LW = 768
MW = 512
NH = 4
NMEM = 256
CK = 31
HK = CK - 1
LK = 4
LH = LK - 1
INW = 4864
EPS = 1e-6
SEGS = [("a", 0, 6), ("b", 768, 6), ("gc", 1536, 6), ("xr", 2304, 6), ("gr", 3072, 6), ("q", 3840, 4), ("gq", 4352, 4)]
NP7 = 42
R_CW, R_CB, R_LNG, R_LNB, R_LCW, R_LCB, R_BA, R_BX, R_LAM = 0, 31, 32, 33, 34, 38, 39, 40, 41
TS = 4


def gate_blocks():
    out = []
    for mo in range(6):
        hlo = (128 * mo) // 96
        hhi = (128 * mo + 127) // 96
        klo = (96 * hlo) // 128
        khi = (96 * hhi + 95) // 128
        for kc in range(klo, khi + 1):
            out.append((mo, kc))
    return out


GBLK = gate_blocks()
NBLK = len(GBLK)


class StopEmit(Exception):
    pass


class Buf:
    __slots__ = ("w", "r", "name", "excl")

    def __init__(self, name="", excl=False):
        self.w = None
        self.r = []
        self.name = name
        self.excl = excl


class Eng:
    def __init__(self, nc, e, name):
        self.e = e
        self.name = name
        self.sem = nc.alloc_semaphore("es_" + name)
        self.cnt = 0
        self.seen = {}


class Tracker:
    def __init__(self, nc, ndma=56):
        self.nc = nc
        self.pe = Eng(nc, nc.tensor, "pe")
        self.act = Eng(nc, nc.scalar, "act")
        self.dve = Eng(nc, nc.vector, "dve")
        self.pool = Eng(nc, nc.gpsimd, "pool")
        self.sp = Eng(nc, nc.sync, "sp")
        self.dpools = {}
        for nm, cnt in (("sp", ndma // 2), ("pool", ndma // 2)):
            self.dpools[nm] = {"sems": [nc.alloc_semaphore(f"ds_{nm}{i}") for i in range(cnt)], "cnt": [0] * cnt, "next": 0}
        self.out_events = []
        import os
        self.nops = 0
        self.maxops = int(os.environ.get("STOPN", "100000000"))
        self.log = []

    def _waits(self, E, reads, writes):
        self.nops += 1
        if self.nops > self.maxops:
            raise StopEmit()
        need = {}

        def add(ev, raw):
            sem, val, eng = ev
            if eng is E and E is not self.pool:
                if not raw or E is self.pe:
                    return
            k = id(sem)
            if k not in need or need[k][1] < val:
                need[k] = (sem, val)

        for b in reads:
            if b.w is not None:
                add(b.w, True)
            if b.excl:
                for ev in b.r:
                    add(ev, False)
        for b in writes:
            if b.w is not None:
                add(b.w, False)
            for ev in b.r:
                add(ev, False)
        for k, (sem, val) in need.items():
            if E.seen.get(k, 0) < val:
                E.e.wait_ge(sem, val)
                E.seen[k] = val

    def _commit(self, ev, reads, writes):
        for b in writes:
            b.w = ev
            b.r = []
        for b in reads:
            b.r = [x for x in b.r if x[0] is not ev[0]]
            b.r.append(ev)

    def op(self, E, fn, reads=(), writes=()):
        self._waits(E, reads, writes)
        ins = fn(E.e)
        E.cnt += 1
        ins.then_inc(E.sem, 1)
        self._commit((E.sem, E.cnt, E), reads, writes)

    def group(self, E, fns, reads=(), writes=()):
        self._waits(E, reads, writes)
        ins = None
        for fn in fns:
            ins = fn(E.e)
        E.cnt += 1
        ins.then_inc(E.sem, 1)
        self._commit((E.sem, E.cnt, E), reads, writes)

    def dma(self, Q, out, in_, reads=(), writes=(), is_output=False, slow=False):
        self._waits(Q, reads, writes)
        dp = self.dpools[Q.name]
        i = dp["next"]
        dp["next"] = (i + 1) % len(dp["sems"])
        if dp["cnt"][i] > 0 and Q.seen.get(id(dp["sems"][i]), 0) < dp["cnt"][i]:
            Q.e.wait_ge(dp["sems"][i], dp["cnt"][i])
            Q.seen[id(dp["sems"][i])] = dp["cnt"][i]
        if slow:
            Q.e.dma_start(out=out, in_=in_, allow_slow_non_contiguous=True).then_inc(dp["sems"][i], 16)
        else:
            Q.e.dma_start(out=out, in_=in_).then_inc(dp["sems"][i], 16)
        dp["cnt"][i] += 16
        ev = (dp["sems"][i], dp["cnt"][i], None)
        self._commit(ev, reads, writes)
        if is_output:
            self.out_events.append(ev)

    def finish(self):
        E = self.sp
        for dp in self.dpools.values():
            for sem, val in zip(dp["sems"], dp["cnt"]):
                if val > 0:
                    E.e.wait_ge(sem, val)
        for G in (self.pe, self.act, self.dve, self.pool):
            if G.cnt > 0:
                E.e.wait_ge(G.sem, G.cnt)


class Cfg:
    def __init__(self, L=4, SEQ=2048, NB=16, T=256, NPE=0, upto=99):
        self.L, self.SEQ, self.NB, self.T, self.NPE = L, SEQ, NB, T, NPE
        self.upto = upto
        self.ndma = 56
        self.NS = NB * TS
        assert SEQ % T == 0 and T >= HK and self.NS <= T


def build(cfg):
    L, SEQ, NB, T = cfg.L, cfg.SEQ, cfg.NB, cfg.T
    NS = cfg.NS
    SL = T
    nc = bass.Bass("TRN2", target_bir_lowering=False)

    def din(name, shape):
        return nc.dram_tensor(name, list(shape), F32, kind="ExternalInput").ap()

    def dout(name, shape):
        return nc.dram_tensor(name, list(shape), F32, kind="ExternalOutput").ap()

    xpT = din("xpT", [D, SEQ]); xsT = din("xsT", [D, NS]); memT = din("memT", [D, NMEM])
    cconvT = din("cconvT", [L, CW, NB, HK]); clruT = din("clruT", [L, LW, NB, LH]); h0T = din("h0T", [L, LW, NB])
    ckT = din("ckT", [L, NB, MW, NMEM]); cv = din("cv", [L, NB, NMEM, MW])
    w_in = din("w_in", [L, D, INW]); w_out = din("w_out", [L, 2 * D, D])
    wk = din("wk", [L, D, MW]); wv = din("wv", [L, D, MW])
    wab = din("wab", [L, 128, NBLK, 128]); wxb = din("wxb", [L, 128, NBLK, 128])
    p768 = din("p768", [L, CW, NP7]); p1024 = din("p1024", [L, D, 3])
    ident_d = din("ident", [128, 128])
    DGd = nc.dram_tensor("dgscr", [L, 6, 128, CK * 128], BF16, kind="Internal").ap()
    DGb = [[Buf(f"dg{l}_{j}") for j in range(6)] for l in range(L)]
    DGLd = nc.dram_tensor("dglscr", [L, 128, 6 * LK * 128], BF16, kind="Internal").ap()
    DGLb = [Buf(f"dgl{l}") for l in range(L)]

    ypT = dout("ypT", [D, SEQ]); ysT = dout("ysT", [D, NS])
    pconvT = dout("pconvT", [L, CW, HK]); plconvT = dout("plconvT", [L, LW, LH]); phT = dout("phT", [L, LW, 1])
    pmkT = dout("pmkT", [L, MW, NMEM]); pmv = dout("pmv", [L, NMEM, MW])
    sconvT = dout("sconvT", [L, CW, NB, HK]); slconvT = dout("slconvT", [L, LW, NB, LH]); shT = dout("shT", [L, LW, NB])
    xscr = [nc.dram_tensor(f"xscr{i}", [D, SEQ + NS], F32, kind="Internal").ap() for i in range(2)]

    tk = Tracker(nc, ndma=cfg.ndma)
    PE, ACT, DVE, POOL, SP = tk.pe, tk.act, tk.dve, tk.pool, tk.sp

    def sb(name, shape, dt):
        return nc.alloc_sbuf_tensor(name, list(shape), dt)

    Wt = {}; Wb = {}
    for (nm, c0, ncs) in SEGS:
        Wt[nm] = sb("W_" + nm, [128, KD, ncs * 128], BF16); Wb[nm] = Buf("W_" + nm)
    WOt = [sb(f"WO{i}", [128, n_, D], BF16) for i, n_ in enumerate((6, 6, 4))]
    WOb = [Buf(f"WO{i}") for i in range(3)]
    WAt = sb("WA", [128, NBLK, 128], BF16); WAb = Buf("WA")
    WXt = sb("WX", [128, NBLK, 128], BF16); WXb = Buf("WX")
    P7t = [sb(f"P7_{i}", [128, 6, NP7], F32) for i in range(2)]; P7b = [Buf(), Buf()]
    P10t = [sb(f"P10_{i}", [128, KD, 3], F32) for i in range(2)]; P10b = [Buf(), Buf()]
    CVt = sb("CV", [128, 6], F32); CVb = Buf("CV")
    CVtmp = sb("CVtmp", [128, 6], F32); CVtmpb = Buf("CVtmp")
    NEGt = sb("NEGP", [128, 6, 4], F32); NEGb = Buf("NEGP")
    ONESt = sb("ONES", [128, 128], BF16); ONESb = Buf("ONES")
    EPSt = sb("EPSc", [128, 1], F32); ONEt = sb("ONEc", [128, 1], F32); CONSTb = Buf("const")
    KTPt = sb("KTP", [128, NH, NMEM], BF16); KTPb = Buf("KTP")
    VPt = sb("VP", [128, 2, MW], BF16); VPb = Buf("VP")
    HISTCt = sb("HISTC", [128, 6, HK], F32); HISTCb = [Buf() for _ in range(3)]
    HISTLt = sb("HISTL", [128, 6, LH], F32); HISTLb = [Buf() for _ in range(3)]
    HSTt = sb("HST", [128, 6], F32); HSTb = [Buf() for _ in range(6)]
    XBt = [sb(f"XB{i}", [128, KD, T], F32) for i in range(2)]; XBb = [Buf(), Buf()]
    XNt = sb("XN", [128, KD, T], BF16); XNb = Buf("XN")
    CATt = sb("CAT", [128, 16, T], BF16); CATb = [Buf(f"cat{i}") for i in range(16)]
    IDENTt = sb("IDENT", [128, 128], BF16); IDENTb = Buf("IDENT")
    DGRt = [sb(f"DGR{i}", [128, CK, 128], BF16) for i in range(2)]; DGRb = [Buf("dgr0"), Buf("dgr1")]
    dgr_state = {"i": 0}
    NSLOT = 41
    TMt = sb("TM", [128, NSLOT * SL], F32)
    TMb = [Buf(f"tm{i}") for i in range(NSLOT)]
    PBt = [nc.alloc_psum_tensor(f"PB{i}", [128, 512], F32) for i in range(8)]
    PBb = [Buf(f"pb{i}", excl=True) for i in range(8)]
    pstate = {"i": 0}

    def bank():
        i = pstate["i"]
        pstate["i"] = (i + 1) % 7
        return PBt[i], [PBb[i]]

    class TV:
        def __init__(self, s0, nel_f32, eoff=0):
            self.e0 = s0 * SL + eoff
            self.nel = nel_f32
            sa = self.e0 // SL
            s1 = (self.e0 + nel_f32 + SL - 1) // SL
            assert s1 <= NSLOT, (s0, nel_f32)
            self.b = TMb[sa:s1]

        def f(self):
            return TMt[:, self.e0:self.e0 + self.nel]

        def h(self):
            return TMt[:, self.e0:self.e0 + self.nel].bitcast(BF16)

    tk.op(POOL, lambda e: e.memset(ONESt[:], 1.0), [], [ONESb])
    tk.op(POOL, lambda e: e.memset(EPSt[:], EPS), [], [CONSTb])
    tk.op(POOL, lambda e: e.memset(ONEt[:], 1.0), [], [CONSTb])
    tk.dma(POOL, IDENTt[:], ident_d, [], [IDENTb])

    def load_weights(l):
        for (nm, c0, ncs) in SEGS:
            tk.dma(POOL, Wt[nm][:], w_in[l, :, c0:c0 + ncs * 128].rearrange("(k p) c -> p k c", p=128), [], [Wb[nm]])
        r0 = 0
        for i, n_ in enumerate((6, 6, 4)):
            tk.dma(POOL, WOt[i][:], w_out[l, r0:r0 + n_ * 128, :].rearrange("(k p) c -> p k c", p=128), [], [WOb[i]])
            r0 += n_ * 128
        tk.dma(POOL, WAt[:], wab[l], [], [WAb])
        tk.dma(POOL, WXt[:], wxb[l], [], [WXb])

    def load_params(l):
        pi = l % 2
        tk.dma(SP, P7t[pi][:], p768[l].rearrange("(j p) r -> p j r", p=128), [], [P7b[pi]])
        tk.dma(SP, P10t[pi][:], p1024[l].rearrange("(j p) r -> p j r", p=128), [], [P10b[pi]])

    def layer_consts(l):
        pi = l % 2
        P7 = P7t[pi]
        tk.op(ACT, lambda e: e.activation(out=CVtmp[:], in_=P7[:, :, R_LAM], func=AF.Exp, scale=-1.0), [P7b[pi]], [CVtmpb])
        tk.op(ACT, lambda e: e.activation(out=CVtmp[:], in_=CVtmp[:], func=AF.Ln, bias=ONEt[:], scale=1.0), [CVtmpb, CONSTb], [CVtmpb])
        tk.op(DVE, lambda e: e.tensor_scalar(out=CVt[:], in0=CVtmp[:], scalar1=-8.0, scalar2=None, op0=ALU.mult), [CVtmpb], [CVb])
        tk.op(DVE, lambda e: e.tensor_scalar(out=NEGt[:, :, 0:2], in0=P7[:, :, R_BA:R_BA + 2], scalar1=-1.0, scalar2=None, op0=ALU.mult), [P7b[pi]], [NEGb])
        tk.op(DVE, lambda e: e.tensor_scalar(out=NEGt[:, :, 2:4], in0=P7[:, :, R_LNG:R_LNG + 2], scalar1=-1.0, scalar2=None, op0=ALU.mult), [P7b[pi]], [NEGb])

    def diag_chunks(l, js):
        pi = l % 2
        for j in js:
            r = dgr_state["i"]; dgr_state["i"] = 1 - r
            tk.op(POOL, lambda e, j=j, r=r: e.tensor_tensor(out=DGRt[r][:], in0=IDENTt[:].unsqueeze(1).broadcast_to([128, CK, 128]),
                                                          in1=P7t[pi][:, j, R_CW:R_CW + CK].unsqueeze(2).broadcast_to([128, CK, 128]), op=ALU.mult),
                  [IDENTb, P7b[pi]], [DGRb[r]])
            tk.dma(SP, DGd[l, j], DGRt[r][:].rearrange("p k c -> p (k c)"), [DGRb[r]], [DGb[l][j]])
        if 5 in js:
            r = dgr_state["i"]; dgr_state["i"] = 1 - r
            for j in range(6):
                tk.op(POOL, lambda e, j=j, r=r: e.tensor_tensor(out=DGRt[r][:, j * LK:(j + 1) * LK, :], in0=IDENTt[:].unsqueeze(1).broadcast_to([128, LK, 128]),
                                                              in1=P7t[pi][:, j, R_LCW:R_LCW + LK].unsqueeze(2).broadcast_to([128, LK, 128]), op=ALU.mult),
                      [IDENTb, P7b[pi]], [DGRb[r]])
            tk.dma(SP, DGLd[l], DGRt[r][:, 0:6 * LK, :].rearrange("p k c -> p (k c)"), [DGRb[r]], [DGLb[l]])

    def mem_phase(l):
        pi = l % 2
        P10 = P10t[pi]
        MEM = TV(0, 8 * SL); MSQ = TV(8, 4 * SL); MN = TV(8, 4 * SL); RM = TV(12, NMEM)
        WKv = TV(13, 8 * SL); WVv = TV(21, 8 * SL); OF = [TV(29, 2 * SL), TV(31, 2 * SL)]
        assert NMEM == SL
        memf = MEM.f().rearrange("p (k m) -> p k m", k=KD)
        msq = MSQ.h().rearrange("p (k m) -> p k m", k=KD)
        mn = MN.h().rearrange("p (k m) -> p k m", k=KD)
        wkv = WKv.h().rearrange("p (k c) -> p k c", k=KD)
        wvv = WVv.h().rearrange("p (k c) -> p k c", k=KD)
        tk.dma(SP, memf, memT.rearrange("(k p) m -> p k m", p=128), [], MEM.b)
        tk.dma(POOL, wkv, wk[l].rearrange("(k p) c -> p k c", p=128), [], WKv.b)
        tk.dma(POOL, wvv, wv[l].rearrange("(k p) c -> p k c", p=128), [], WVv.b)
        tk.op(ACT, lambda e: e.activation(out=msq, in_=memf, func=AF.Square), MEM.b, MSQ.b)
        pb, pbb = bank()
        tk.group(PE, [(lambda e, kc=kc: e.matmul(pb[:, 0:NMEM], lhsT=ONESt[:], rhs=msq[:, kc, :], start=(kc == 0), stop=(kc == KD - 1)))
                      for kc in range(KD)], MSQ.b + [ONESb], pbb)
        tk.op(ACT, lambda e: e.activation(out=RM.f(), in_=pb[:, 0:NMEM], func=AF.Ln, bias=EPSt[:], scale=1.0 / D), pbb + [CONSTb], RM.b)
        tk.op(ACT, lambda e: e.activation(out=RM.f(), in_=RM.f(), func=AF.Exp, scale=-0.5), RM.b, RM.b)
        for kc in range(KD):
            tk.op(DVE, lambda e, kc=kc: e.scalar_tensor_tensor(out=mn[:, kc, :], in0=memf[:, kc, :], scalar=P10[:, kc, 2:3], in1=RM.f(),
                                                             op0=ALU.mult, op1=ALU.mult), MEM.b + RM.b + [P10b[pi]], MN.b)
        for hp in range(2):
            pb, pbb = bank()
            for s in range(2):
                h = 2 * hp + s
                tk.group(PE, [(lambda e, kc=kc, h=h, s=s: e.matmul(pb[:, s * NMEM:(s + 1) * NMEM], lhsT=wkv[:, kc, h * 128:(h + 1) * 128],
                                                                rhs=mn[:, kc, :], start=(kc == 0), stop=(kc == KD - 1))) for kc in range(KD)],
                         WKv.b + MN.b, pbb)
            tk.op(ACT, lambda e, hp=hp: e.activation(out=KTPt[:, 2 * hp:2 * hp + 2, :], in_=pb[:, :].rearrange("p (s m) -> p s m", s=2), func=AF.Copy),
                  pbb, [KTPb])
            of = OF[hp % 2]
            import os
            dbg = os.environ.get("DBG", "")
            if "A" in dbg:
                tk.op(ACT, lambda e: e.activation(out=of.f(), in_=pb[:, :], func=AF.Copy), pbb, of.b)
            elif "S" in dbg:
                tk.op(DVE, lambda e: e.tensor_copy(out=of.f()[:, 0:256], in_=RM.f()), pbb + RM.b, of.b)
            elif "N" in dbg:
                tk.op(DVE, lambda e: e.tensor_copy(out=of.f(), in_=pb[:, :]), [], of.b)
            elif "T" in dbg:
                tk.op(DVE, lambda e: e.tensor_scalar(out=of.f(), in0=pb[:, :], scalar1=1.0, scalar2=None, op0=ALU.mult), pbb, of.b)
            elif "H" in dbg:
                tk.op(DVE, lambda e: e.tensor_copy(out=of.f()[:, 0:256], in_=pb[:, 0:256]), pbb, of.b)
                tk.op(DVE, lambda e: e.tensor_copy(out=of.f()[:, 256:512], in_=pb[:, 256:512]), pbb, of.b)
            else:
                tk.op(DVE, lambda e: e.tensor_copy(out=of.f(), in_=pb[:, :]), pbb, of.b)
            tk.dma(SP, pmkT[l, hp * 256:(hp + 1) * 256, :].rearrange("(s p) m -> p s m", p=128), of.f().rearrange("p (s m) -> p s m", s=2),
                   of.b, [], is_output=True)
        for mc in range(2):
            pb, pbb = bank()
            tk.group(PE, [(lambda e, kc=kc, mc=mc: e.matmul(pb[:, :], lhsT=mn[:, kc, mc * 128:(mc + 1) * 128], rhs=wvv[:, kc, :],
                                                          start=(kc == 0), stop=(kc == KD - 1))) for kc in range(KD)], WVv.b + MN.b, pbb)
            tk.op(ACT, lambda e, mc=mc: e.activation(out=VPt[:, mc, :], in_=pb[:, :], func=AF.Copy), pbb, [VPb])
            of = OF[mc % 2]
            tk.op(DVE, lambda e: e.tensor_copy(out=of.f(), in_=pb[:, :]), pbb, of.b)
            tk.dma(SP, pmv[l, mc * 128:(mc + 1) * 128, :], of.f(), of.b, [], is_output=True)

    class Grp:
        pass

    def make_groups():
        gs = []
        for ti in range(SEQ // T):
            g = Grp(); g.kind = "p"; g.n = T; g.nseq = 1; g.tlen = T; g.c0 = ti * T
            g.first = (ti == 0); g.last = (ti == SEQ // T - 1); g.idx = ti
            gs.append(g)
        g = Grp(); g.kind = "s"; g.n = NS; g.nseq = NB; g.tlen = TS; g.c0 = SEQ; g.first = True; g.last = True; g.idx = SEQ // T
        gs.append(g)
        return gs

    groups = make_groups()
    NG = len(groups)
    scrb = [[Buf() for _ in range(NG)] for _ in range(2)]
    xslot = {"i": 0}

    def proj_pair(seg, j0, n, npair=2):
        pb, pbb = bank()
        for s in range(npair):
            j = j0 + s
            tk.group(PE, [(lambda e, kc=kc, j=j, s=s: e.matmul(pb[:, s * n:(s + 1) * n], lhsT=Wt[seg][:, kc, j * 128:(j + 1) * 128],
                                                             rhs=XNt[:, kc, 0:n], start=(kc == 0), stop=(kc == KD - 1))) for kc in range(KD)],
                     [Wb[seg], XNb], pbb)
        return pb, pbb

    def rsqrt_act(out_tv, in_ap, rd, scale):
        tk.op(ACT, lambda e: e.activation(out=out_tv.f(), in_=in_ap, func=AF.Ln, bias=EPSt[:], scale=scale), rd + [CONSTb], out_tv.b)
        tk.op(ACT, lambda e: e.activation(out=out_tv.f(), in_=out_tv.f(), func=AF.Exp, scale=-0.5), out_tv.b, out_tv.b)

    def sigmoid_act(out_ap, out_b, in_ap, rd, nscale=-1.0, nbias=None):
        if nbias is None:
            tk.op(ACT, lambda e: e.activation(out=out_ap, in_=in_ap, func=AF.Exp, scale=nscale), rd, out_b)
        else:
            tk.op(ACT, lambda e: e.activation(out=out_ap, in_=in_ap, func=AF.Exp, bias=nbias, scale=nscale), rd, out_b)
        tk.op(ACT, lambda e: e.activation(out=out_ap, in_=out_ap, func=AF.Ln, bias=ONEt[:], scale=1.0), out_b + [CONSTb], out_b)
        tk.op(ACT, lambda e: e.activation(out=out_ap, in_=out_ap, func=AF.Exp, scale=-1.0), out_b, out_b)

    def stage_A(l, g):
        pi = l % 2
        n = g.n
        xi = xslot["i"]; xslot["i"] = 1 - xi
        g.xi = xi
        X = XBt[xi]
        if l == 0:
            src = (xpT[:, g.c0:g.c0 + n] if g.kind == "p" else xsT[:, 0:n])
            rd = []
        else:
            src = xscr[(l - 1) % 2][:, g.c0:g.c0 + n]
            rd = [scrb[(l - 1) % 2][g.idx]]
        tk.dma(SP, X[:, :, 0:n], src.rearrange("(k p) t -> p k t", p=128), rd, [XBb[xi]])
        SQ = TV(0, 4 * SL); RT = TV(4, n)
        sq = SQ.h().rearrange("p (k t) -> p k t", k=KD)
        tk.op(ACT, lambda e: e.activation(out=sq[:, :, 0:n], in_=X[:, :, 0:n], func=AF.Square), [XBb[xi]], SQ.b)
        pb, pbb = bank()
        tk.group(PE, [(lambda e, kc=kc: e.matmul(pb[:, 0:n], lhsT=ONESt[:], rhs=sq[:, kc, 0:n], start=(kc == 0), stop=(kc == KD - 1)))
                      for kc in range(KD)], SQ.b + [ONESb], pbb)
        rsqrt_act(RT, pb[:, 0:n], pbb, 1.0 / D)
        for kc in range(KD):
            tk.op(DVE, lambda e, kc=kc: e.scalar_tensor_tensor(out=XNt[:, kc, 0:n], in0=X[:, kc, 0:n], scalar=P10t[pi][:, kc, 0:1], in1=RT.f(),
                                                             op0=ALU.mult, op1=ALU.mult), [XBb[xi], P10b[pi]] + RT.b, [XNb])

    def stage_B(l, g, ck=lambda lv: None, after_conv=None, after_lru=None):
        pi = l % 2
        P7 = P7t[pi]; P7B = P7b[pi]
        n, nseq, tlen = g.n, g.nseq, g.tlen
        isp = (g.kind == "p")

        def v3(ap2):
            return ap2.rearrange("p (s t) -> p s t", s=nseq)

        ulen = HK + tlen
        SIGs = [TV(5 + 2 * i, 2 * n) for i in range(3)]
        UPBs = [TV(11 + 3 * i, nseq * ulen) for i in range(3)]
        UP = TV(20, 2 * nseq * ulen)
        CS = TV(25, 6 * n)
        CSB = TV(31, n); CSQ = TV(32, n)
        MEAN = TV(5, n); MSQv = TV(6, n); VAR = TV(7, n)
        cs = CS.f().rearrange("p (j t) -> p j t", j=6)
        up = UP.f().rearrange("p (j s u) -> p j s u", j=2, s=nseq)
        csb = CSB.h().rearrange("p (j t) -> p j t", j=2)
        csq = CSQ.h().rearrange("p (j t) -> p j t", j=2)
        upbs = [u_.h().rearrange("p (j s u) -> p j s u", j=2, s=nseq) for u_ in UPBs]
        for jp in range(3):
            j0 = 2 * jp
            SIG = SIGs[jp]; UPB = UPBs[jp]; upb = upbs[jp]
            pbB, pbBb = proj_pair("b", j0, n)
            sigmoid_act(SIG.f(), SIG.b, pbB[:, 0:2 * n], pbBb)
            pbA, pbAb = proj_pair("a", j0, n)
            if isp:
                if g.first:
                    tk.op(POOL, lambda e: e.memset(up[:, :, :, 0:HK], 0.0), [], UP.b)
                else:
                    tk.op(POOL, lambda e: e.tensor_copy(out=up[:, :, 0, 0:HK], in_=HISTCt[:, j0:j0 + 2, :]), [HISTCb[jp]], UP.b)
            else:
                STG = TV(27, 2 * NB * HK)
                stg = STG.f().rearrange("p (j s r) -> p j s r", j=2, s=nseq)
                tk.dma(SP, STG.f().rearrange("p (j q) -> p j q", j=2),
                       cconvT[l, j0 * 128:(j0 + 2) * 128].rearrange("(j p) b r -> p j (b r)", p=128), [], STG.b)
                tk.op(POOL, lambda e, stg=stg: e.tensor_copy(out=up[:, :, :, 0:HK], in_=stg), STG.b, UP.b)
            tk.op(POOL, lambda e, upb=upb: e.tensor_copy(out=upb[:, :, :, 0:HK], in_=up[:, :, :, 0:HK]), UP.b, UPB.b)
            a4 = pbA[:, 0:2 * n].rearrange("p (j s t) -> p j s t", j=2, s=nseq)
            s4 = SIG.f().rearrange("p (j s t) -> p j s t", j=2, s=nseq)
            tk.op(DVE, lambda e, a4=a4, s4=s4: e.tensor_tensor(out=up[:, :, :, HK:HK + tlen], in0=a4, in1=s4, op=ALU.mult), pbAb + SIG.b, UP.b)
            tk.op(DVE, lambda e, a4=a4, s4=s4, upb=upb: e.tensor_tensor(out=upb[:, :, :, HK:HK + tlen], in0=a4, in1=s4, op=ALU.mult), pbAb + SIG.b, UPB.b)
            if isp:
                if g.last:
                    tk.dma(SP, pconvT[l, j0 * 128:(j0 + 2) * 128, :].rearrange("(j p) r -> p j r", p=128), up[:, :, 0, tlen:tlen + HK], UP.b, [], is_output=True)
                else:
                    tk.op(POOL, lambda e: e.tensor_copy(out=HISTCt[:, j0:j0 + 2, :], in_=up[:, :, 0, tlen:tlen + HK]), UP.b, [HISTCb[jp]])
            else:
                tk.op(POOL, lambda e, stg=stg: e.tensor_copy(out=stg, in_=up[:, :, :, tlen:tlen + HK]), UP.b, STG.b)
                tk.dma(SP, sconvT[l, j0 * 128:(j0 + 2) * 128].rearrange("(j p) b r -> p j (b r)", p=128),
                       STG.f().rearrange("p (j q) -> p j q", j=2), STG.b, [], is_output=True)
        if isp:
            SGcs = [TV(0, 2 * n), TV(2, 2 * n), TV(23, 2 * n)]
        else:
            SGcs = [TV(0, 2 * n), TV(2, 2 * n), TV(4, 2 * n)]
        for jp in range(3):
            pbG, pbGb = proj_pair("gc", 2 * jp, n)
            sigmoid_act(SGcs[jp].f(), SGcs[jp].b, pbG[:, 0:2 * n], pbGb)
            tk.op(DVE, lambda e, jp=jp, pbG=pbG: e.tensor_tensor(out=SGcs[jp].f(), in0=SGcs[jp].f(), in1=pbG[:, 0:2 * n], op=ALU.mult),
                  SGcs[jp].b + pbGb, SGcs[jp].b)
        pst, pstb = PBt[7], [PBb[7]]
        for jp in range(3):
            j0 = 2 * jp
            upb = upbs[jp]; UPB = UPBs[jp]
            pbc, pbcb = bank()
            for s in range(2):
                j = j0 + s
                r = dgr_state["i"]; dgr_state["i"] = 1 - r
                tk.dma(SP, DGRt[r][:].rearrange("p k c -> p (k c)"), DGd[l, j], [DGb[l][j]], [DGRb[r]])
                tk.group(PE, [(lambda e, k=k, s=s, r=r, upb=upb: e.matmul(pbc[:, s * n:(s + 1) * n], lhsT=DGRt[r][:, k, :], rhs=upb[:, s, :, k:k + tlen],
                                                                        start=(k == 0), stop=(k == CK - 1))) for k in range(CK)],
                         [DGRb[r]] + UPB.b, pbcb)
            for s in range(2):
                j = j0 + s
                tk.op(ACT, lambda e, s=s, j=j: e.activation(out=cs[:, j, :], in_=pbc[:, s * n:(s + 1) * n], func=AF.Identity,
                                                           bias=P7[:, j, R_CB:R_CB + 1], scale=1.0), pbcb + [P7B], CS.b)
            tk.op(ACT, lambda e, j0=j0: e.activation(out=csb, in_=cs[:, j0:j0 + 2, :], func=AF.Copy), CS.b, CSB.b)
            tk.op(ACT, lambda e, j0=j0: e.activation(out=csq, in_=cs[:, j0:j0 + 2, :], func=AF.Square), CS.b, CSQ.b)
            tk.group(PE, [(lambda e, s=s: e.matmul(pst[:, 0:n], lhsT=ONESt[:], rhs=csb[:, s, :], start=(jp == 0 and s == 0), stop=(jp == 2 and s == 1),
                                                   skip_group_check=True)) for s in range(2)], CSB.b + [ONESb], pstb)
            tk.group(PE, [(lambda e, s=s: e.matmul(pst[:, n:2 * n], lhsT=ONESt[:], rhs=csq[:, s, :], start=False, stop=(jp == 2 and s == 1),
                                                   skip_group_check=True)) for s in range(2)], CSQ.b + [ONESb], pstb)
        if after_conv is not None:
            after_conv()
        tk.op(DVE, lambda e: e.tensor_scalar(out=MEAN.f(), in0=pst[:, 0:n], scalar1=1.0 / CW, scalar2=None, op0=ALU.mult), pstb, MEAN.b)
        tk.op(DVE, lambda e: e.tensor_tensor(out=MSQv.f(), in0=MEAN.f(), in1=MEAN.f(), op=ALU.mult), MEAN.b, MSQv.b)
        tk.op(DVE, lambda e: e.scalar_tensor_tensor(out=VAR.f(), in0=pst[:, n:2 * n], scalar=1.0 / CW, in1=MSQv.f(), op0=ALU.mult, op1=ALU.subtract),
              pstb + MSQv.b, VAR.b)
        rsqrt_act(VAR, VAR.f(), VAR.b, 1.0)
        TT = TV(8, 6 * n); ZZ = TV(14, 6 * n)
        tt = TT.f().rearrange("p (j t) -> p j t", j=6)
        zz = ZZ.f().rearrange("p (j t) -> p j t", j=6)
        mean_b = MEAN.f().unsqueeze(1).broadcast_to([128, 6, n])
        rs_b = VAR.f().unsqueeze(1).broadcast_to([128, 6, n])
        tk.op(DVE, lambda e: e.tensor_tensor(out=tt, in0=cs, in1=mean_b, op=ALU.subtract), CS.b + MEAN.b, TT.b)
        B0 = 5
        xlen = LH + tlen
        XRP = TV(33, 2 * nseq * xlen)
        XC = TV(25, 6 * n); XCB = TV(20, 3 * n)
        T0 = TV(4, nseq)
        H0v = TV(3, 6 * NB); SHOv = TV(40, 6 * NB)
        H0t = H0v.f().rearrange("p (j b) -> p j b", j=6); SHOt = SHOv.f().rearrange("p (j b) -> p j b", j=6)
        H0b = None; SHOb = None
        xc = XC.f().rearrange("p (j t) -> p j t", j=6)
        xcb = XCB.h().rearrange("p (j t) -> p j t", j=6)
        xrp = XRP.f().rearrange("p (j s u) -> p j s u", j=2, s=nseq)
        if not isp:
            tk.dma(SP, H0t, h0T[l].rearrange("(j p) b -> p j b", p=128), [], H0v.b)
        XRPB = TV(36, nseq * xlen)
        xrpb = XRPB.h().rearrange("p (j s u) -> p j s u", j=2, s=nseq)
        rl = dgr_state["i"]; dgr_state["i"] = 1 - rl
        tk.dma(SP, DGRt[rl][:, 0:6 * LK, :].rearrange("p k c -> p (k c)"), DGLd[l], [DGLb[l]], [DGRb[rl]])
        for jp in range(3):
            j0 = 2 * jp
            pbX, pbXb = proj_pair("xr", j0, n)
            if isp:
                if g.first:
                    tk.op(POOL, lambda e: e.memset(xrp[:, :, :, 0:LH], 0.0), [], XRP.b)
                else:
                    tk.op(POOL, lambda e: e.tensor_copy(out=xrp[:, :, 0, 0:LH], in_=HISTLt[:, j0:j0 + 2, :]), [HISTLb[jp]], XRP.b)
            else:
                STL = TV(40, 2 * NB * LH)
                stl = STL.f().rearrange("p (j s r) -> p j s r", j=2, s=nseq)
                tk.dma(SP, STL.f().rearrange("p (j q) -> p j q", j=2),
                       clruT[l, j0 * 128:(j0 + 2) * 128].rearrange("(j p) b r -> p j (b r)", p=128), [], STL.b)
                tk.op(POOL, lambda e, stl=stl: e.tensor_copy(out=xrp[:, :, :, 0:LH], in_=stl), STL.b, XRP.b)
            tk.op(POOL, lambda e: e.tensor_copy(out=xrpb[:, :, :, 0:LH], in_=xrp[:, :, :, 0:LH]), XRP.b, XRPB.b)
            x4 = pbX[:, 0:2 * n].rearrange("p (j s t) -> p j s t", j=2, s=nseq)
            tk.op(ACT, lambda e, x4=x4: e.activation(out=xrp[:, :, :, LH:LH + tlen], in_=x4, func=AF.Copy), pbXb, XRP.b)
            tk.op(ACT, lambda e, x4=x4: e.activation(out=xrpb[:, :, :, LH:LH + tlen], in_=x4, func=AF.Copy), pbXb, XRPB.b)
            if isp:
                if g.last:
                    tk.dma(SP, plconvT[l, j0 * 128:(j0 + 2) * 128, :].rearrange("(j p) r -> p j r", p=128), xrp[:, :, 0, tlen:tlen + LH], XRP.b, [], is_output=True)
                else:
                    tk.op(POOL, lambda e: e.tensor_copy(out=HISTLt[:, j0:j0 + 2, :], in_=xrp[:, :, 0, tlen:tlen + LH]), XRP.b, [HISTLb[jp]])
            else:
                tk.op(POOL, lambda e, stl=stl: e.tensor_copy(out=stl, in_=xrp[:, :, :, tlen:tlen + LH]), XRP.b, STL.b)
                tk.dma(SP, slconvT[l, j0 * 128:(j0 + 2) * 128].rearrange("(j p) b r -> p j (b r)", p=128),
                       STL.f().rearrange("p (j q) -> p j q", j=2), STL.b, [], is_output=True)
            pbx, pbxb = bank()
            for s in range(2):
                j = j0 + s
                tk.group(PE, [(lambda e, k=k, s=s, j=j: e.matmul(pbx[:, s * n:(s + 1) * n], lhsT=DGRt[rl][:, j * LK + k, :], rhs=xrpb[:, s, :, k:k + tlen],
                                                               start=(k == 0), stop=(k == LK - 1))) for k in range(LK)],
                         [DGRb[rl]] + XRPB.b, pbxb)
            for s in range(2):
                j = j0 + s
                tk.op(ACT, lambda e, s=s, j=j, pbx=pbx: e.activation(out=xc[:, j, :], in_=pbx[:, s * n:(s + 1) * n], func=AF.Identity,
                                                                    bias=P7[:, j, R_LCB:R_LCB + 1], scale=1.0), pbxb + [P7B], XC.b)
        tk.op(ACT, lambda e: e.activation(out=xcb, in_=xc, func=AF.Copy), XC.b, XCB.b)
        tk.op(DVE, lambda e: e.tensor_tensor(out=tt, in0=tt, in1=rs_b, op=ALU.mult), TT.b + VAR.b, TT.b)
        for j in range(6):
            tk.op(DVE, lambda e, j=j: e.tensor_scalar(out=zz[:, j, :], in0=tt[:, j, :], scalar1=P7[:, j, R_LNG:R_LNG + 1], scalar2=P7[:, j, R_LNB:R_LNB + 1],
                                                     op0=ALU.mult, op1=ALU.add), TT.b + [P7B], ZZ.b)
        for j in range(6):
            tk.op(ACT, lambda e, j=j: e.activation(out=tt[:, j, :], in_=tt[:, j, :], func=AF.Exp, bias=NEGt[:, j, 3:4], scale=NEGt[:, j, 2:3]),
                  TT.b + ZZ.b + [NEGb], TT.b)
        tk.op(ACT, lambda e: e.activation(out=TT.f(), in_=TT.f(), func=AF.Ln, bias=ONEt[:], scale=1.0), TT.b + [CONSTb], TT.b)
        tk.op(ACT, lambda e: e.activation(out=TT.f(), in_=TT.f(), func=AF.Exp, scale=-1.0), TT.b, TT.b)
        tk.op(DVE, lambda e: e.tensor_tensor(out=ZZ.f(), in0=ZZ.f(), in1=TT.f(), op=ALU.mult), ZZ.b + TT.b, ZZ.b)
        for jp in range(3):
            j0 = 2 * jp
            tk.op(DVE, lambda e, j0=j0, jp=jp: e.tensor_tensor(out=CATt[:, j0:j0 + 2, 0:n], in0=zz[:, j0:j0 + 2, :],
                                                              in1=SGcs[jp].f().rearrange("p (j t) -> p j t", j=2), op=ALU.mult),
                  ZZ.b + SGcs[jp].b, CATb[j0:j0 + 2])

        ck(5)
        blk_of = {}
        for bi, (mo, kc) in enumerate(GBLK):
            blk_of.setdefault(mo, []).append((bi, kc))
        SG2s = [TV(0, 2 * n), TV(2, 2 * n), TV(23, 2 * n)]
        for jp in range(3):
            pbG, pbGb = proj_pair("gr", 2 * jp, n)
            sigmoid_act(SG2s[jp].f(), SG2s[jp].b, pbG[:, 0:2 * n], pbGb)
            tk.op(DVE, lambda e, jp=jp: e.tensor_tensor(out=SG2s[jp].f(), in0=SG2s[jp].f(), in1=pbG[:, 0:2 * n], op=ALU.mult), SG2s[jp].b + pbGb, SG2s[jp].b)
        QT = TV(33, 2 * n); SGQ = TV(35, 4 * n)
        HB = max(1, min(NH, 512 // (2 * n)))
        qt = QT.h().rearrange("p (h t) -> p h t", h=NH)
        sgq = SGQ.f().rearrange("p (h t) -> p h t", h=NH)
        for jp in range(2):
            pbQ, pbQb = proj_pair("q", 2 * jp, n)
            tk.op(ACT, lambda e, jp=jp: e.activation(out=qt[:, 2 * jp:2 * jp + 2, :], in_=pbQ[:, 0:2 * n].rearrange("p (h t) -> p h t", h=2), func=AF.Copy),
                  pbQb, QT.b)
            pbG, pbGb = proj_pair("gq", 2 * jp, n)
            g3 = pbG[:, 0:2 * n].rearrange("p (h t) -> p h t", h=2)
            sigmoid_act(sgq[:, 2 * jp:2 * jp + 2, :], SGQ.b, g3, pbGb)
            tk.op(DVE, lambda e, jp=jp, g3=g3: e.tensor_tensor(out=sgq[:, 2 * jp:2 * jp + 2, :], in0=sgq[:, 2 * jp:2 * jp + 2, :], in1=g3, op=ALU.mult),
                  SGQ.b + pbGb, SGQ.b)
        sets = []
        for (base, a2s, hhs) in ((5, 9, 11), (13, 17, 39)):
            sets.append((TV(base, 2 * n), TV(base, 2 * n, eoff=2 * n), TV(a2s, 2 * n), TV(hhs, 2 * n)))
        for bt in range(3):
            m0 = 2 * bt
            RGv, IGv, A2v, HHv = sets[bt % 2]
            rg = RGv.f().rearrange("p (j t) -> p j t", j=2); ig = IGv.f().rearrange("p (j t) -> p j t", j=2)
            hhv = HHv.f().rearrange("p (j t) -> p j t", j=2)
            for q in range(2):
                mo = m0 + q
                pb, pbb = bank()
                lst = blk_of[mo]
                tk.group(PE, [(lambda e, bi=bi, kc=kc, i=i: e.matmul(pb[:, 0:n], lhsT=WAt[:, bi, :], rhs=xcb[:, kc, :], start=(i == 0), stop=(i == len(lst) - 1)))
                              for i, (bi, kc) in enumerate(lst)], [WAb] + XCB.b, pbb)
                tk.group(PE, [(lambda e, bi=bi, kc=kc, i=i: e.matmul(pb[:, n:2 * n], lhsT=WXt[:, bi, :], rhs=xcb[:, kc, :], start=(i == 0), stop=(i == len(lst) - 1)))
                              for i, (bi, kc) in enumerate(lst)], [WXb] + XCB.b, pbb)
                tk.op(ACT, lambda e, mo=mo, q=q, pb=pb: e.activation(out=rg[:, q, :], in_=pb[:, 0:n], func=AF.Exp, bias=NEGt[:, mo, 0:1], scale=-1.0), pbb + [NEGb], RGv.b)
                tk.op(ACT, lambda e, mo=mo, q=q, pb=pb: e.activation(out=ig[:, q, :], in_=pb[:, n:2 * n], func=AF.Exp, bias=NEGt[:, mo, 1:2], scale=-1.0), pbb + [NEGb], IGv.b)
            RI = TV((5, 13)[bt % 2], 4 * n)
            tk.op(ACT, lambda e, RI=RI: e.activation(out=RI.f(), in_=RI.f(), func=AF.Ln, bias=ONEt[:], scale=1.0), RI.b + [CONSTb], RI.b)
            tk.op(ACT, lambda e, RI=RI: e.activation(out=RI.f(), in_=RI.f(), func=AF.Exp, scale=-1.0), RI.b, RI.b)
            for q in range(2):
                mo = m0 + q
                tk.op(ACT, lambda e, mo=mo, q=q: e.activation(out=rg[:, q, :], in_=rg[:, q, :], func=AF.Exp, scale=CVt[:, mo:mo + 1]), RGv.b + [CVb], RGv.b)
            tk.op(DVE, lambda e: e.tensor_tensor(out=A2v.f(), in0=RGv.f(), in1=RGv.f(), op=ALU.mult), RGv.b, A2v.b)
            tk.op(ACT, lambda e: e.activation(out=A2v.f(), in_=A2v.f(), func=AF.Ln, bias=ONEt[:], scale=-1.0), A2v.b + [CONSTb], A2v.b)
            tk.op(ACT, lambda e: e.activation(out=A2v.f(), in_=A2v.f(), func=AF.Exp, scale=0.5), A2v.b, A2v.b)
            tk.op(DVE, lambda e, m0=m0: e.tensor_tensor(out=ig, in0=xc[:, m0:m0 + 2, :], in1=ig, op=ALU.mult), IGv.b + XC.b, IGv.b)
            tk.op(DVE, lambda e: e.tensor_tensor(out=IGv.f(), in0=IGv.f(), in1=A2v.f(), op=ALU.mult), IGv.b + A2v.b, IGv.b)
            for q in range(2):
                mo = m0 + q
                aq = rg[:, q, :]; bq = ig[:, q, :]; hq = hhv[:, q, :]
                if isp:
                    if g.first:
                        init = 0.0; rdi = []
                    else:
                        init = HSTt[:, mo:mo + 1]; rdi = [HSTb[mo]]
                else:
                    aa3 = v3(aq); bx3 = v3(bq)
                    tk.op(DVE, lambda e, mo=mo, aa3=aa3: e.tensor_tensor(out=T0.f(), in0=aa3[:, :, 0], in1=H0t[:, mo, :], op=ALU.mult), RGv.b + H0v.b, T0.b)
                    tk.op(DVE, lambda e, bx3=bx3: e.tensor_tensor(out=bx3[:, :, 0], in0=bx3[:, :, 0], in1=T0.f(), op=ALU.add), IGv.b + T0.b, IGv.b)
                    tk.op(DVE, lambda e, aa3=aa3: e.memset(aa3[:, :, 0], 0.0), RGv.b, RGv.b)
                    init = 0.0; rdi = []
                tk.op(DVE, lambda e, init=init, aq=aq, bq=bq, hq=hq: e.tensor_tensor_scan(out=hq, data0=aq, data1=bq, initial=init, op0=ALU.mult, op1=ALU.add),
                      RGv.b + IGv.b + rdi, HHv.b)
                if isp:
                    tk.op(POOL, lambda e, mo=mo, hq=hq: e.tensor_copy(out=HSTt[:, mo:mo + 1], in_=hq[:, n - 1:n]), HHv.b, [HSTb[mo]])
                else:
                    tk.op(POOL, lambda e, mo=mo, hq=hq: e.tensor_copy(out=SHOt[:, mo, :], in_=v3(hq)[:, :, tlen - 1]), HHv.b, SHOv.b)
            sgs = SG2s[bt]
            tk.op(DVE, lambda e, m0=m0, sgs=sgs: e.tensor_tensor(out=CATt[:, 6 + m0:6 + m0 + 2, 0:n], in0=hhv, in1=sgs.f().rearrange("p (j t) -> p j t", j=2), op=ALU.mult),
                  HHv.b + sgs.b, CATb[6 + m0:6 + m0 + 2])
        if isp:
            if g.last:
                tk.dma(SP, phT[l].rearrange("(j p) o -> p (j o)", p=128), HSTt[:], HSTb, [], is_output=True, slow=True)
        else:
            tk.dma(SP, shT[l].rearrange("(j p) b -> p j b", p=128), SHOt, SHOv.b, [], is_output=True)

        if after_lru is not None:
            after_lru()
        ck(6)
        B0 = 5
        HB = max(1, min(NH, 512 // (2 * n)))
        PT = [TV(B0 + 6, HB * n), TV(B0 + 7, HB * n)]
        RD = TV(B0 + 8, HB * n); OT = TV(B0 + 9, HB * n)
        sc = 1.0 / math.sqrt(128.0)
        if not isp:
            kring = []
            vring = []
            for i in range(4):
                kv_ = TV(15 + 2 * i, NH * NMEM // 2)
                kring.append((kv_.h().rearrange("p (h m) -> p h m", h=NH), kv_.b))
                vv_ = TV(23 + 2 * i, 2 * MW // 2)
                vring.append((vv_.h().rearrange("p (c d) -> p c d", c=2), vv_.b))
            NR = len(kring)
        for hg in range(NH // HB):
            h0 = hg * HB
            pbS, pbSb = bank()
            pbO, pbOb = bank()
            sS = pbS[:, 0:HB * 2 * n].rearrange("p (h c t) -> p h c t", h=HB, c=2)
            sO = pbO[:, 0:HB * 2 * n].rearrange("p (h c t) -> p h c t", h=HB, c=2)
            pt = PT[hg % 2]
            ptv = pt.h().rearrange("p (h c t) -> p h c t", h=HB, c=2)
            if isp:
                for hh in range(HB):
                    h = h0 + hh
                    for mc in range(2):
                        tk.group(PE, [lambda e, hh=hh, h=h, mc=mc: e.matmul(sS[:, hh, mc, 0:n], lhsT=KTPt[:, h, mc * 128:(mc + 1) * 128], rhs=qt[:, h, 0:n],
                                                                          start=True, stop=True)], [KTPb] + QT.b, pbSb)
            else:
                for b in range(NB):
                    kap, kb_ = kring[b % NR]
                    tk.dma(POOL, kap, ckT[l, b].rearrange("(h d) m -> d h m", d=128), [], kb_)
                    c0, c1 = b * TS, (b + 1) * TS
                    fns = []
                    for hh in range(HB):
                        h = h0 + hh
                        for mc in range(2):
                            fns.append(lambda e, hh=hh, h=h, mc=mc, kap=kap, c0=c0, c1=c1: e.matmul(sS[:, hh, mc, c0:c1], lhsT=kap[:, h, mc * 128:(mc + 1) * 128],
                                                                                                  rhs=qt[:, h, c0:c1], start=True, stop=True))
                    tk.group(PE, fns, kb_ + QT.b, pbSb)
            tk.op(ACT, lambda e: e.activation(out=ptv, in_=sS, func=AF.Exp, scale=sc), pbSb, pt.b)
            for hh in range(HB):
                tk.group(PE, [(lambda e, hh=hh, mc=mc: e.matmul(sO[:, hh, 0, :], lhsT=ONESt[:], rhs=ptv[:, hh, mc, :], start=(mc == 0), stop=(mc == 1)))
                              for mc in range(2)], pt.b + [ONESb], pbOb)
            if isp:
                for hh in range(HB):
                    h = h0 + hh
                    tk.group(PE, [(lambda e, hh=hh, h=h, mc=mc: e.matmul(sO[:, hh, 1, 0:n], lhsT=VPt[:, mc, h * 128:(h + 1) * 128], rhs=ptv[:, hh, mc, 0:n],
                                                                       start=(mc == 0), stop=(mc == 1))) for mc in range(2)], [VPb] + pt.b, pbOb)
            else:
                for b in range(NB):
                    vap, vb_ = vring[b % NR]
                    tk.dma(POOL, vap, cv[l, b].rearrange("(c p) d -> p c d", p=128), [], vb_)
                    c0, c1 = b * TS, (b + 1) * TS
                    for hh in range(HB):
                        h = h0 + hh
                        tk.group(PE, [(lambda e, hh=hh, h=h, mc=mc, vap=vap, c0=c0, c1=c1: e.matmul(sO[:, hh, 1, c0:c1], lhsT=vap[:, mc, h * 128:(h + 1) * 128],
                                                                                                  rhs=ptv[:, hh, mc, c0:c1], start=(mc == 0), stop=(mc == 1)))
                                      for mc in range(2)], vb_ + pt.b, pbOb)
            rd = RD.f().rearrange("p (h t) -> p h t", h=HB)
            ot = OT.f().rearrange("p (h t) -> p h t", h=HB)
            tk.op(DVE, lambda e: e.reciprocal(out=rd, in_=sO[:, :, 0, :]), pbOb, RD.b)
            tk.op(DVE, lambda e: e.tensor_tensor(out=ot, in0=sO[:, :, 1, :], in1=rd, op=ALU.mult), pbOb + RD.b, OT.b)
            tk.op(DVE, lambda e, h0=h0: e.tensor_tensor(out=CATt[:, 12 + h0:12 + h0 + HB, 0:n], in0=ot, in1=sgq[:, h0:h0 + HB, :], op=ALU.mult),
                  OT.b + SGQ.b, CATb[12 + h0:12 + h0 + HB])

    def stage_C(l, g):
        pi = l % 2
        n = g.n
        xi = g.xi
        X = XBt[xi]
        O32 = TV(25, 8 * SL); OSQ = TV(21, 4 * SL); R2 = TV(20, n)
        o32 = O32.f().rearrange("p (k t) -> p k t", k=KD)
        osq = OSQ.h().rearrange("p (k t) -> p k t", k=KD)
        for dp in range(4):
            pb, pbb = bank()
            for s in range(2):
                d = 2 * dp + s
                fns = []
                for kc in range(16):
                    si, kl = (0, kc) if kc < 6 else ((1, kc - 6) if kc < 12 else (2, kc - 12))
                    fns.append(lambda e, kc=kc, si=si, kl=kl, d=d, s=s: e.matmul(pb[:, s * n:(s + 1) * n], lhsT=WOt[si][:, kl, d * 128:(d + 1) * 128],
                                                                               rhs=CATt[:, kc, 0:n], start=(kc == 0), stop=(kc == 15)))
                tk.group(PE, fns, WOb + CATb, pbb)
            pv = pb[:, 0:2 * n].rearrange("p (s t) -> p s t", s=2)
            tk.op(ACT, lambda e, dp=dp, pv=pv: e.activation(out=o32[:, 2 * dp:2 * dp + 2, 0:n], in_=pv, func=AF.Copy), pbb, O32.b)
            tk.op(ACT, lambda e, dp=dp, pv=pv: e.activation(out=osq[:, 2 * dp:2 * dp + 2, 0:n], in_=pv, func=AF.Square), pbb, OSQ.b)
        pb, pbb = bank()
        tk.group(PE, [(lambda e, kc=kc: e.matmul(pb[:, 0:n], lhsT=ONESt[:], rhs=osq[:, kc, 0:n], start=(kc == 0), stop=(kc == KD - 1)))
                      for kc in range(KD)], OSQ.b + [ONESb], pbb)
        rsqrt_act(R2, pb[:, 0:n], pbb, 1.0 / D)
        tk.op(DVE, lambda e: e.tensor_tensor(out=o32[:, :, 0:n], in0=o32[:, :, 0:n], in1=R2.f().unsqueeze(1).broadcast_to([128, KD, n]), op=ALU.mult),
              O32.b + R2.b, O32.b)
        for d in range(KD):
            tk.op(DVE, lambda e, d=d: e.scalar_tensor_tensor(out=X[:, d, 0:n], in0=o32[:, d, 0:n], scalar=P10t[pi][:, d, 1:2], in1=X[:, d, 0:n],
                                                           op0=ALU.mult, op1=ALU.add), O32.b + [XBb[xi], P10b[pi]], [XBb[xi]])
        if l == L - 1:
            dst = (ypT[:, g.c0:g.c0 + n] if g.kind == "p" else ysT[:, 0:n])
            tk.dma(SP, dst.rearrange("(k p) t -> p k t", p=128), X[:, :, 0:n], [XBb[xi]], [], is_output=True)
        else:
            tk.dma(SP, xscr[l % 2][:, g.c0:g.c0 + n].rearrange("(k p) t -> p k t", p=128), X[:, :, 0:n], [XBb[xi]], [scrb[l % 2][g.idx]])

    def ck(level):
        if cfg.upto < level:
            raise StopEmit()

    def emit_all():
        load_params(0)
        load_weights(0)
        ck(1)
        diag_chunks(0, range(6))
        order = [groups[-1]] + groups[:-1]

        def reload(l, names):
            for nm_ in names:
                if nm_ == "WA":
                    tk.dma(POOL, WAt[:], wab[l], [], [WAb])
                elif nm_ == "WX":
                    tk.dma(POOL, WXt[:], wxb[l], [], [WXb])
                elif nm_ == "WO":
                    r0 = 0
                    for i, n_ in enumerate((6, 6, 4)):
                        tk.dma(POOL, WOt[i][:], w_out[l, r0:r0 + n_ * 128, :].rearrange("(k p) c -> p k c", p=128), [], [WOb[i]])
                        r0 += n_ * 128
                else:
                    (nm, c0, ncs) = [x for x in SEGS if x[0] == nm_][0]
                    tk.dma(POOL, Wt[nm][:], w_in[l, :, c0:c0 + ncs * 128].rearrange("(k p) c -> p k c", p=128), [], [Wb[nm]])

        for l in range(L):
            layer_consts(l)
            if l + 1 < L:
                load_params(l + 1)
            ck(2)
            mem_phase(l)
            if l > 0:
                reload(l, ["gc", "xr", "gr", "WA", "WX", "q", "gq"])
            ck(3)
            stage_A(l, order[0])
            for gi, g in enumerate(order):
                ck(4)
                ac = None; al = None
                last = (gi == NG - 1 and l + 1 < L)
                ngen = min(3, NG - 1)
                per = (6 + ngen - 1) // ngen
                dg = None
                if l + 1 < L and 1 <= gi <= ngen:
                    dg = (lambda gi=gi: diag_chunks(l + 1, range(per * (gi - 1), min(6, per * gi))))
                if last:
                    def ac(dg=dg):
                        if dg is not None:
                            dg()
                        reload(l + 1, ["b", "a"])
                    al = None
                else:
                    ac = dg
                stage_B(l, g, ck, after_conv=ac, after_lru=al)
                if gi == 0 and l > 0:
                    reload(l, ["WO"])
                if gi + 1 < NG:
                    stage_A(l, order[gi + 1])
                ck(8)
                stage_C(l, g)

    try:
        emit_all()
    except StopEmit:
        pass
    tk.finish()
    return nc


def host_inputs(inp, cfg, ncores):
    L, SEQ, NB = cfg.L, cfg.SEQ, cfg.NB
    f = lambda a: np.ascontiguousarray(np.asarray(a, dtype=np.float32))
    p768 = np.concatenate([
        np.asarray(inp["conv_w"])[:L], np.asarray(inp["conv_b"])[:L, None], np.asarray(inp["conv_ln_g"])[:L, None],
        np.asarray(inp["conv_ln_b"])[:L, None], np.asarray(inp["lru_conv_w"])[:L], np.asarray(inp["lru_conv_b"])[:L, None],
        np.asarray(inp["lru_ba"])[:L, None], np.asarray(inp["lru_bx"])[:L, None], np.asarray(inp["lru_lambda"])[:L, None]], axis=1)
    assert p768.shape[1] == NP7
    p768 = f(p768.transpose(0, 2, 1))
    p1024 = f(np.stack([np.asarray(inp["norm_pre_g"])[:L], np.asarray(inp["norm_post_g"])[:L], np.asarray(inp["mem_norm_g"])[:L]], axis=2))

    def blocks(w):
        w = np.asarray(w)[:L]
        bd = np.zeros((L, LW, LW), np.float32)
        for h in range(8):
            bd[:, 96 * h:96 * h + 96, 96 * h:96 * h + 96] = w[:, h]
        out = np.zeros((L, 128, NBLK, 128), np.float32)
        for bi, (mo, kc) in enumerate(GBLK):
            out[:, :, bi, :] = bd[:, kc * 128:(kc + 1) * 128, mo * 128:(mo + 1) * 128]
        return out

    shared = {
        "w_in": f(np.asarray(inp["w_in"])[:L]), "w_out": f(np.asarray(inp["w_out"])[:L]),
        "wk": f(np.asarray(inp["w_mem_k"])[:L]), "wv": f(np.asarray(inp["w_mem_v"])[:L]),
        "wab": blocks(inp["lru_wa"]), "wxb": blocks(inp["lru_wx"]), "p768": p768, "p1024": p1024,
        "ident": np.eye(128, dtype=np.float32),
    }
    xp = np.asarray(inp["x_prompt"]); xs = np.asarray(inp["x_sample"]); mem = np.asarray(inp["mem_prompt"])
    cc = np.asarray(inp["cache_conv"]); cl = np.asarray(inp["cache_lru_conv"]); h0 = np.asarray(inp["state_lru_h"])
    ck = np.asarray(inp["cache_mem_k"]); cvv = np.asarray(inp["cache_mem_v"])
    maps = []
    for i in range(ncores):
        sl = slice(i * NB, (i + 1) * NB)
        m = dict(shared)
        m["xpT"] = f(xp[i, :SEQ].T)
        m["xsT"] = f(xs[sl].reshape(NB * TS, D).T)
        m["memT"] = f(mem[i].T)
        m["cconvT"] = f(cc[:L, sl].transpose(0, 3, 1, 2))
        m["clruT"] = f(cl[:L, sl].transpose(0, 3, 1, 2))
        m["h0T"] = f(h0[:L, sl].transpose(0, 2, 1))
        m["ckT"] = f(ck[:L, sl].reshape(L, NB, NMEM, MW).transpose(0, 1, 3, 2))
        m["cv"] = f(cvv[:L, sl].reshape(L, NB, NMEM, MW))
        maps.append(m)
    return maps


def host_outputs(results, cfg, ncores):
    L, SEQ, NB = cfg.L, cfg.SEQ, cfg.NB
    yp = np.stack([r["ypT"].T for r in results])
    ys = np.concatenate([r["ysT"].T.reshape(NB, TS, D) for r in results])
    pconv = np.stack([r["pconvT"].transpose(0, 2, 1) for r in results], axis=1)
    plconv = np.stack([r["plconvT"].transpose(0, 2, 1) for r in results], axis=1)
    ph = np.stack([r["phT"][:, :, 0] for r in results], axis=1)
    pmk = np.stack([r["pmkT"].transpose(0, 2, 1).reshape(L, NMEM, NH, 128) for r in results], axis=1)
    pmv = np.stack([r["pmv"].reshape(L, NMEM, NH, 128) for r in results], axis=1)
    sconv = np.concatenate([r["sconvT"].transpose(0, 2, 3, 1) for r in results], axis=1)
    slconv = np.concatenate([r["slconvT"].transpose(0, 2, 3, 1) for r in results], axis=1)
    sh = np.concatenate([r["shT"].transpose(0, 2, 1) for r in results], axis=1)
    outs = (yp, ys, pconv, plconv, ph, pmk, pmv, sconv, slconv, sh)
    return tuple(np.ascontiguousarray(o.astype(np.float32)) for o in outs)


_CACHE = {}


def kernel(**inputs):
    cfg = Cfg()
    ncores = 8
    if "nc" not in _CACHE:
        _CACHE["nc"] = build(cfg)
    nc = _CACHE["nc"]
    maps = host_inputs(inputs, cfg, ncores)
    res = run_bass_kernel_spmd(nc, maps, core_ids=list(range(ncores)))
    return host_outputs(res.results, cfg, ncores)
```

```python
import math
import numpy as np
import concourse.bass as bass
import concourse.mybir as mybir
from concourse.bass_utils import run_bass_kernel_spmd

F32 = mybir.dt.float32
BF16 = mybir.dt.bfloat16
AF = mybir.ActivationFunctionType
ALU = mybir.AluOpType

D = 1024
KD = 8
CW = 768
LW = 768
MW = 512
NH = 4
NMEM = 256
CK = 31
HK = CK - 1
LK = 4
LH = LK - 1
INW = 4864
EPS = 1e-6
SEGS = [("a", 0, 6), ("b", 768, 6), ("gc", 1536, 6), ("xr", 2304, 6), ("gr", 3072, 6), ("q", 3840, 4), ("gq", 4352, 4)]
NP7 = 42
R_CW, R_CB, R_LNG, R_LNB, R_LCW, R_LCB, R_BA, R_BX, R_LAM = 0, 31, 32, 33, 34, 38, 39, 40, 41
TS = 4


def gate_blocks():
    out = []
    for mo in range(6):
        hlo = (128 * mo) // 96
        hhi = (128 * mo + 127) // 96
        klo = (96 * hlo) // 128
        khi = (96 * hhi + 95) // 128
        for kc in range(klo, khi + 1):
            out.append((mo, kc))
    return out


GBLK = gate_blocks()
NBLK = len(GBLK)


class StopEmit(Exception):
    pass


class Buf:
    __slots__ = ("w", "r", "name", "excl")

    def __init__(self, name="", excl=False):
        self.w = None
        self.r = []
        self.name = name
        self.excl = excl


class Eng:
    def __init__(self, nc, e, name):
        self.e = e
        self.name = name
        self.sem = nc.alloc_semaphore("es_" + name)
        self.cnt = 0
        self.seen = {}


class Tracker:
    def __init__(self, nc, ndma=56):
        self.nc = nc
        self.pe = Eng(nc, nc.tensor, "pe")
        self.act = Eng(nc, nc.scalar, "act")
        self.dve = Eng(nc, nc.vector, "dve")
        self.pool = Eng(nc, nc.gpsimd, "pool")
        self.sp = Eng(nc, nc.sync, "sp")
        self.dpools = {}
        for nm, cnt in (("sp", ndma // 2), ("pool", ndma // 2)):
            self.dpools[nm] = {"sems": [nc.alloc_semaphore(f"ds_{nm}{i}") for i in range(cnt)], "cnt": [0] * cnt, "next": 0}
        self.out_events = []
        import os
        self.nops = 0
        self.maxops = int(os.environ.get("STOPN", "100000000"))
        self.log = []

    def _waits(self, E, reads, writes):
        self.nops += 1
        if self.nops > self.maxops:
            raise StopEmit()
        need = {}

        def add(ev, raw):
            sem, val, eng = ev
            if eng is E and E is not self.pool:
                if not raw or E is self.pe:
                    return
            k = id(sem)
            if k not in need or need[k][1] < val:
                need[k] = (sem, val)

        for b in reads:
            if b.w is not None:
                add(b.w, True)
            if b.excl:
                for ev in b.r:
                    add(ev, False)
        for b in writes:
            if b.w is not None:
                add(b.w, False)
            for ev in b.r:
                add(ev, False)
        for k, (sem, val) in need.items():
            if E.seen.get(k, 0) < val:
                E.e.wait_ge(sem, val)
                E.seen[k] = val

    def _commit(self, ev, reads, writes):
        for b in writes:
            b.w = ev
            b.r = []
        for b in reads:
            b.r = [x for x in b.r if x[0] is not ev[0]]
            b.r.append(ev)

    def op(self, E, fn, reads=(), writes=()):
        self._waits(E, reads, writes)
        ins = fn(E.e)
        E.cnt += 1
        ins.then_inc(E.sem, 1)
        self._commit((E.sem, E.cnt, E), reads, writes)

    def group(self, E, fns, reads=(), writes=()):
        self._waits(E, reads, writes)
        ins = None
        for fn in fns:
            ins = fn(E.e)
        E.cnt += 1
        ins.then_inc(E.sem, 1)
        self._commit((E.sem, E.cnt, E), reads, writes)

    def dma(self, Q, out, in_, reads=(), writes=(), is_output=False, slow=False):
        self._waits(Q, reads, writes)
        dp = self.dpools[Q.name]
        i = dp["next"]
        dp["next"] = (i + 1) % len(dp["sems"])
        if dp["cnt"][i] > 0 and Q.seen.get(id(dp["sems"][i]), 0) < dp["cnt"][i]:
            Q.e.wait_ge(dp["sems"][i], dp["cnt"][i])
            Q.seen[id(dp["sems"][i])] = dp["cnt"][i]
        if slow:
            Q.e.dma_start(out=out, in_=in_, allow_slow_non_contiguous=True).then_inc(dp["sems"][i], 16)
        else:
            Q.e.dma_start(out=out, in_=in_).then_inc(dp["sems"][i], 16)
        dp["cnt"][i] += 16
        ev = (dp["sems"][i], dp["cnt"][i], None)
        self._commit(ev, reads, writes)
        if is_output:
            self.out_events.append(ev)

    def finish(self):
        E = self.sp
        for dp in self.dpools.values():
            for sem, val in zip(dp["sems"], dp["cnt"]):
                if val > 0:
                    E.e.wait_ge(sem, val)
        for G in (self.pe, self.act, self.dve, self.pool):
            if G.cnt > 0:
                E.e.wait_ge(G.sem, G.cnt)


class Cfg:
    def __init__(self, L=4, SEQ=2048, NB=16, T=256, NPE=0, upto=99):
        self.L, self.SEQ, self.NB, self.T, self.NPE = L, SEQ, NB, T, NPE
        self.upto = upto
        self.ndma = 56
        self.NS = NB * TS
        assert SEQ % T == 0 and T >= HK and self.NS <= T


def build(cfg):
    L, SEQ, NB, T = cfg.L, cfg.SEQ, cfg.NB, cfg.T
    NS = cfg.NS
    SL = T
    nc = bass.Bass("TRN2", target_bir_lowering=False)

    def din(name, shape):
        return nc.dram_tensor(name, list(shape), F32, kind="ExternalInput").ap()

    def dout(name, shape):
        return nc.dram_tensor(name, list(shape), F32, kind="ExternalOutput").ap()

    xpT = din("xpT", [D, SEQ]); xsT = din("xsT", [D, NS]); memT = din("memT", [D, NMEM])
    cconvT = din("cconvT", [L, CW, NB, HK]); clruT = din("clruT", [L, LW, NB, LH]); h0T = din("h0T", [L, LW, NB])
    ckT = din("ckT", [L, NB, MW, NMEM]); cv = din("cv", [L, NB, NMEM, MW])
    w_in = din("w_in", [L, D, INW]); w_out = din("w_out", [L, 2 * D, D])
    wk = din("wk", [L, D, MW]); wv = din("wv", [L, D, MW])
    wab = din("wab", [L, 128, NBLK, 128]); wxb = din("wxb", [L, 128, NBLK, 128])
    p768 = din("p768", [L, CW, NP7]); p1024 = din("p1024", [L, D, 3])
    ident_d = din("ident", [128, 128])
    DGd = nc.dram_tensor("dgscr", [L, 6, 128, CK * 128], BF16, kind="Internal").ap()
    DGb = [[Buf(f"dg{l}_{j}") for j in range(6)] for l in range(L)]
    DGLd = nc.dram_tensor("dglscr", [L, 128, 6 * LK * 128], BF16, kind="Internal").ap()
    DGLb = [Buf(f"dgl{l}") for l in range(L)]

    ypT = dout("ypT", [D, SEQ]); ysT = dout("ysT", [D, NS])
    pconvT = dout("pconvT", [L, CW, HK]); plconvT = dout("plconvT", [L, LW, LH]); phT = dout("phT", [L, LW, 1])
    pmkT = dout("pmkT", [L, MW, NMEM]); pmv = dout("pmv", [L, NMEM, MW])
    sconvT = dout("sconvT", [L, CW, NB, HK]); slconvT = dout("slconvT", [L, LW, NB, LH]); shT = dout("shT", [L, LW, NB])
    xscr = [nc.dram_tensor(f"xscr{i}", [D, SEQ + NS], F32, kind="Internal").ap() for i in range(2)]

    tk = Tracker(nc, ndma=cfg.ndma)
    PE, ACT, DVE, POOL, SP = tk.pe, tk.act, tk.dve, tk.pool, tk.sp

    def sb(name, shape, dt):
        return nc.alloc_sbuf_tensor(name, list(shape), dt)

    Wt = {}; Wb = {}
    for (nm, c0, ncs) in SEGS:
        Wt[nm] = sb("W_" + nm, [128, KD, ncs * 128], BF16); Wb[nm] = Buf("W_" + nm)
    WOt = [sb(f"WO{i}", [128, n_, D], BF16) for i, n_ in enumerate((6, 6, 4))]
    WOb = [Buf(f"WO{i}") for i in range(3)]
    WAt = sb("WA", [128, NBLK, 128], BF16); WAb = Buf("WA")
    WXt = sb("WX", [128, NBLK, 128], BF16); WXb = Buf("WX")
    P7t = [sb(f"P7_{i}", [128, 6, NP7], F32) for i in range(2)]; P7b = [Buf(), Buf()]
    P10t = [sb(f"P10_{i}", [128, KD, 3], F32) for i in range(2)]; P10b = [Buf(), Buf()]
    CVt = sb("CV", [128, 6], F32); CVb = Buf("CV")
    CVtmp = sb("CVtmp", [128, 6], F32); CVtmpb = Buf("CVtmp")
    NEGt = sb("NEGP", [128, 6, 4], F32); NEGb = Buf("NEGP")
    ONESt = sb("ONES", [128, 128], BF16); ONESb = Buf("ONES")
    EPSt = sb("EPSc", [128, 1], F32); ONEt = sb("ONEc", [128, 1], F32); CONSTb = Buf("const")
    KTPt = sb("KTP", [128, NH, NMEM], BF16); KTPb = Buf("KTP")
    VPt = sb("VP", [128, 2, MW], BF16); VPb = Buf("VP")
    KTSt = [sb(f"KTS{i}", [128, NH, NMEM], BF16) for i in range(2)]; KTSb = [Buf(), Buf()]
    VSt = [sb(f"VS{i}", [128, 2, MW], BF16) for i in range(2)]; VSb = [Buf(), Buf()]
    HISTCt = sb("HISTC", [128, 6, HK], F32); HISTCb = [Buf() for _ in range(3)]
    HISTLt = sb("HISTL", [128, 6, LH], F32); HISTLb = [Buf() for _ in range(3)]
    HSTt = sb("HST", [128, 6], F32); HSTb = [Buf() for _ in range(6)]
    XBt = [sb(f"XB{i}", [128, KD, T], F32) for i in range(2)]; XBb = [Buf(), Buf()]
    XNt = sb("XN", [128, KD, T], BF16); XNb = Buf("XN")
    CATt = sb("CAT", [128, 16, T], BF16); CATb = [Buf(f"cat{i}") for i in range(16)]
    IDENTt = sb("IDENT", [128, 128], BF16); IDENTb = Buf("IDENT")
    DGRt = [sb(f"DGR{i}", [128, CK, 128], BF16) for i in range(2)]; DGRb = [Buf("dgr0"), Buf("dgr1")]
    dgr_state = {"i": 0}
    NSLOT = 33
    TMt = sb("TM", [128, NSLOT * SL], F32)
    TMb = [Buf(f"tm{i}") for i in range(NSLOT)]
    PBt = [nc.alloc_psum_tensor(f"PB{i}", [128, 512], F32) for i in range(8)]
    PBb = [Buf(f"pb{i}", excl=True) for i in range(8)]
    pstate = {"i": 0}

    def bank():
        i = pstate["i"]
        pstate["i"] = (i + 1) % 7
        return PBt[i], [PBb[i]]

    class TV:
        def __init__(self, s0, nel_f32, eoff=0):
            self.e0 = s0 * SL + eoff
            self.nel = nel_f32
            sa = self.e0 // SL
            s1 = (self.e0 + nel_f32 + SL - 1) // SL
            assert s1 <= NSLOT, (s0, nel_f32)
            self.b = TMb[sa:s1]

        def f(self):
            return TMt[:, self.e0:self.e0 + self.nel]

        def h(self):
            return TMt[:, self.e0:self.e0 + self.nel].bitcast(BF16)

    tk.op(POOL, lambda e: e.memset(ONESt[:], 1.0), [], [ONESb])
    tk.op(POOL, lambda e: e.memset(EPSt[:], EPS), [], [CONSTb])
    tk.op(POOL, lambda e: e.memset(ONEt[:], 1.0), [], [CONSTb])
    tk.dma(POOL, IDENTt[:], ident_d, [], [IDENTb])

    def load_weights(l):
        for (nm, c0, ncs) in SEGS:
            tk.dma(POOL, Wt[nm][:], w_in[l, :, c0:c0 + ncs * 128].rearrange("(k p) c -> p k c", p=128), [], [Wb[nm]])
        r0 = 0
        for i, n_ in enumerate((6, 6, 4)):
            tk.dma(POOL, WOt[i][:], w_out[l, r0:r0 + n_ * 128, :].rearrange("(k p) c -> p k c", p=128), [], [WOb[i]])
            r0 += n_ * 128
        tk.dma(POOL, WAt[:], wab[l], [], [WAb])
        tk.dma(POOL, WXt[:], wxb[l], [], [WXb])

    def load_params(l):
        pi = l % 2
        tk.dma(SP, P7t[pi][:], p768[l].rearrange("(j p) r -> p j r", p=128), [], [P7b[pi]])
        tk.dma(SP, P10t[pi][:], p1024[l].rearrange("(j p) r -> p j r", p=128), [], [P10b[pi]])

    def layer_consts(l):
        pi = l % 2
        P7 = P7t[pi]
        tk.op(ACT, lambda e: e.activation(out=CVtmp[:], in_=P7[:, :, R_LAM], func=AF.Exp, scale=-1.0), [P7b[pi]], [CVtmpb])
        tk.op(ACT, lambda e: e.activation(out=CVtmp[:], in_=CVtmp[:], func=AF.Ln, bias=ONEt[:], scale=1.0), [CVtmpb, CONSTb], [CVtmpb])
        tk.op(DVE, lambda e: e.tensor_scalar(out=CVt[:], in0=CVtmp[:], scalar1=-8.0, scalar2=None, op0=ALU.mult), [CVtmpb], [CVb])
        tk.op(DVE, lambda e: e.tensor_scalar(out=NEGt[:, :, 0:2], in0=P7[:, :, R_BA:R_BA + 2], scalar1=-1.0, scalar2=None, op0=ALU.mult), [P7b[pi]], [NEGb])
        tk.op(DVE, lambda e: e.tensor_scalar(out=NEGt[:, :, 2:4], in0=P7[:, :, R_LNG:R_LNG + 2], scalar1=-1.0, scalar2=None, op0=ALU.mult), [P7b[pi]], [NEGb])

    def diag_chunks(l, js):
        pi = l % 2
        for j in js:
            r = dgr_state["i"]; dgr_state["i"] = 1 - r
            tk.op(POOL, lambda e, j=j, r=r: e.tensor_tensor(out=DGRt[r][:], in0=IDENTt[:].unsqueeze(1).broadcast_to([128, CK, 128]),
                                                          in1=P7t[pi][:, j, R_CW:R_CW + CK].unsqueeze(2).broadcast_to([128, CK, 128]), op=ALU.mult),
                  [IDENTb, P7b[pi]], [DGRb[r]])
            tk.dma(SP, DGd[l, j], DGRt[r][:].rearrange("p k c -> p (k c)"), [DGRb[r]], [DGb[l][j]])
        if 5 in js:
            r = dgr_state["i"]; dgr_state["i"] = 1 - r
            for j in range(6):
                tk.op(POOL, lambda e, j=j, r=r: e.tensor_tensor(out=DGRt[r][:, j * LK:(j + 1) * LK, :], in0=IDENTt[:].unsqueeze(1).broadcast_to([128, LK, 128]),
                                                              in1=P7t[pi][:, j, R_LCW:R_LCW + LK].unsqueeze(2).broadcast_to([128, LK, 128]), op=ALU.mult),
                      [IDENTb, P7b[pi]], [DGRb[r]])
            tk.dma(SP, DGLd[l], DGRt[r][:, 0:6 * LK, :].rearrange("p k c -> p (k c)"), [DGRb[r]], [DGLb[l]])

    def mem_phase(l):
        pi = l % 2
        P10 = P10t[pi]
        MEM = TV(0, 8 * SL); MSQ = TV(8, 4 * SL); MN = TV(8, 4 * SL); RM = TV(12, NMEM)
        WKv = TV(13, 8 * SL); WVv = TV(21, 8 * SL); OF = [TV(29, 2 * SL), TV(31, 2 * SL)]
        assert NMEM == SL
        memf = MEM.f().rearrange("p (k m) -> p k m", k=KD)
        msq = MSQ.h().rearrange("p (k m) -> p k m", k=KD)
        mn = MN.h().rearrange("p (k m) -> p k m", k=KD)
        wkv = WKv.h().rearrange("p (k c) -> p k c", k=KD)
        wvv = WVv.h().rearrange("p (k c) -> p k c", k=KD)
        tk.dma(SP, memf, memT.rearrange("(k p) m -> p k m", p=128), [], MEM.b)
        tk.dma(POOL, wkv, wk[l].rearrange("(k p) c -> p k c", p=128), [], WKv.b)
        tk.dma(POOL, wvv, wv[l].rearrange("(k p) c -> p k c", p=128), [], WVv.b)
        tk.op(ACT, lambda e: e.activation(out=msq, in_=memf, func=AF.Square), MEM.b, MSQ.b)
        pb, pbb = bank()
        tk.group(PE, [(lambda e, kc=kc: e.matmul(pb[:, 0:NMEM], lhsT=ONESt[:], rhs=msq[:, kc, :], start=(kc == 0), stop=(kc == KD - 1)))
                      for kc in range(KD)], MSQ.b + [ONESb], pbb)
        tk.op(ACT, lambda e: e.activation(out=RM.f(), in_=pb[:, 0:NMEM], func=AF.Ln, bias=EPSt[:], scale=1.0 / D), pbb + [CONSTb], RM.b)
        tk.op(ACT, lambda e: e.activation(out=RM.f(), in_=RM.f(), func=AF.Exp, scale=-0.5), RM.b, RM.b)
        for kc in range(KD):
            tk.op(DVE, lambda e, kc=kc: e.scalar_tensor_tensor(out=mn[:, kc, :], in0=memf[:, kc, :], scalar=P10[:, kc, 2:3], in1=RM.f(),
                                                             op0=ALU.mult, op1=ALU.mult), MEM.b + RM.b + [P10b[pi]], MN.b)
        for hp in range(2):
            pb, pbb = bank()
            for s in range(2):
                h = 2 * hp + s
                tk.group(PE, [(lambda e, kc=kc, h=h, s=s: e.matmul(pb[:, s * NMEM:(s + 1) * NMEM], lhsT=wkv[:, kc, h * 128:(h + 1) * 128],
                                                                rhs=mn[:, kc, :], start=(kc == 0), stop=(kc == KD - 1))) for kc in range(KD)],
                         WKv.b + MN.b, pbb)
            tk.op(ACT, lambda e, hp=hp: e.activation(out=KTPt[:, 2 * hp:2 * hp + 2, :], in_=pb[:, :].rearrange("p (s m) -> p s m", s=2), func=AF.Copy),
                  pbb, [KTPb])
            of = OF[hp % 2]
            import os
            dbg = os.environ.get("DBG", "")
            if "A" in dbg:
                tk.op(ACT, lambda e: e.activation(out=of.f(), in_=pb[:, :], func=AF.Copy), pbb, of.b)
            elif "S" in dbg:
                tk.op(DVE, lambda e: e.tensor_copy(out=of.f()[:, 0:256], in_=RM.f()), pbb + RM.b, of.b)
            elif "N" in dbg:
                tk.op(DVE, lambda e: e.tensor_copy(out=of.f(), in_=pb[:, :]), [], of.b)
            elif "T" in dbg:
                tk.op(DVE, lambda e: e.tensor_scalar(out=of.f(), in0=pb[:, :], scalar1=1.0, scalar2=None, op0=ALU.mult), pbb, of.b)
            elif "H" in dbg:
                tk.op(DVE, lambda e: e.tensor_copy(out=of.f()[:, 0:256], in_=pb[:, 0:256]), pbb, of.b)
                tk.op(DVE, lambda e: e.tensor_copy(out=of.f()[:, 256:512], in_=pb[:, 256:512]), pbb, of.b)
            else:
                tk.op(DVE, lambda e: e.tensor_copy(out=of.f(), in_=pb[:, :]), pbb, of.b)
            tk.dma(SP, pmkT[l, hp * 256:(hp + 1) * 256, :].rearrange("(s p) m -> p s m", p=128), of.f().rearrange("p (s m) -> p s m", s=2),
                   of.b, [], is_output=True)
        for mc in range(2):
            pb, pbb = bank()
            tk.group(PE, [(lambda e, kc=kc, mc=mc: e.matmul(pb[:, :], lhsT=mn[:, kc, mc * 128:(mc + 1) * 128], rhs=wvv[:, kc, :],
                                                          start=(kc == 0), stop=(kc == KD - 1))) for kc in range(KD)], WVv.b + MN.b, pbb)
            tk.op(ACT, lambda e, mc=mc: e.activation(out=VPt[:, mc, :], in_=pb[:, :], func=AF.Copy), pbb, [VPb])
            of = OF[mc % 2]
            tk.op(DVE, lambda e: e.tensor_copy(out=of.f(), in_=pb[:, :]), pbb, of.b)
            tk.dma(SP, pmv[l, mc * 128:(mc + 1) * 128, :], of.f(), of.b, [], is_output=True)

    class Grp:
        pass

    def make_groups():
        gs = []
        for ti in range(SEQ // T):
            g = Grp(); g.kind = "p"; g.n = T; g.nseq = 1; g.tlen = T; g.c0 = ti * T
            g.first = (ti == 0); g.last = (ti == SEQ // T - 1); g.idx = ti
            gs.append(g)
        g = Grp(); g.kind = "s"; g.n = NS; g.nseq = NB; g.tlen = TS; g.c0 = SEQ; g.first = True; g.last = True; g.idx = SEQ // T
        gs.append(g)
        return gs

    groups = make_groups()
    NG = len(groups)
    scrb = [[Buf() for _ in range(NG)] for _ in range(2)]
    xslot = {"i": 0}

    def proj_pair(seg, j0, n, npair=2):
        pb, pbb = bank()
        for s in range(npair):
            j = j0 + s
            tk.group(PE, [(lambda e, kc=kc, j=j, s=s: e.matmul(pb[:, s * n:(s + 1) * n], lhsT=Wt[seg][:, kc, j * 128:(j + 1) * 128],
                                                             rhs=XNt[:, kc, 0:n], start=(kc == 0), stop=(kc == KD - 1))) for kc in range(KD)],
                     [Wb[seg], XNb], pbb)
        return pb, pbb

    def rsqrt_act(out_tv, in_ap, rd, scale):
        tk.op(ACT, lambda e: e.activation(out=out_tv.f(), in_=in_ap, func=AF.Ln, bias=EPSt[:], scale=scale), rd + [CONSTb], out_tv.b)
        tk.op(ACT, lambda e: e.activation(out=out_tv.f(), in_=out_tv.f(), func=AF.Exp, scale=-0.5), out_tv.b, out_tv.b)

    def sigmoid_act(out_ap, out_b, in_ap, rd, nscale=-1.0, nbias=None):
        if nbias is None:
            tk.op(ACT, lambda e: e.activation(out=out_ap, in_=in_ap, func=AF.Exp, scale=nscale), rd, out_b)
        else:
            tk.op(ACT, lambda e: e.activation(out=out_ap, in_=in_ap, func=AF.Exp, bias=nbias, scale=nscale), rd, out_b)
        tk.op(ACT, lambda e: e.activation(out=out_ap, in_=out_ap, func=AF.Ln, bias=ONEt[:], scale=1.0), out_b + [CONSTb], out_b)
        tk.op(ACT, lambda e: e.activation(out=out_ap, in_=out_ap, func=AF.Exp, scale=-1.0), out_b, out_b)

    def stage_A(l, g):
        pi = l % 2
        n = g.n
        xi = xslot["i"]; xslot["i"] = 1 - xi
        g.xi = xi
        X = XBt[xi]
        if l == 0:
            src = (xpT[:, g.c0:g.c0 + n] if g.kind == "p" else xsT[:, 0:n])
            rd = []
        else:
            src = xscr[(l - 1) % 2][:, g.c0:g.c0 + n]
            rd = [scrb[(l - 1) % 2][g.idx]]
        tk.dma(SP, X[:, :, 0:n], src.rearrange("(k p) t -> p k t", p=128), rd, [XBb[xi]])
        SQ = TV(0, 4 * SL); RT = TV(4, n)
        sq = SQ.h().rearrange("p (k t) -> p k t", k=KD)
        tk.op(ACT, lambda e: e.activation(out=sq[:, :, 0:n], in_=X[:, :, 0:n], func=AF.Square), [XBb[xi]], SQ.b)
        pb, pbb = bank()
        tk.group(PE, [(lambda e, kc=kc: e.matmul(pb[:, 0:n], lhsT=ONESt[:], rhs=sq[:, kc, 0:n], start=(kc == 0), stop=(kc == KD - 1)))
                      for kc in range(KD)], SQ.b + [ONESb], pbb)
        rsqrt_act(RT, pb[:, 0:n], pbb, 1.0 / D)
        for kc in range(KD):
            tk.op(DVE, lambda e, kc=kc: e.scalar_tensor_tensor(out=XNt[:, kc, 0:n], in0=X[:, kc, 0:n], scalar=P10t[pi][:, kc, 0:1], in1=RT.f(),
                                                             op0=ALU.mult, op1=ALU.mult), [XBb[xi], P10b[pi]] + RT.b, [XNb])

    def stage_B(l, g, ck=lambda lv: None, after_conv=None, after_lru=None):
        pi = l % 2
        if g.kind == "s":
            for b_ in range(min(2, NB)):
                tk.dma(POOL, KTSt[b_][:], ckT[l, b_].rearrange("(h d) m -> d h m", d=128), [], [KTSb[b_]])
                tk.dma(POOL, VSt[b_][:], cv[l, b_].rearrange("(c p) d -> p c d", p=128), [], [VSb[b_]])
        P7 = P7t[pi]; P7B = P7b[pi]
        n, nseq, tlen = g.n, g.nseq, g.tlen
        isp = (g.kind == "p")

        def v3(ap2):
            return ap2.rearrange("p (s t) -> p s t", s=nseq)

        ulen = HK + tlen
        SIGs = [TV(5 + 2 * i, 2 * n) for i in range(3)]
        UPBs = [TV(11 + 3 * i, nseq * ulen) for i in range(3)]
        UP = TV(20, 2 * nseq * ulen)
        CS = TV(25, 6 * n)
        CSB = TV(31, n); CSQ = TV(32, n)
        MEAN = TV(5, n); MSQv = TV(6, n); VAR = TV(7, n)
        cs = CS.f().rearrange("p (j t) -> p j t", j=6)
        up = UP.f().rearrange("p (j s u) -> p j s u", j=2, s=nseq)
        csb = CSB.h().rearrange("p (j t) -> p j t", j=2)
        csq = CSQ.h().rearrange("p (j t) -> p j t", j=2)
        upbs = [u_.h().rearrange("p (j s u) -> p j s u", j=2, s=nseq) for u_ in UPBs]
        for jp in range(3):
            j0 = 2 * jp
            SIG = SIGs[jp]; UPB = UPBs[jp]; upb = upbs[jp]
            pbB, pbBb = proj_pair("b", j0, n)
            sigmoid_act(SIG.f(), SIG.b, pbB[:, 0:2 * n], pbBb)
            pbA, pbAb = proj_pair("a", j0, n)
            if isp:
                if g.first:
                    tk.op(POOL, lambda e: e.memset(up[:, :, :, 0:HK], 0.0), [], UP.b)
                else:
                    tk.op(POOL, lambda e: e.tensor_copy(out=up[:, :, 0, 0:HK], in_=HISTCt[:, j0:j0 + 2, :]), [HISTCb[jp]], UP.b)
            else:
                STG = TV(27, 2 * NB * HK)
                stg = STG.f().rearrange("p (j s r) -> p j s r", j=2, s=nseq)
                tk.dma(SP, STG.f().rearrange("p (j q) -> p j q", j=2),
                       cconvT[l, j0 * 128:(j0 + 2) * 128].rearrange("(j p) b r -> p j (b r)", p=128), [], STG.b)
                tk.op(POOL, lambda e, stg=stg: e.tensor_copy(out=up[:, :, :, 0:HK], in_=stg), STG.b, UP.b)
            tk.op(POOL, lambda e, upb=upb: e.tensor_copy(out=upb[:, :, :, 0:HK], in_=up[:, :, :, 0:HK]), UP.b, UPB.b)
            a4 = pbA[:, 0:2 * n].rearrange("p (j s t) -> p j s t", j=2, s=nseq)
            s4 = SIG.f().rearrange("p (j s t) -> p j s t", j=2, s=nseq)
            tk.op(DVE, lambda e, a4=a4, s4=s4: e.tensor_tensor(out=up[:, :, :, HK:HK + tlen], in0=a4, in1=s4, op=ALU.mult), pbAb + SIG.b, UP.b)
            tk.op(DVE, lambda e, a4=a4, s4=s4, upb=upb: e.tensor_tensor(out=upb[:, :, :, HK:HK + tlen], in0=a4, in1=s4, op=ALU.mult), pbAb + SIG.b, UPB.b)
            if isp:
                if g.last:
                    tk.dma(SP, pconvT[l, j0 * 128:(j0 + 2) * 128, :].rearrange("(j p) r -> p j r", p=128), up[:, :, 0, tlen:tlen + HK], UP.b, [], is_output=True)
                else:
                    tk.op(POOL, lambda e: e.tensor_copy(out=HISTCt[:, j0:j0 + 2, :], in_=up[:, :, 0, tlen:tlen + HK]), UP.b, [HISTCb[jp]])
            else:
                tk.op(POOL, lambda e, stg=stg: e.tensor_copy(out=stg, in_=up[:, :, :, tlen:tlen + HK]), UP.b, STG.b)
                tk.dma(SP, sconvT[l, j0 * 128:(j0 + 2) * 128].rearrange("(j p) b r -> p j (b r)", p=128),
                       STG.f().rearrange("p (j q) -> p j q", j=2), STG.b, [], is_output=True)
        if isp:
            SGcs = [TV(0, 2 * n), TV(2, 2 * n), TV(23, 2 * n)]
        else:
            SGcs = [TV(0, 2 * n), TV(2, 2 * n), TV(4, 2 * n)]
        for jp in range(3):
            pbG, pbGb = proj_pair("gc", 2 * jp, n)
            sigmoid_act(SGcs[jp].f(), SGcs[jp].b, pbG[:, 0:2 * n], pbGb)
            tk.op(DVE, lambda e, jp=jp, pbG=pbG: e.tensor_tensor(out=SGcs[jp].f(), in0=SGcs[jp].f(), in1=pbG[:, 0:2 * n], op=ALU.mult),
                  SGcs[jp].b + pbGb, SGcs[jp].b)
        pst, pstb = PBt[7], [PBb[7]]
        for jp in range(3):
            j0 = 2 * jp
            upb = upbs[jp]; UPB = UPBs[jp]
            pbc, pbcb = bank()
            for s in range(2):
                j = j0 + s
                r = dgr_state["i"]; dgr_state["i"] = 1 - r
                tk.dma(SP, DGRt[r][:].rearrange("p k c -> p (k c)"), DGd[l, j], [DGb[l][j]], [DGRb[r]])
                tk.group(PE, [(lambda e, k=k, s=s, r=r, upb=upb: e.matmul(pbc[:, s * n:(s + 1) * n], lhsT=DGRt[r][:, k, :], rhs=upb[:, s, :, k:k + tlen],
                                                                        start=(k == 0), stop=(k == CK - 1))) for k in range(CK)],
                         [DGRb[r]] + UPB.b, pbcb)
            for s in range(2):
                j = j0 + s
                tk.op(ACT, lambda e, s=s, j=j: e.activation(out=cs[:, j, :], in_=pbc[:, s * n:(s + 1) * n], func=AF.Identity,
                                                           bias=P7[:, j, R_CB:R_CB + 1], scale=1.0), pbcb + [P7B], CS.b)
            tk.op(ACT, lambda e, j0=j0: e.activation(out=csb, in_=cs[:, j0:j0 + 2, :], func=AF.Copy), CS.b, CSB.b)
            tk.op(ACT, lambda e, j0=j0: e.activation(out=csq, in_=cs[:, j0:j0 + 2, :], func=AF.Square), CS.b, CSQ.b)
            tk.group(PE, [(lambda e, s=s: e.matmul(pst[:, 0:n], lhsT=ONESt[:], rhs=csb[:, s, :], start=(jp == 0 and s == 0), stop=(jp == 2 and s == 1),
                                                   skip_group_check=True)) for s in range(2)], CSB.b + [ONESb], pstb)
            tk.group(PE, [(lambda e, s=s: e.matmul(pst[:, n:2 * n], lhsT=ONESt[:], rhs=csq[:, s, :], start=False, stop=(jp == 2 and s == 1),
                                                   skip_group_check=True)) for s in range(2)], CSQ.b + [ONESb], pstb)
        if after_conv is not None:
            after_conv()
        tk.op(DVE, lambda e: e.tensor_scalar(out=MEAN.f(), in0=pst[:, 0:n], scalar1=1.0 / CW, scalar2=None, op0=ALU.mult), pstb, MEAN.b)
        tk.op(DVE, lambda e: e.tensor_tensor(out=MSQv.f(), in0=MEAN.f(), in1=MEAN.f(), op=ALU.mult), MEAN.b, MSQv.b)
        tk.op(DVE, lambda e: e.scalar_tensor_tensor(out=VAR.f(), in0=pst[:, n:2 * n], scalar=1.0 / CW, in1=MSQv.f(), op0=ALU.mult, op1=ALU.subtract),
              pstb + MSQv.b, VAR.b)
        rsqrt_act(VAR, VAR.f(), VAR.b, 1.0)
        TT = TV(8, 6 * n); ZZ = TV(14, 6 * n)
        tt = TT.f().rearrange("p (j t) -> p j t", j=6)
        zz = ZZ.f().rearrange("p (j t) -> p j t", j=6)
        mean_b = MEAN.f().unsqueeze(1).broadcast_to([128, 6, n])
        rs_b = VAR.f().unsqueeze(1).broadcast_to([128, 6, n])
        tk.op(DVE, lambda e: e.tensor_tensor(out=tt, in0=cs, in1=mean_b, op=ALU.subtract), CS.b + MEAN.b, TT.b)
        tk.op(DVE, lambda e: e.tensor_tensor(out=tt, in0=tt, in1=rs_b, op=ALU.mult), TT.b + VAR.b, TT.b)
        for j in range(6):
            tk.op(DVE, lambda e, j=j: e.tensor_scalar(out=zz[:, j, :], in0=tt[:, j, :], scalar1=P7[:, j, R_LNG:R_LNG + 1], scalar2=P7[:, j, R_LNB:R_LNB + 1],
                                                     op0=ALU.mult, op1=ALU.add), TT.b + [P7B], ZZ.b)
        EE = TV(25, 6 * n)
        ee = EE.f().rearrange("p (j t) -> p j t", j=6)
        for j in range(6):
            tk.op(ACT, lambda e, j=j: e.activation(out=ee[:, j, :], in_=tt[:, j, :], func=AF.Exp, bias=NEGt[:, j, 3:4], scale=NEGt[:, j, 2:3]),
                  TT.b + [NEGb], EE.b)
        tk.op(ACT, lambda e: e.activation(out=EE.f(), in_=EE.f(), func=AF.Ln, bias=ONEt[:], scale=1.0), EE.b + [CONSTb], EE.b)
        tk.op(ACT, lambda e: e.activation(out=EE.f(), in_=EE.f(), func=AF.Exp, scale=-1.0), EE.b, EE.b)
        tk.op(DVE, lambda e: e.tensor_tensor(out=ZZ.f(), in0=ZZ.f(), in1=EE.f(), op=ALU.mult), ZZ.b + EE.b, ZZ.b)
        for jp in range(3):
            j0 = 2 * jp
            tk.op(DVE, lambda e, j0=j0, jp=jp: e.tensor_tensor(out=CATt[:, j0:j0 + 2, 0:n], in0=zz[:, j0:j0 + 2, :],
                                                              in1=SGcs[jp].f().rearrange("p (j t) -> p j t", j=2), op=ALU.mult),
                  ZZ.b + SGcs[jp].b, CATb[j0:j0 + 2])

        ck(5)
        B0 = 5
        xlen = LH + tlen
        XRP = TV(B0 + 0, 2 * nseq * xlen)
        XC = TV(B0 + 3, 6 * n); XCB = TV(B0 + 9, 3 * n)
        T0 = TV(4, nseq)
        H0v = TV(3, 6 * NB); SHOv = TV(7, 6 * NB)
        H0t = H0v.f().rearrange("p (j b) -> p j b", j=6); SHOt = SHOv.f().rearrange("p (j b) -> p j b", j=6)
        H0b = None; SHOb = None
        xc = XC.f().rearrange("p (j t) -> p j t", j=6)
        xcb = XCB.h().rearrange("p (j t) -> p j t", j=6)
        xrp = XRP.f().rearrange("p (j s u) -> p j s u", j=2, s=nseq)
        if not isp:
            tk.dma(SP, H0t, h0T[l].rearrange("(j p) b -> p j b", p=128), [], H0v.b)
        XRPB = TV(29, nseq * xlen)
        xrpb = XRPB.h().rearrange("p (j s u) -> p j s u", j=2, s=nseq)
        rl = dgr_state["i"]; dgr_state["i"] = 1 - rl
        tk.dma(SP, DGRt[rl][:, 0:6 * LK, :].rearrange("p k c -> p (k c)"), DGLd[l], [DGLb[l]], [DGRb[rl]])
        for jp in range(3):
            j0 = 2 * jp
            pbX, pbXb = proj_pair("xr", j0, n)
            if isp:
                if g.first:
                    tk.op(POOL, lambda e: e.memset(xrp[:, :, :, 0:LH], 0.0), [], XRP.b)
                else:
                    tk.op(POOL, lambda e: e.tensor_copy(out=xrp[:, :, 0, 0:LH], in_=HISTLt[:, j0:j0 + 2, :]), [HISTLb[jp]], XRP.b)
            else:
                STL = TV(7, 2 * NB * LH)
                stl = STL.f().rearrange("p (j s r) -> p j s r", j=2, s=nseq)
                tk.dma(SP, STL.f().rearrange("p (j q) -> p j q", j=2),
                       clruT[l, j0 * 128:(j0 + 2) * 128].rearrange("(j p) b r -> p j (b r)", p=128), [], STL.b)
                tk.op(POOL, lambda e, stl=stl: e.tensor_copy(out=xrp[:, :, :, 0:LH], in_=stl), STL.b, XRP.b)
            tk.op(POOL, lambda e: e.tensor_copy(out=xrpb[:, :, :, 0:LH], in_=xrp[:, :, :, 0:LH]), XRP.b, XRPB.b)
            x4 = pbX[:, 0:2 * n].rearrange("p (j s t) -> p j s t", j=2, s=nseq)
            tk.op(ACT, lambda e, x4=x4: e.activation(out=xrp[:, :, :, LH:LH + tlen], in_=x4, func=AF.Copy), pbXb, XRP.b)
            tk.op(ACT, lambda e, x4=x4: e.activation(out=xrpb[:, :, :, LH:LH + tlen], in_=x4, func=AF.Copy), pbXb, XRPB.b)
            if isp:
                if g.last:
                    tk.dma(SP, plconvT[l, j0 * 128:(j0 + 2) * 128, :].rearrange("(j p) r -> p j r", p=128), xrp[:, :, 0, tlen:tlen + LH], XRP.b, [], is_output=True)
                else:
                    tk.op(POOL, lambda e: e.tensor_copy(out=HISTLt[:, j0:j0 + 2, :], in_=xrp[:, :, 0, tlen:tlen + LH]), XRP.b, [HISTLb[jp]])
            else:
                tk.op(POOL, lambda e, stl=stl: e.tensor_copy(out=stl, in_=xrp[:, :, :, tlen:tlen + LH]), XRP.b, STL.b)
                tk.dma(SP, slconvT[l, j0 * 128:(j0 + 2) * 128].rearrange("(j p) b r -> p j (b r)", p=128),
                       STL.f().rearrange("p (j q) -> p j q", j=2), STL.b, [], is_output=True)
            pbx, pbxb = bank()
            for s in range(2):
                j = j0 + s
                tk.group(PE, [(lambda e, k=k, s=s, j=j: e.matmul(pbx[:, s * n:(s + 1) * n], lhsT=DGRt[rl][:, j * LK + k, :], rhs=xrpb[:, s, :, k:k + tlen],
                                                               start=(k == 0), stop=(k == LK - 1))) for k in range(LK)],
                         [DGRb[rl]] + XRPB.b, pbxb)
            for s in range(2):
                j = j0 + s
                tk.op(ACT, lambda e, s=s, j=j, pbx=pbx: e.activation(out=xc[:, j, :], in_=pbx[:, s * n:(s + 1) * n], func=AF.Identity,
                                                                    bias=P7[:, j, R_LCB:R_LCB + 1], scale=1.0), pbxb + [P7B], XC.b)
        tk.op(ACT, lambda e: e.activation(out=xcb, in_=xc, func=AF.Copy), XC.b, XCB.b)
        blk_of = {}
        for bi, (mo, kc) in enumerate(GBLK):
            blk_of.setdefault(mo, []).append((bi, kc))
        SG2s = [TV(0, 2 * n), TV(2, 2 * n), TV(5, 2 * n)]
        for jp in range(3):
            pbG, pbGb = proj_pair("gr", 2 * jp, n)
            sigmoid_act(SG2s[jp].f(), SG2s[jp].b, pbG[:, 0:2 * n], pbGb)
            tk.op(DVE, lambda e, jp=jp: e.tensor_tensor(out=SG2s[jp].f(), in0=SG2s[jp].f(), in1=pbG[:, 0:2 * n], op=ALU.mult), SG2s[jp].b + pbGb, SG2s[jp].b)
        sets = []
        for si in range(2):
            base = 17 + 8 * si
            sets.append((TV(base, 2 * n), TV(base, 2 * n, eoff=2 * n), TV(base + 4, 2 * n), TV(base + 6, 2 * n)))
        for bt in range(3):
            m0 = 2 * bt
            RGv, IGv, A2v, HHv = sets[bt % 2]
            rg = RGv.f().rearrange("p (j t) -> p j t", j=2); ig = IGv.f().rearrange("p (j t) -> p j t", j=2)
            hhv = HHv.f().rearrange("p (j t) -> p j t", j=2)
            for q in range(2):
                mo = m0 + q
                pb, pbb = bank()
                lst = blk_of[mo]
                tk.group(PE, [(lambda e, bi=bi, kc=kc, i=i: e.matmul(pb[:, 0:n], lhsT=WAt[:, bi, :], rhs=xcb[:, kc, :], start=(i == 0), stop=(i == len(lst) - 1)))
                              for i, (bi, kc) in enumerate(lst)], [WAb] + XCB.b, pbb)
                tk.group(PE, [(lambda e, bi=bi, kc=kc, i=i: e.matmul(pb[:, n:2 * n], lhsT=WXt[:, bi, :], rhs=xcb[:, kc, :], start=(i == 0), stop=(i == len(lst) - 1)))
                              for i, (bi, kc) in enumerate(lst)], [WXb] + XCB.b, pbb)
                tk.op(ACT, lambda e, mo=mo, q=q, pb=pb: e.activation(out=rg[:, q, :], in_=pb[:, 0:n], func=AF.Exp, bias=NEGt[:, mo, 0:1], scale=-1.0), pbb + [NEGb], RGv.b)
                tk.op(ACT, lambda e, mo=mo, q=q, pb=pb: e.activation(out=ig[:, q, :], in_=pb[:, n:2 * n], func=AF.Exp, bias=NEGt[:, mo, 1:2], scale=-1.0), pbb + [NEGb], IGv.b)
            RI = TV(17 + 8 * (bt % 2), 4 * n)
            tk.op(ACT, lambda e, RI=RI: e.activation(out=RI.f(), in_=RI.f(), func=AF.Ln, bias=ONEt[:], scale=1.0), RI.b + [CONSTb], RI.b)
            tk.op(ACT, lambda e, RI=RI: e.activation(out=RI.f(), in_=RI.f(), func=AF.Exp, scale=-1.0), RI.b, RI.b)
            for q in range(2):
                mo = m0 + q
                tk.op(ACT, lambda e, mo=mo, q=q: e.activation(out=rg[:, q, :], in_=rg[:, q, :], func=AF.Exp, scale=CVt[:, mo:mo + 1]), RGv.b + [CVb], RGv.b)
            tk.op(DVE, lambda e: e.tensor_tensor(out=A2v.f(), in0=RGv.f(), in1=RGv.f(), op=ALU.mult), RGv.b, A2v.b)
            tk.op(ACT, lambda e: e.activation(out=A2v.f(), in_=A2v.f(), func=AF.Ln, bias=ONEt[:], scale=-1.0), A2v.b + [CONSTb], A2v.b)
            tk.op(ACT, lambda e: e.activation(out=A2v.f(), in_=A2v.f(), func=AF.Exp, scale=0.5), A2v.b, A2v.b)
            tk.op(DVE, lambda e, m0=m0: e.tensor_tensor(out=ig, in0=xc[:, m0:m0 + 2, :], in1=ig, op=ALU.mult), IGv.b + XC.b, IGv.b)
            tk.op(DVE, lambda e: e.tensor_tensor(out=IGv.f(), in0=IGv.f(), in1=A2v.f(), op=ALU.mult), IGv.b + A2v.b, IGv.b)
            for q in range(2):
                mo = m0 + q
                aq = rg[:, q, :]; bq = ig[:, q, :]; hq = hhv[:, q, :]
                if isp:
                    if g.first:
                        init = 0.0; rdi = []
                    else:
                        init = HSTt[:, mo:mo + 1]; rdi = [HSTb[mo]]
                else:
                    aa3 = v3(aq); bx3 = v3(bq)
                    tk.op(DVE, lambda e, mo=mo, aa3=aa3: e.tensor_tensor(out=T0.f(), in0=aa3[:, :, 0], in1=H0t[:, mo, :], op=ALU.mult), RGv.b + H0v.b, T0.b)
                    tk.op(DVE, lambda e, bx3=bx3: e.tensor_tensor(out=bx3[:, :, 0], in0=bx3[:, :, 0], in1=T0.f(), op=ALU.add), IGv.b + T0.b, IGv.b)
                    tk.op(DVE, lambda e, aa3=aa3: e.memset(aa3[:, :, 0], 0.0), RGv.b, RGv.b)
                    init = 0.0; rdi = []
                tk.op(DVE, lambda e, init=init, aq=aq, bq=bq, hq=hq: e.tensor_tensor_scan(out=hq, data0=aq, data1=bq, initial=init, op0=ALU.mult, op1=ALU.add),
                      RGv.b + IGv.b + rdi, HHv.b)
                if isp:
                    tk.op(POOL, lambda e, mo=mo, hq=hq: e.tensor_copy(out=HSTt[:, mo:mo + 1], in_=hq[:, n - 1:n]), HHv.b, [HSTb[mo]])
                else:
                    tk.op(POOL, lambda e, mo=mo, hq=hq: e.tensor_copy(out=SHOt[:, mo, :], in_=v3(hq)[:, :, tlen - 1]), HHv.b, SHOv.b)
            sgs = SG2s[bt]
            tk.op(DVE, lambda e, m0=m0, sgs=sgs: e.tensor_tensor(out=CATt[:, 6 + m0:6 + m0 + 2, 0:n], in0=hhv, in1=sgs.f().rearrange("p (j t) -> p j t", j=2), op=ALU.mult),
                  HHv.b + sgs.b, CATb[6 + m0:6 + m0 + 2])
        if isp:
            if g.last:
                tk.dma(SP, phT[l].rearrange("(j p) o -> p (j o)", p=128), HSTt[:], HSTb, [], is_output=True, slow=True)
        else:
            tk.dma(SP, shT[l].rearrange("(j p) b -> p j b", p=128), SHOt, SHOv.b, [], is_output=True)

        if after_lru is not None:
            after_lru()
        ck(6)
        QT = TV(B0 + 0, 2 * n); SGQ = TV(B0 + 2, 4 * n)
        HB = max(1, min(NH, 512 // (2 * n)))
        PT = [TV(B0 + 6, HB * n), TV(B0 + 7, HB * n)]
        RD = TV(B0 + 8, HB * n); OT = TV(B0 + 9, HB * n)
        qt = QT.h().rearrange("p (h t) -> p h t", h=NH)
        sgq = SGQ.f().rearrange("p (h t) -> p h t", h=NH)
        for jp in range(2):
            pbQ, pbQb = proj_pair("q", 2 * jp, n)
            tk.op(ACT, lambda e, jp=jp: e.activation(out=qt[:, 2 * jp:2 * jp + 2, :], in_=pbQ[:, 0:2 * n].rearrange("p (h t) -> p h t", h=2), func=AF.Copy),
                  pbQb, QT.b)
            pbG, pbGb = proj_pair("gq", 2 * jp, n)
            g3 = pbG[:, 0:2 * n].rearrange("p (h t) -> p h t", h=2)
            sigmoid_act(sgq[:, 2 * jp:2 * jp + 2, :], SGQ.b, g3, pbGb)
            tk.op(DVE, lambda e, jp=jp, g3=g3: e.tensor_tensor(out=sgq[:, 2 * jp:2 * jp + 2, :], in0=sgq[:, 2 * jp:2 * jp + 2, :], in1=g3, op=ALU.mult),
                  SGQ.b + pbGb, SGQ.b)
        sc = 1.0 / math.sqrt(128.0)
        if not isp:
            kring = [(KTSt[i][:], [KTSb[i]]) for i in range(2)]
            vring = [(VSt[i][:], [VSb[i]]) for i in range(2)]
            for i in range(4):
                kv_ = TV(15 + 2 * i, NH * NMEM // 2)
                kring.append((kv_.h().rearrange("p (h m) -> p h m", h=NH), kv_.b))
                vv_ = TV(23 + 2 * i, 2 * MW // 2)
                vring.append((vv_.h().rearrange("p (c d) -> p c d", c=2), vv_.b))
            NR = len(kring)
        for hg in range(NH // HB):
            h0 = hg * HB
            pbS, pbSb = bank()
            pbO, pbOb = bank()
            sS = pbS[:, 0:HB * 2 * n].rearrange("p (h c t) -> p h c t", h=HB, c=2)
            sO = pbO[:, 0:HB * 2 * n].rearrange("p (h c t) -> p h c t", h=HB, c=2)
            pt = PT[hg % 2]
            ptv = pt.h().rearrange("p (h c t) -> p h c t", h=HB, c=2)
            if isp:
                for hh in range(HB):
                    h = h0 + hh
                    for mc in range(2):
                        tk.group(PE, [lambda e, hh=hh, h=h, mc=mc: e.matmul(sS[:, hh, mc, 0:n], lhsT=KTPt[:, h, mc * 128:(mc + 1) * 128], rhs=qt[:, h, 0:n],
                                                                          start=True, stop=True)], [KTPb] + QT.b, pbSb)
            else:
                for b in range(NB):
                    kap, kb_ = kring[b % NR]
                    if b >= 2:
                        tk.dma(POOL, kap, ckT[l, b].rearrange("(h d) m -> d h m", d=128), [], kb_)
                    c0, c1 = b * TS, (b + 1) * TS
                    fns = []
                    for hh in range(HB):
                        h = h0 + hh
                        for mc in range(2):
                            fns.append(lambda e, hh=hh, h=h, mc=mc, kap=kap, c0=c0, c1=c1: e.matmul(sS[:, hh, mc, c0:c1], lhsT=kap[:, h, mc * 128:(mc + 1) * 128],
                                                                                                  rhs=qt[:, h, c0:c1], start=True, stop=True))
                    tk.group(PE, fns, kb_ + QT.b, pbSb)
            tk.op(ACT, lambda e: e.activation(out=ptv, in_=sS, func=AF.Exp, scale=sc), pbSb, pt.b)
            for hh in range(HB):
                tk.group(PE, [(lambda e, hh=hh, mc=mc: e.matmul(sO[:, hh, 0, :], lhsT=ONESt[:], rhs=ptv[:, hh, mc, :], start=(mc == 0), stop=(mc == 1)))
                              for mc in range(2)], pt.b + [ONESb], pbOb)
            if isp:
                for hh in range(HB):
                    h = h0 + hh
                    tk.group(PE, [(lambda e, hh=hh, h=h, mc=mc: e.matmul(sO[:, hh, 1, 0:n], lhsT=VPt[:, mc, h * 128:(h + 1) * 128], rhs=ptv[:, hh, mc, 0:n],
                                                                       start=(mc == 0), stop=(mc == 1))) for mc in range(2)], [VPb] + pt.b, pbOb)
            else:
                for b in range(NB):
                    vap, vb_ = vring[b % NR]
                    if b >= 2:
                        tk.dma(POOL, vap, cv[l, b].rearrange("(c p) d -> p c d", p=128), [], vb_)
                    c0, c1 = b * TS, (b + 1) * TS
                    for hh in range(HB):
                        h = h0 + hh
                        tk.group(PE, [(lambda e, hh=hh, h=h, mc=mc, vap=vap, c0=c0, c1=c1: e.matmul(sO[:, hh, 1, c0:c1], lhsT=vap[:, mc, h * 128:(h + 1) * 128],
                                                                                                  rhs=ptv[:, hh, mc, c0:c1], start=(mc == 0), stop=(mc == 1)))
                                      for mc in range(2)], vb_ + pt.b, pbOb)
            rd = RD.f().rearrange("p (h t) -> p h t", h=HB)
            ot = OT.f().rearrange("p (h t) -> p h t", h=HB)
            tk.op(DVE, lambda e: e.reciprocal(out=rd, in_=sO[:, :, 0, :]), pbOb, RD.b)
            tk.op(DVE, lambda e: e.tensor_tensor(out=ot, in0=sO[:, :, 1, :], in1=rd, op=ALU.mult), pbOb + RD.b, OT.b)
            tk.op(DVE, lambda e, h0=h0: e.tensor_tensor(out=CATt[:, 12 + h0:12 + h0 + HB, 0:n], in0=ot, in1=sgq[:, h0:h0 + HB, :], op=ALU.mult),
                  OT.b + SGQ.b, CATb[12 + h0:12 + h0 + HB])

    def stage_C(l, g):
        pi = l % 2
        n = g.n
        xi = g.xi
        X = XBt[xi]
        O32 = TV(25, 8 * SL); OSQ = TV(21, 4 * SL); R2 = TV(20, n)
        o32 = O32.f().rearrange("p (k t) -> p k t", k=KD)
        osq = OSQ.h().rearrange("p (k t) -> p k t", k=KD)
        for dp in range(4):
            pb, pbb = bank()
            for s in range(2):
                d = 2 * dp + s
                fns = []
                for kc in range(16):
                    si, kl = (0, kc) if kc < 6 else ((1, kc - 6) if kc < 12 else (2, kc - 12))
                    fns.append(lambda e, kc=kc, si=si, kl=kl, d=d, s=s: e.matmul(pb[:, s * n:(s + 1) * n], lhsT=WOt[si][:, kl, d * 128:(d + 1) * 128],
                                                                               rhs=CATt[:, kc, 0:n], start=(kc == 0), stop=(kc == 15)))
                tk.group(PE, fns, WOb + CATb, pbb)
            pv = pb[:, 0:2 * n].rearrange("p (s t) -> p s t", s=2)
            tk.op(ACT, lambda e, dp=dp, pv=pv: e.activation(out=o32[:, 2 * dp:2 * dp + 2, 0:n], in_=pv, func=AF.Copy), pbb, O32.b)
            tk.op(ACT, lambda e, dp=dp, pv=pv: e.activation(out=osq[:, 2 * dp:2 * dp + 2, 0:n], in_=pv, func=AF.Square), pbb, OSQ.b)
        pb, pbb = bank()
        tk.group(PE, [(lambda e, kc=kc: e.matmul(pb[:, 0:n], lhsT=ONESt[:], rhs=osq[:, kc, 0:n], start=(kc == 0), stop=(kc == KD - 1)))
                      for kc in range(KD)], OSQ.b + [ONESb], pbb)
        rsqrt_act(R2, pb[:, 0:n], pbb, 1.0 / D)
        tk.op(DVE, lambda e: e.tensor_tensor(out=o32[:, :, 0:n], in0=o32[:, :, 0:n], in1=R2.f().unsqueeze(1).broadcast_to([128, KD, n]), op=ALU.mult),
              O32.b + R2.b, O32.b)
        for d in range(KD):
            tk.op(DVE, lambda e, d=d: e.scalar_tensor_tensor(out=X[:, d, 0:n], in0=o32[:, d, 0:n], scalar=P10t[pi][:, d, 1:2], in1=X[:, d, 0:n],
                                                           op0=ALU.mult, op1=ALU.add), O32.b + [XBb[xi], P10b[pi]], [XBb[xi]])
        if l == L - 1:
            dst = (ypT[:, g.c0:g.c0 + n] if g.kind == "p" else ysT[:, 0:n])
            tk.dma(SP, dst.rearrange("(k p) t -> p k t", p=128), X[:, :, 0:n], [XBb[xi]], [], is_output=True)
        else:
            tk.dma(SP, xscr[l % 2][:, g.c0:g.c0 + n].rearrange("(k p) t -> p k t", p=128), X[:, :, 0:n], [XBb[xi]], [scrb[l % 2][g.idx]])

    def ck(level):
        if cfg.upto < level:
            raise StopEmit()

    def emit_all():
        load_params(0)
        load_weights(0)
        ck(1)
        diag_chunks(0, range(6))
        order = [groups[-1]] + groups[:-1]

        def reload(l, names):
            for nm_ in names:
                if nm_ == "WA":
                    tk.dma(POOL, WAt[:], wab[l], [], [WAb])
                elif nm_ == "WX":
                    tk.dma(POOL, WXt[:], wxb[l], [], [WXb])
                elif nm_ == "WO":
                    r0 = 0
                    for i, n_ in enumerate((6, 6, 4)):
                        tk.dma(POOL, WOt[i][:], w_out[l, r0:r0 + n_ * 128, :].rearrange("(k p) c -> p k c", p=128), [], [WOb[i]])
                        r0 += n_ * 128
                else:
                    (nm, c0, ncs) = [x for x in SEGS if x[0] == nm_][0]
                    tk.dma(POOL, Wt[nm][:], w_in[l, :, c0:c0 + ncs * 128].rearrange("(k p) c -> p k c", p=128), [], [Wb[nm]])

        for l in range(L):
            layer_consts(l)
            if l + 1 < L:
                load_params(l + 1)
            ck(2)
            mem_phase(l)
            if l > 0:
                reload(l, ["gc", "xr", "gr", "WA", "WX", "q", "gq"])
            ck(3)
            stage_A(l, order[0])
            for gi, g in enumerate(order):
                ck(4)
                ac = None; al = None
                last = (gi == NG - 1 and l + 1 < L)
                ngen = min(3, NG - 1)
                per = (6 + ngen - 1) // ngen
                dg = None
                if l + 1 < L and 1 <= gi <= ngen:
                    dg = (lambda gi=gi: diag_chunks(l + 1, range(per * (gi - 1), min(6, per * gi))))
                if last:
                    def ac(dg=dg):
                        if dg is not None:
                            dg()
                        reload(l + 1, ["b", "a"])
                    al = None
                else:
                    ac = dg
                stage_B(l, g, ck, after_conv=ac, after_lru=al)
                if gi == 0 and l > 0:
                    reload(l, ["WO"])
                if gi + 1 < NG:
                    stage_A(l, order[gi + 1])
                ck(8)
                stage_C(l, g)

    try:
        emit_all()
    except StopEmit:
        pass
    tk.finish()
    return nc


def host_inputs(inp, cfg, ncores):
    L, SEQ, NB = cfg.L, cfg.SEQ, cfg.NB
    f = lambda a: np.ascontiguousarray(np.asarray(a, dtype=np.float32))
    p768 = np.concatenate([
        np.asarray(inp["conv_w"])[:L], np.asarray(inp["conv_b"])[:L, None], np.asarray(inp["conv_ln_g"])[:L, None],
        np.asarray(inp["conv_ln_b"])[:L, None], np.asarray(inp["lru_conv_w"])[:L], np.asarray(inp["lru_conv_b"])[:L, None],
        np.asarray(inp["lru_ba"])[:L, None], np.asarray(inp["lru_bx"])[:L, None], np.asarray(inp["lru_lambda"])[:L, None]], axis=1)
    assert p768.shape[1] == NP7
    p768 = f(p768.transpose(0, 2, 1))
    p1024 = f(np.stack([np.asarray(inp["norm_pre_g"])[:L], np.asarray(inp["norm_post_g"])[:L], np.asarray(inp["mem_norm_g"])[:L]], axis=2))

    def blocks(w):
        w = np.asarray(w)[:L]
        bd = np.zeros((L, LW, LW), np.float32)
        for h in range(8):
            bd[:, 96 * h:96 * h + 96, 96 * h:96 * h + 96] = w[:, h]
        out = np.zeros((L, 128, NBLK, 128), np.float32)
        for bi, (mo, kc) in enumerate(GBLK):
            out[:, :, bi, :] = bd[:, kc * 128:(kc + 1) * 128, mo * 128:(mo + 1) * 128]
        return out

    shared = {
        "w_in": f(np.asarray(inp["w_in"])[:L]), "w_out": f(np.asarray(inp["w_out"])[:L]),
        "wk": f(np.asarray(inp["w_mem_k"])[:L]), "wv": f(np.asarray(inp["w_mem_v"])[:L]),
        "wab": blocks(inp["lru_wa"]), "wxb": blocks(inp["lru_wx"]), "p768": p768, "p1024": p1024,
        "ident": np.eye(128, dtype=np.float32),
    }
    xp = np.asarray(inp["x_prompt"]); xs = np.asarray(inp["x_sample"]); mem = np.asarray(inp["mem_prompt"])
    cc = np.asarray(inp["cache_conv"]); cl = np.asarray(inp["cache_lru_conv"]); h0 = np.asarray(inp["state_lru_h"])
    ck = np.asarray(inp["cache_mem_k"]); cvv = np.asarray(inp["cache_mem_v"])
    maps = []
    for i in range(ncores):
        sl = slice(i * NB, (i + 1) * NB)
        m = dict(shared)
        m["xpT"] = f(xp[i, :SEQ].T)
        m["xsT"] = f(xs[sl].reshape(NB * TS, D).T)
        m["memT"] = f(mem[i].T)
        m["cconvT"] = f(cc[:L, sl].transpose(0, 3, 1, 2))
        m["clruT"] = f(cl[:L, sl].transpose(0, 3, 1, 2))
        m["h0T"] = f(h0[:L, sl].transpose(0, 2, 1))
        m["ckT"] = f(ck[:L, sl].reshape(L, NB, NMEM, MW).transpose(0, 1, 3, 2))
        m["cv"] = f(cvv[:L, sl].reshape(L, NB, NMEM, MW))
        maps.append(m)
    return maps


def host_outputs(results, cfg, ncores):
    L, SEQ, NB = cfg.L, cfg.SEQ, cfg.NB
    yp = np.stack([r["ypT"].T for r in results])
    ys = np.concatenate([r["ysT"].T.reshape(NB, TS, D) for r in results])
    pconv = np.stack([r["pconvT"].transpose(0, 2, 1) for r in results], axis=1)
    plconv = np.stack([r["plconvT"].transpose(0, 2, 1) for r in results], axis=1)
    ph = np.stack([r["phT"][:, :, 0] for r in results], axis=1)
    pmk = np.stack([r["pmkT"].transpose(0, 2, 1).reshape(L, NMEM, NH, 128) for r in results], axis=1)
    pmv = np.stack([r["pmv"].reshape(L, NMEM, NH, 128) for r in results], axis=1)
    sconv = np.concatenate([r["sconvT"].transpose(0, 2, 3, 1) for r in results], axis=1)
    slconv = np.concatenate([r["slconvT"].transpose(0, 2, 3, 1) for r in results], axis=1)
    sh = np.concatenate([r["shT"].transpose(0, 2, 1) for r in results], axis=1)
    outs = (yp, ys, pconv, plconv, ph, pmk, pmv, sconv, slconv, sh)
    return tuple(np.ascontiguousarray(o.astype(np.float32)) for o in outs)


_CACHE = {}


def kernel(**inputs):
    cfg = Cfg()
    ncores = 8
    if "nc" not in _CACHE:
        _CACHE["nc"] = build(cfg)
    nc = _CACHE["nc"]
    maps = host_inputs(inputs, cfg, ncores)
    res = run_bass_kernel_spmd(nc, maps, core_ids=list(range(ncores)))
    return host_outputs(res.results, cfg, ncores)
```

```python
import math
import numpy as np
import concourse.bass as bass
import concourse.mybir as mybir
from concourse.bass_utils import run_bass_kernel_spmd

F32 = mybir.dt.float32
BF16 = mybir.dt.bfloat16
AF = mybir.ActivationFunctionType
ALU = mybir.AluOpType

D = 1024
KD = 8
CW = 768
LW = 768
MW = 512
NH = 4
NMEM = 256
CK = 31
HK = CK - 1
LK = 4
LH = LK - 1
INW = 4864
EPS = 1e-6
SEGS = [("a", 0, 6), ("b", 768, 6), ("gc", 1536, 6), ("xr", 2304, 6), ("gr", 3072, 6), ("q", 3840, 4), ("gq", 4352, 4)]
NP7 = 42
R_CW, R_CB, R_LNG, R_LNB, R_LCW, R_LCB, R_BA, R_BX, R_LAM = 0, 31, 32, 33, 34, 38, 39, 40, 41
TS = 4


def gate_blocks():
    out = []
    for mo in range(6):
        hlo = (128 * mo) // 96
        hhi = (128 * mo + 127) // 96
        klo = (96 * hlo) // 128
        khi = (96 * hhi + 95) // 128
        for kc in range(klo, khi + 1):
            out.append((mo, kc))
    return out


GBLK = gate_blocks()
NBLK = len(GBLK)


class StopEmit(Exception):
    pass


class Buf:
    __slots__ = ("w", "r", "name", "excl")

    def __init__(self, name="", excl=False):
        self.w = None
        self.r = []
        self.name = name
        self.excl = excl


class Eng:
    def __init__(self, nc, e, name):
        self.e = e
        self.name = name
        self.sem = nc.alloc_semaphore("es_" + name)
        self.cnt = 0
        self.seen = {}


class Tracker:
    def __init__(self, nc, ndma=56):
        self.nc = nc
        self.pe = Eng(nc, nc.tensor, "pe")
        self.act = Eng(nc, nc.scalar, "act")
        self.dve = Eng(nc, nc.vector, "dve")
        self.pool = Eng(nc, nc.gpsimd, "pool")
        self.sp = Eng(nc, nc.sync, "sp")
        self.dpools = {}
        for nm, cnt in (("sp", ndma // 2), ("pool", ndma // 2)):
            self.dpools[nm] = {"sems": [nc.alloc_semaphore(f"ds_{nm}{i}") for i in range(cnt)], "cnt": [0] * cnt, "next": 0}
        self.out_events = []
        import os
        self.nops = 0
        self.maxops = int(os.environ.get("STOPN", "100000000"))
        self.log = []

    def _waits(self, E, reads, writes):
        self.nops += 1
        if self.nops > self.maxops:
            raise StopEmit()
        need = {}

        def add(ev, raw):
            sem, val, eng = ev
            if eng is E and E is not self.pool:
                if not raw or E is self.pe:
                    return
            k = id(sem)
            if k not in need or need[k][1] < val:
                need[k] = (sem, val)

        for b in reads:
            if b.w is not None:
                add(b.w, True)
            if b.excl:
                for ev in b.r:
                    add(ev, False)
        for b in writes:
            if b.w is not None:
                add(b.w, False)
            for ev in b.r:
                add(ev, False)
        for k, (sem, val) in need.items():
            if E.seen.get(k, 0) < val:
                E.e.wait_ge(sem, val)
                E.seen[k] = val

    def _commit(self, ev, reads, writes):
        for b in writes:
            b.w = ev
            b.r = []
        for b in reads:
            b.r = [x for x in b.r if x[0] is not ev[0]]
            b.r.append(ev)

    def op(self, E, fn, reads=(), writes=()):
        self._waits(E, reads, writes)
        ins = fn(E.e)
        E.cnt += 1
        ins.then_inc(E.sem, 1)
        self._commit((E.sem, E.cnt, E), reads, writes)

    def group(self, E, fns, reads=(), writes=()):
        self._waits(E, reads, writes)
        ins = None
        for fn in fns:
            ins = fn(E.e)
        E.cnt += 1
        ins.then_inc(E.sem, 1)
        self._commit((E.sem, E.cnt, E), reads, writes)

    def dma(self, Q, out, in_, reads=(), writes=(), is_output=False, slow=False):
        self._waits(Q, reads, writes)
        dp = self.dpools[Q.name]
        i = dp["next"]
        dp["next"] = (i + 1) % len(dp["sems"])
        if dp["cnt"][i] > 0 and Q.seen.get(id(dp["sems"][i]), 0) < dp["cnt"][i]:
            Q.e.wait_ge(dp["sems"][i], dp["cnt"][i])
            Q.seen[id(dp["sems"][i])] = dp["cnt"][i]
        if slow:
            Q.e.dma_start(out=out, in_=in_, allow_slow_non_contiguous=True).then_inc(dp["sems"][i], 16)
        else:
            Q.e.dma_start(out=out, in_=in_).then_inc(dp["sems"][i], 16)
        dp["cnt"][i] += 16
        ev = (dp["sems"][i], dp["cnt"][i], None)
        self._commit(ev, reads, writes)
        if is_output:
            self.out_events.append(ev)

    def finish(self):
        E = self.sp
        for dp in self.dpools.values():
            for sem, val in zip(dp["sems"], dp["cnt"]):
                if val > 0:
                    E.e.wait_ge(sem, val)
        for G in (self.pe, self.act, self.dve, self.pool):
            if G.cnt > 0:
                E.e.wait_ge(G.sem, G.cnt)


class Cfg:
    def __init__(self, L=4, SEQ=2048, NB=16, T=256, NPE=0, upto=99):
        self.L, self.SEQ, self.NB, self.T, self.NPE = L, SEQ, NB, T, NPE
        self.upto = upto
        self.ndma = 56
        self.NS = NB * TS
        assert SEQ % T == 0 and T >= HK and self.NS <= T


def build(cfg):
    L, SEQ, NB, T = cfg.L, cfg.SEQ, cfg.NB, cfg.T
    NS = cfg.NS
    SL = T
    nc = bass.Bass("TRN2", target_bir_lowering=False)

    def din(name, shape):
        return nc.dram_tensor(name, list(shape), F32, kind="ExternalInput").ap()

    def dout(name, shape):
        return nc.dram_tensor(name, list(shape), F32, kind="ExternalOutput").ap()

    xpT = din("xpT", [D, SEQ]); xsT = din("xsT", [D, NS]); memT = din("memT", [D, NMEM])
    cconvT = din("cconvT", [L, CW, NB, HK]); clruT = din("clruT", [L, LW, NB, LH]); h0T = din("h0T", [L, LW, NB])
    ckT = din("ckT", [L, NB, MW, NMEM]); cv = din("cv", [L, NB, NMEM, MW])
    w_in = din("w_in", [L, D, INW]); w_out = din("w_out", [L, 2 * D, D])
    wk = din("wk", [L, D, MW]); wv = din("wv", [L, D, MW])
    wab = din("wab", [L, 128, NBLK, 128]); wxb = din("wxb", [L, 128, NBLK, 128])
    p768 = din("p768", [L, CW, NP7]); p1024 = din("p1024", [L, D, 3])
    ident_d = din("ident", [128, 128])
    DGd = nc.dram_tensor("dgscr", [L, 6, 128, CK * 128], BF16, kind="Internal").ap()
    DGb = [[Buf(f"dg{l}_{j}") for j in range(6)] for l in range(L)]
    DGLd = nc.dram_tensor("dglscr", [L, 128, 6 * LK * 128], BF16, kind="Internal").ap()
    DGLb = [Buf(f"dgl{l}") for l in range(L)]

    ypT = dout("ypT", [D, SEQ]); ysT = dout("ysT", [D, NS])
    pconvT = dout("pconvT", [L, CW, HK]); plconvT = dout("plconvT", [L, LW, LH]); phT = dout("phT", [L, LW, 1])
    pmkT = dout("pmkT", [L, MW, NMEM]); pmv = dout("pmv", [L, NMEM, MW])
    sconvT = dout("sconvT", [L, CW, NB, HK]); slconvT = dout("slconvT", [L, LW, NB, LH]); shT = dout("shT", [L, LW, NB])
    xscr = [nc.dram_tensor(f"xscr{i}", [D, SEQ + NS], F32, kind="Internal").ap() for i in range(2)]

    tk = Tracker(nc, ndma=cfg.ndma)
    PE, ACT, DVE, POOL, SP = tk.pe, tk.act, tk.dve, tk.pool, tk.sp

    def sb(name, shape, dt):
        return nc.alloc_sbuf_tensor(name, list(shape), dt)

    Wt = {}; Wb = {}
    for (nm, c0, ncs) in SEGS:
        Wt[nm] = sb("W_" + nm, [128, KD, ncs * 128], BF16); Wb[nm] = Buf("W_" + nm)
    WOt = [sb(f"WO{i}", [128, n_, D], BF16) for i, n_ in enumerate((6, 6, 4))]
    WOb = [Buf(f"WO{i}") for i in range(3)]
    WAt = sb("WA", [128, NBLK, 128], BF16); WAb = Buf("WA")
    WXt = sb("WX", [128, NBLK, 128], BF16); WXb = Buf("WX")
    P7t = [sb(f"P7_{i}", [128, 6, NP7], F32) for i in range(2)]; P7b = [Buf(), Buf()]
    P10t = [sb(f"P10_{i}", [128, KD, 3], F32) for i in range(2)]; P10b = [Buf(), Buf()]
    CVt = sb("CV", [128, 6], F32); CVb = Buf("CV")
    CVtmp = sb("CVtmp", [128, 6], F32); CVtmpb = Buf("CVtmp")
    NEGt = sb("NEGP", [128, 6, 4], F32); NEGb = Buf("NEGP")
    ONESt = sb("ONES", [128, 128], BF16); ONESb = Buf("ONES")
    EPSt = sb("EPSc", [128, 1], F32); ONEt = sb("ONEc", [128, 1], F32); CONSTb = Buf("const")
    KTPt = sb("KTP", [128, NH, NMEM], BF16); KTPb = Buf("KTP")
    VPt = sb("VP", [128, 2, MW], BF16); VPb = Buf("VP")
    HISTCt = sb("HISTC", [128, 6, HK], F32); HISTCb = [Buf() for _ in range(3)]
    HISTLt = sb("HISTL", [128, 6, LH], F32); HISTLb = [Buf() for _ in range(3)]
    HSTt = sb("HST", [128, 6], F32); HSTb = [Buf() for _ in range(6)]
    XBt = [sb(f"XB{i}", [128, KD, T], F32) for i in range(2)]; XBb = [Buf(), Buf()]
    XNt = sb("XN", [128, KD, T], BF16); XNb = Buf("XN")
    CATt = sb("CAT", [128, 16, T], BF16); CATb = [Buf(f"cat{i}") for i in range(16)]
    IDENTt = sb("IDENT", [128, 128], BF16); IDENTb = Buf("IDENT")
    DGRt = [sb(f"DGR{i}", [128, CK, 128], BF16) for i in range(2)]; DGRb = [Buf("dgr0"), Buf("dgr1")]
    dgr_state = {"i": 0}
    NSLOT = 41
    TMt = sb("TM", [128, NSLOT * SL], F32)
    TMb = [Buf(f"tm{i}") for i in range(NSLOT)]
    PBt = [nc.alloc_psum_tensor(f"PB{i}", [128, 512], F32) for i in range(8)]
    PBb = [Buf(f"pb{i}", excl=True) for i in range(8)]
    pstate = {"i": 0}

    def bank():
        i = pstate["i"]
        pstate["i"] = (i + 1) % 7
        return PBt[i], [PBb[i]]

    class TV:
        def __init__(self, s0, nel_f32, eoff=0):
            self.e0 = s0 * SL + eoff
            self.nel = nel_f32
            sa = self.e0 // SL
            s1 = (self.e0 + nel_f32 + SL - 1) // SL
            assert s1 <= NSLOT, (s0, nel_f32)
            self.b = TMb[sa:s1]

        def f(self):
            return TMt[:, self.e0:self.e0 + self.nel]

        def h(self):
            return TMt[:, self.e0:self.e0 + self.nel].bitcast(BF16)

    tk.op(POOL, lambda e: e.memset(ONESt[:], 1.0), [], [ONESb])
    tk.op(POOL, lambda e: e.memset(EPSt[:], EPS), [], [CONSTb])
    tk.op(POOL, lambda e: e.memset(ONEt[:], 1.0), [], [CONSTb])
    tk.dma(POOL, IDENTt[:], ident_d, [], [IDENTb])

    def load_weights(l):
        for (nm, c0, ncs) in SEGS:
            tk.dma(POOL, Wt[nm][:], w_in[l, :, c0:c0 + ncs * 128].rearrange("(k p) c -> p k c", p=128), [], [Wb[nm]])
        r0 = 0
        for i, n_ in enumerate((6, 6, 4)):
            tk.dma(POOL, WOt[i][:], w_out[l, r0:r0 + n_ * 128, :].rearrange("(k p) c -> p k c", p=128), [], [WOb[i]])
            r0 += n_ * 128
        tk.dma(POOL, WAt[:], wab[l], [], [WAb])
        tk.dma(POOL, WXt[:], wxb[l], [], [WXb])

    def load_params(l):
        pi = l % 2
        tk.dma(SP, P7t[pi][:], p768[l].rearrange("(j p) r -> p j r", p=128), [], [P7b[pi]])
        tk.dma(SP, P10t[pi][:], p1024[l].rearrange("(j p) r -> p j r", p=128), [], [P10b[pi]])

    def layer_consts(l):
        pi = l % 2
        P7 = P7t[pi]
        tk.op(ACT, lambda e: e.activation(out=CVtmp[:], in_=P7[:, :, R_LAM], func=AF.Exp, scale=-1.0), [P7b[pi]], [CVtmpb])
        tk.op(ACT, lambda e: e.activation(out=CVtmp[:], in_=CVtmp[:], func=AF.Ln, bias=ONEt[:], scale=1.0), [CVtmpb, CONSTb], [CVtmpb])
        tk.op(DVE, lambda e: e.tensor_scalar(out=CVt[:], in0=CVtmp[:], scalar1=-8.0, scalar2=None, op0=ALU.mult), [CVtmpb], [CVb])
        tk.op(DVE, lambda e: e.tensor_scalar(out=NEGt[:, :, 0:2], in0=P7[:, :, R_BA:R_BA + 2], scalar1=-1.0, scalar2=None, op0=ALU.mult), [P7b[pi]], [NEGb])
        tk.op(DVE, lambda e: e.tensor_scalar(out=NEGt[:, :, 2:4], in0=P7[:, :, R_LNG:R_LNG + 2], scalar1=-1.0, scalar2=None, op0=ALU.mult), [P7b[pi]], [NEGb])

    def diag_chunks(l, js):
        pi = l % 2
        for j in js:
            r = dgr_state["i"]; dgr_state["i"] = 1 - r
            tk.op(POOL, lambda e, j=j, r=r: e.tensor_tensor(out=DGRt[r][:], in0=IDENTt[:].unsqueeze(1).broadcast_to([128, CK, 128]),
                                                          in1=P7t[pi][:, j, R_CW:R_CW + CK].unsqueeze(2).broadcast_to([128, CK, 128]), op=ALU.mult),
                  [IDENTb, P7b[pi]], [DGRb[r]])
            tk.dma(SP, DGd[l, j], DGRt[r][:].rearrange("p k c -> p (k c)"), [DGRb[r]], [DGb[l][j]])
        if 5 in js:
            r = dgr_state["i"]; dgr_state["i"] = 1 - r
            for j in range(6):
                tk.op(POOL, lambda e, j=j, r=r: e.tensor_tensor(out=DGRt[r][:, j * LK:(j + 1) * LK, :], in0=IDENTt[:].unsqueeze(1).broadcast_to([128, LK, 128]),
                                                              in1=P7t[pi][:, j, R_LCW:R_LCW + LK].unsqueeze(2).broadcast_to([128, LK, 128]), op=ALU.mult),
                      [IDENTb, P7b[pi]], [DGRb[r]])
            tk.dma(SP, DGLd[l], DGRt[r][:, 0:6 * LK, :].rearrange("p k c -> p (k c)"), [DGRb[r]], [DGLb[l]])

    def mem_phase(l):
        pi = l % 2
        P10 = P10t[pi]
        MEM = TV(0, 8 * SL); MSQ = TV(8, 4 * SL); MN = TV(8, 4 * SL); RM = TV(12, NMEM)
        WKv = TV(13, 8 * SL); WVv = TV(21, 8 * SL); OF = [TV(29, 2 * SL), TV(31, 2 * SL)]
        assert NMEM == SL
        memf = MEM.f().rearrange("p (k m) -> p k m", k=KD)
        msq = MSQ.h().rearrange("p (k m) -> p k m", k=KD)
        mn = MN.h().rearrange("p (k m) -> p k m", k=KD)
        wkv = WKv.h().rearrange("p (k c) -> p k c", k=KD)
        wvv = WVv.h().rearrange("p (k c) -> p k c", k=KD)
        tk.dma(SP, memf, memT.rearrange("(k p) m -> p k m", p=128), [], MEM.b)
        tk.dma(POOL, wkv, wk[l].rearrange("(k p) c -> p k c", p=128), [], WKv.b)
        tk.dma(POOL, wvv, wv[l].rearrange("(k p) c -> p k c", p=128), [], WVv.b)
        tk.op(ACT, lambda e: e.activation(out=msq, in_=memf, func=AF.Square), MEM.b, MSQ.b)
        pb, pbb = bank()
        tk.group(PE, [(lambda e, kc=kc: e.matmul(pb[:, 0:NMEM], lhsT=ONESt[:], rhs=msq[:, kc, :], start=(kc == 0), stop=(kc == KD - 1)))
                      for kc in range(KD)], MSQ.b + [ONESb], pbb)
        tk.op(ACT, lambda e: e.activation(out=RM.f(), in_=pb[:, 0:NMEM], func=AF.Ln, bias=EPSt[:], scale=1.0 / D), pbb + [CONSTb], RM.b)
        tk.op(ACT, lambda e: e.activation(out=RM.f(), in_=RM.f(), func=AF.Exp, scale=-0.5), RM.b, RM.b)
        for kc in range(KD):
            tk.op(DVE, lambda e, kc=kc: e.scalar_tensor_tensor(out=mn[:, kc, :], in0=memf[:, kc, :], scalar=P10[:, kc, 2:3], in1=RM.f(),
                                                             op0=ALU.mult, op1=ALU.mult), MEM.b + RM.b + [P10b[pi]], MN.b)
        for hp in range(2):
            pb, pbb = bank()
            for s in range(2):
                h = 2 * hp + s
                tk.group(PE, [(lambda e, kc=kc, h=h, s=s: e.matmul(pb[:, s * NMEM:(s + 1) * NMEM], lhsT=wkv[:, kc, h * 128:(h + 1) * 128],
                                                                rhs=mn[:, kc, :], start=(kc == 0), stop=(kc == KD - 1))) for kc in range(KD)],
                         WKv.b + MN.b, pbb)
            tk.op(ACT, lambda e, hp=hp: e.activation(out=KTPt[:, 2 * hp:2 * hp + 2, :], in_=pb[:, :].rearrange("p (s m) -> p s m", s=2), func=AF.Copy),
                  pbb, [KTPb])
            of = OF[hp % 2]
            import os
            dbg = os.environ.get("DBG", "")
            if "A" in dbg:
                tk.op(ACT, lambda e: e.activation(out=of.f(), in_=pb[:, :], func=AF.Copy), pbb, of.b)
            elif "S" in dbg:
                tk.op(DVE, lambda e: e.tensor_copy(out=of.f()[:, 0:256], in_=RM.f()), pbb + RM.b, of.b)
            elif "N" in dbg:
                tk.op(DVE, lambda e: e.tensor_copy(out=of.f(), in_=pb[:, :]), [], of.b)
            elif "T" in dbg:
                tk.op(DVE, lambda e: e.tensor_scalar(out=of.f(), in0=pb[:, :], scalar1=1.0, scalar2=None, op0=ALU.mult), pbb, of.b)
            elif "H" in dbg:
                tk.op(DVE, lambda e: e.tensor_copy(out=of.f()[:, 0:256], in_=pb[:, 0:256]), pbb, of.b)
                tk.op(DVE, lambda e: e.tensor_copy(out=of.f()[:, 256:512], in_=pb[:, 256:512]), pbb, of.b)
            else:
                tk.op(DVE, lambda e: e.tensor_copy(out=of.f(), in_=pb[:, :]), pbb, of.b)
            tk.dma(SP, pmkT[l, hp * 256:(hp + 1) * 256, :].rearrange("(s p) m -> p s m", p=128), of.f().rearrange("p (s m) -> p s m", s=2),
                   of.b, [], is_output=True)
        for mc in range(2):
            pb, pbb = bank()
            tk.group(PE, [(lambda e, kc=kc, mc=mc: e.matmul(pb[:, :], lhsT=mn[:, kc, mc * 128:(mc + 1) * 128], rhs=wvv[:, kc, :],
                                                          start=(kc == 0), stop=(kc == KD - 1))) for kc in range(KD)], WVv.b + MN.b, pbb)
            tk.op(ACT, lambda e, mc=mc: e.activation(out=VPt[:, mc, :], in_=pb[:, :], func=AF.Copy), pbb, [VPb])
            of = OF[mc % 2]
            tk.op(DVE, lambda e: e.tensor_copy(out=of.f(), in_=pb[:, :]), pbb, of.b)
            tk.dma(SP, pmv[l, mc * 128:(mc + 1) * 128, :], of.f(), of.b, [], is_output=True)

    class Grp:
        pass

    def make_groups():
        gs = []
        for ti in range(SEQ // T):
            g = Grp(); g.kind = "p"; g.n = T; g.nseq = 1; g.tlen = T; g.c0 = ti * T
            g.first = (ti == 0); g.last = (ti == SEQ // T - 1); g.idx = ti
            gs.append(g)
        g = Grp(); g.kind = "s"; g.n = NS; g.nseq = NB; g.tlen = TS; g.c0 = SEQ; g.first = True; g.last = True; g.idx = SEQ // T
        gs.append(g)
        return gs

    groups = make_groups()
    NG = len(groups)
    scrb = [[Buf() for _ in range(NG)] for _ in range(2)]
    xslot = {"i": 0}

    def proj_pair(seg, j0, n, npair=2):
        pb, pbb = bank()
        for s in range(npair):
            j = j0 + s
            tk.group(PE, [(lambda e, kc=kc, j=j, s=s: e.matmul(pb[:, s * n:(s + 1) * n], lhsT=Wt[seg][:, kc, j * 128:(j + 1) * 128],
                                                             rhs=XNt[:, kc, 0:n], start=(kc == 0), stop=(kc == KD - 1))) for kc in range(KD)],
                     [Wb[seg], XNb], pbb)
        return pb, pbb

    def rsqrt_act(out_tv, in_ap, rd, scale):
        tk.op(ACT, lambda e: e.activation(out=out_tv.f(), in_=in_ap, func=AF.Ln, bias=EPSt[:], scale=scale), rd + [CONSTb], out_tv.b)
        tk.op(ACT, lambda e: e.activation(out=out_tv.f(), in_=out_tv.f(), func=AF.Exp, scale=-0.5), out_tv.b, out_tv.b)

    def sigmoid_act(out_ap, out_b, in_ap, rd, nscale=-1.0, nbias=None):
        if nbias is None:
            tk.op(ACT, lambda e: e.activation(out=out_ap, in_=in_ap, func=AF.Exp, scale=nscale), rd, out_b)
        else:
            tk.op(ACT, lambda e: e.activation(out=out_ap, in_=in_ap, func=AF.Exp, bias=nbias, scale=nscale), rd, out_b)
        tk.op(ACT, lambda e: e.activation(out=out_ap, in_=out_ap, func=AF.Ln, bias=ONEt[:], scale=1.0), out_b + [CONSTb], out_b)
        tk.op(ACT, lambda e: e.activation(out=out_ap, in_=out_ap, func=AF.Exp, scale=-1.0), out_b, out_b)

    def stage_A(l, g):
        pi = l % 2
        n = g.n
        xi = xslot["i"]; xslot["i"] = 1 - xi
        g.xi = xi
        X = XBt[xi]
        if l == 0:
            src = (xpT[:, g.c0:g.c0 + n] if g.kind == "p" else xsT[:, 0:n])
            rd = []
        else:
            src = xscr[(l - 1) % 2][:, g.c0:g.c0 + n]
            rd = [scrb[(l - 1) % 2][g.idx]]
        tk.dma(SP, X[:, :, 0:n], src.rearrange("(k p) t -> p k t", p=128), rd, [XBb[xi]])
        SQ = TV(0, 4 * SL); RT = TV(4, n)
        sq = SQ.h().rearrange("p (k t) -> p k t", k=KD)
        tk.op(ACT, lambda e: e.activation(out=sq[:, :, 0:n], in_=X[:, :, 0:n], func=AF.Square), [XBb[xi]], SQ.b)
        pb, pbb = bank()
        tk.group(PE, [(lambda e, kc=kc: e.matmul(pb[:, 0:n], lhsT=ONESt[:], rhs=sq[:, kc, 0:n], start=(kc == 0), stop=(kc == KD - 1)))
                      for kc in range(KD)], SQ.b + [ONESb], pbb)
        rsqrt_act(RT, pb[:, 0:n], pbb, 1.0 / D)
        for kc in range(KD):
            tk.op(DVE, lambda e, kc=kc: e.scalar_tensor_tensor(out=XNt[:, kc, 0:n], in0=X[:, kc, 0:n], scalar=P10t[pi][:, kc, 0:1], in1=RT.f(),
                                                             op0=ALU.mult, op1=ALU.mult), [XBb[xi], P10b[pi]] + RT.b, [XNb])

    def stage_B(l, g, ck=lambda lv: None, after_conv=None, after_lru=None):
        pi = l % 2
        P7 = P7t[pi]; P7B = P7b[pi]
        n, nseq, tlen = g.n, g.nseq, g.tlen
        isp = (g.kind == "p")

        def v3(ap2):
            return ap2.rearrange("p (s t) -> p s t", s=nseq)

        ulen = HK + tlen
        SIGs = [TV(5 + 2 * i, 2 * n) for i in range(3)]
        UPBs = [TV(11 + 3 * i, nseq * ulen) for i in range(3)]
        UP = TV(20, 2 * nseq * ulen)
        CS = TV(25, 6 * n)
        CSB = TV(31, n); CSQ = TV(32, n)
        MEAN = TV(5, n); MSQv = TV(6, n); VAR = TV(7, n)
        cs = CS.f().rearrange("p (j t) -> p j t", j=6)
        up = UP.f().rearrange("p (j s u) -> p j s u", j=2, s=nseq)
        csb = CSB.h().rearrange("p (j t) -> p j t", j=2)
        csq = CSQ.h().rearrange("p (j t) -> p j t", j=2)
        upbs = [u_.h().rearrange("p (j s u) -> p j s u", j=2, s=nseq) for u_ in UPBs]
        for jp in range(3):
            j0 = 2 * jp
            SIG = SIGs[jp]; UPB = UPBs[jp]; upb = upbs[jp]
            pbB, pbBb = proj_pair("b", j0, n)
            sigmoid_act(SIG.f(), SIG.b, pbB[:, 0:2 * n], pbBb)
            pbA, pbAb = proj_pair("a", j0, n)
            if isp:
                if g.first:
                    tk.op(POOL, lambda e: e.memset(up[:, :, :, 0:HK], 0.0), [], UP.b)
                else:
                    tk.op(POOL, lambda e: e.tensor_copy(out=up[:, :, 0, 0:HK], in_=HISTCt[:, j0:j0 + 2, :]), [HISTCb[jp]], UP.b)
            else:
                STG = TV(27, 2 * NB * HK)
                stg = STG.f().rearrange("p (j s r) -> p j s r", j=2, s=nseq)
                tk.dma(SP, STG.f().rearrange("p (j q) -> p j q", j=2),
                       cconvT[l, j0 * 128:(j0 + 2) * 128].rearrange("(j p) b r -> p j (b r)", p=128), [], STG.b)
                tk.op(POOL, lambda e, stg=stg: e.tensor_copy(out=up[:, :, :, 0:HK], in_=stg), STG.b, UP.b)
            tk.op(POOL, lambda e, upb=upb: e.tensor_copy(out=upb[:, :, :, 0:HK], in_=up[:, :, :, 0:HK]), UP.b, UPB.b)
            a4 = pbA[:, 0:2 * n].rearrange("p (j s t) -> p j s t", j=2, s=nseq)
            s4 = SIG.f().rearrange("p (j s t) -> p j s t", j=2, s=nseq)
            tk.op(DVE, lambda e, a4=a4, s4=s4: e.tensor_tensor(out=up[:, :, :, HK:HK + tlen], in0=a4, in1=s4, op=ALU.mult), pbAb + SIG.b, UP.b)
            tk.op(DVE, lambda e, a4=a4, s4=s4, upb=upb: e.tensor_tensor(out=upb[:, :, :, HK:HK + tlen], in0=a4, in1=s4, op=ALU.mult), pbAb + SIG.b, UPB.b)
            if isp:
                if g.last:
                    tk.dma(SP, pconvT[l, j0 * 128:(j0 + 2) * 128, :].rearrange("(j p) r -> p j r", p=128), up[:, :, 0, tlen:tlen + HK], UP.b, [], is_output=True)
                else:
                    tk.op(POOL, lambda e: e.tensor_copy(out=HISTCt[:, j0:j0 + 2, :], in_=up[:, :, 0, tlen:tlen + HK]), UP.b, [HISTCb[jp]])
            else:
                tk.op(POOL, lambda e, stg=stg: e.tensor_copy(out=stg, in_=up[:, :, :, tlen:tlen + HK]), UP.b, STG.b)
                tk.dma(SP, sconvT[l, j0 * 128:(j0 + 2) * 128].rearrange("(j p) b r -> p j (b r)", p=128),
                       STG.f().rearrange("p (j q) -> p j q", j=2), STG.b, [], is_output=True)
        if isp:
            SGcs = [TV(0, 2 * n), TV(2, 2 * n), TV(23, 2 * n)]
        else:
            SGcs = [TV(0, 2 * n), TV(2, 2 * n), TV(4, 2 * n)]
        for jp in range(3):
            pbG, pbGb = proj_pair("gc", 2 * jp, n)
            sigmoid_act(SGcs[jp].f(), SGcs[jp].b, pbG[:, 0:2 * n], pbGb)
            tk.op(DVE, lambda e, jp=jp, pbG=pbG: e.tensor_tensor(out=SGcs[jp].f(), in0=SGcs[jp].f(), in1=pbG[:, 0:2 * n], op=ALU.mult),
                  SGcs[jp].b + pbGb, SGcs[jp].b)
        SG2s = [TV(33, 2 * n), TV(35, 2 * n), TV(37, 2 * n)]
        for jp in range(3):
            pbG, pbGb = proj_pair("gr", 2 * jp, n)
            sigmoid_act(SG2s[jp].f(), SG2s[jp].b, pbG[:, 0:2 * n], pbGb)
            tk.op(DVE, lambda e, jp=jp, pbG=pbG: e.tensor_tensor(out=SG2s[jp].f(), in0=SG2s[jp].f(), in1=pbG[:, 0:2 * n], op=ALU.mult),
                  SG2s[jp].b + pbGb, SG2s[jp].b)
        pst, pstb = PBt[7], [PBb[7]]
        for jp in range(3):
            j0 = 2 * jp
            upb = upbs[jp]; UPB = UPBs[jp]
            pbc, pbcb = bank()
            for s in range(2):
                j = j0 + s
                r = dgr_state["i"]; dgr_state["i"] = 1 - r
                tk.dma(SP, DGRt[r][:].rearrange("p k c -> p (k c)"), DGd[l, j], [DGb[l][j]], [DGRb[r]])
                tk.group(PE, [(lambda e, k=k, s=s, r=r, upb=upb: e.matmul(pbc[:, s * n:(s + 1) * n], lhsT=DGRt[r][:, k, :], rhs=upb[:, s, :, k:k + tlen],
                                                                        start=(k == 0), stop=(k == CK - 1))) for k in range(CK)],
                         [DGRb[r]] + UPB.b, pbcb)
            for s in range(2):
                j = j0 + s
                tk.op(ACT, lambda e, s=s, j=j: e.activation(out=cs[:, j, :], in_=pbc[:, s * n:(s + 1) * n], func=AF.Identity,
                                                           bias=P7[:, j, R_CB:R_CB + 1], scale=1.0), pbcb + [P7B], CS.b)
            tk.op(ACT, lambda e, j0=j0: e.activation(out=csb, in_=cs[:, j0:j0 + 2, :], func=AF.Copy), CS.b, CSB.b)
            tk.op(ACT, lambda e, j0=j0: e.activation(out=csq, in_=cs[:, j0:j0 + 2, :], func=AF.Square), CS.b, CSQ.b)
            tk.group(PE, [(lambda e, s=s: e.matmul(pst[:, 0:n], lhsT=ONESt[:], rhs=csb[:, s, :], start=(jp == 0 and s == 0), stop=(jp == 2 and s == 1),
                                                   skip_group_check=True)) for s in range(2)], CSB.b + [ONESb], pstb)
            tk.group(PE, [(lambda e, s=s: e.matmul(pst[:, n:2 * n], lhsT=ONESt[:], rhs=csq[:, s, :], start=False, stop=(jp == 2 and s == 1),
                                                   skip_group_check=True)) for s in range(2)], CSQ.b + [ONESb], pstb)
        if after_conv is not None:
            after_conv()
        tk.op(DVE, lambda e: e.tensor_scalar(out=MEAN.f(), in0=pst[:, 0:n], scalar1=1.0 / CW, scalar2=None, op0=ALU.mult), pstb, MEAN.b)
        tk.op(DVE, lambda e: e.tensor_tensor(out=MSQv.f(), in0=MEAN.f(), in1=MEAN.f(), op=ALU.mult), MEAN.b, MSQv.b)
        tk.op(DVE, lambda e: e.scalar_tensor_tensor(out=VAR.f(), in0=pst[:, n:2 * n], scalar=1.0 / CW, in1=MSQv.f(), op0=ALU.mult, op1=ALU.subtract),
              pstb + MSQv.b, VAR.b)
        rsqrt_act(VAR, VAR.f(), VAR.b, 1.0)
        TT = TV(8, 6 * n); ZZ = TV(14, 6 * n)
        tt = TT.f().rearrange("p (j t) -> p j t", j=6)
        zz = ZZ.f().rearrange("p (j t) -> p j t", j=6)
        mean_b = MEAN.f().unsqueeze(1).broadcast_to([128, 6, n])
        rs_b = VAR.f().unsqueeze(1).broadcast_to([128, 6, n])
        tk.op(DVE, lambda e: e.tensor_tensor(out=tt, in0=cs, in1=mean_b, op=ALU.subtract), CS.b + MEAN.b, TT.b)
        tk.op(DVE, lambda e: e.tensor_tensor(out=tt, in0=tt, in1=rs_b, op=ALU.mult), TT.b + VAR.b, TT.b)
        for j in range(6):
            tk.op(DVE, lambda e, j=j: e.tensor_scalar(out=zz[:, j, :], in0=tt[:, j, :], scalar1=P7[:, j, R_LNG:R_LNG + 1], scalar2=P7[:, j, R_LNB:R_LNB + 1],
                                                     op0=ALU.mult, op1=ALU.add), TT.b + [P7B], ZZ.b)
        EE = TV(25, 6 * n)
        ee = EE.f().rearrange("p (j t) -> p j t", j=6)
        for j in range(6):
            tk.op(ACT, lambda e, j=j: e.activation(out=ee[:, j, :], in_=tt[:, j, :], func=AF.Exp, bias=NEGt[:, j, 3:4], scale=NEGt[:, j, 2:3]),
                  TT.b + [NEGb], EE.b)
        tk.op(ACT, lambda e: e.activation(out=EE.f(), in_=EE.f(), func=AF.Ln, bias=ONEt[:], scale=1.0), EE.b + [CONSTb], EE.b)
        tk.op(ACT, lambda e: e.activation(out=EE.f(), in_=EE.f(), func=AF.Exp, scale=-1.0), EE.b, EE.b)
        tk.op(DVE, lambda e: e.tensor_tensor(out=ZZ.f(), in0=ZZ.f(), in1=EE.f(), op=ALU.mult), ZZ.b + EE.b, ZZ.b)
        for jp in range(3):
            j0 = 2 * jp
            tk.op(DVE, lambda e, j0=j0, jp=jp: e.tensor_tensor(out=CATt[:, j0:j0 + 2, 0:n], in0=zz[:, j0:j0 + 2, :],
                                                              in1=SGcs[jp].f().rearrange("p (j t) -> p j t", j=2), op=ALU.mult),
                  ZZ.b + SGcs[jp].b, CATb[j0:j0 + 2])

        ck(5)
        B0 = 5
        xlen = LH + tlen
        XRP = TV(B0 + 0, 2 * nseq * xlen)
        XC = TV(B0 + 3, 6 * n); XCB = TV(B0 + 9, 3 * n)
        T0 = TV(4, nseq)
        H0v = TV(3, 6 * NB); SHOv = TV(7, 6 * NB)
        H0t = H0v.f().rearrange("p (j b) -> p j b", j=6); SHOt = SHOv.f().rearrange("p (j b) -> p j b", j=6)
        H0b = None; SHOb = None
        xc = XC.f().rearrange("p (j t) -> p j t", j=6)
        xcb = XCB.h().rearrange("p (j t) -> p j t", j=6)
        xrp = XRP.f().rearrange("p (j s u) -> p j s u", j=2, s=nseq)
        if not isp:
            tk.dma(SP, H0t, h0T[l].rearrange("(j p) b -> p j b", p=128), [], H0v.b)
        XRPB = TV(29, nseq * xlen)
        xrpb = XRPB.h().rearrange("p (j s u) -> p j s u", j=2, s=nseq)
        rl = dgr_state["i"]; dgr_state["i"] = 1 - rl
        tk.dma(SP, DGRt[rl][:, 0:6 * LK, :].rearrange("p k c -> p (k c)"), DGLd[l], [DGLb[l]], [DGRb[rl]])
        for jp in range(3):
            j0 = 2 * jp
            pbX, pbXb = proj_pair("xr", j0, n)
            if isp:
                if g.first:
                    tk.op(POOL, lambda e: e.memset(xrp[:, :, :, 0:LH], 0.0), [], XRP.b)
                else:
                    tk.op(POOL, lambda e: e.tensor_copy(out=xrp[:, :, 0, 0:LH], in_=HISTLt[:, j0:j0 + 2, :]), [HISTLb[jp]], XRP.b)
            else:
                STL = TV(7, 2 * NB * LH)
                stl = STL.f().rearrange("p (j s r) -> p j s r", j=2, s=nseq)
                tk.dma(SP, STL.f().rearrange("p (j q) -> p j q", j=2),
                       clruT[l, j0 * 128:(j0 + 2) * 128].rearrange("(j p) b r -> p j (b r)", p=128), [], STL.b)
                tk.op(POOL, lambda e, stl=stl: e.tensor_copy(out=xrp[:, :, :, 0:LH], in_=stl), STL.b, XRP.b)
            tk.op(POOL, lambda e: e.tensor_copy(out=xrpb[:, :, :, 0:LH], in_=xrp[:, :, :, 0:LH]), XRP.b, XRPB.b)
            x4 = pbX[:, 0:2 * n].rearrange("p (j s t) -> p j s t", j=2, s=nseq)
            tk.op(ACT, lambda e, x4=x4: e.activation(out=xrp[:, :, :, LH:LH + tlen], in_=x4, func=AF.Copy), pbXb, XRP.b)
            tk.op(ACT, lambda e, x4=x4: e.activation(out=xrpb[:, :, :, LH:LH + tlen], in_=x4, func=AF.Copy), pbXb, XRPB.b)
            if isp:
                if g.last:
                    tk.dma(SP, plconvT[l, j0 * 128:(j0 + 2) * 128, :].rearrange("(j p) r -> p j r", p=128), xrp[:, :, 0, tlen:tlen + LH], XRP.b, [], is_output=True)
                else:
                    tk.op(POOL, lambda e: e.tensor_copy(out=HISTLt[:, j0:j0 + 2, :], in_=xrp[:, :, 0, tlen:tlen + LH]), XRP.b, [HISTLb[jp]])
            else:
                tk.op(POOL, lambda e, stl=stl: e.tensor_copy(out=stl, in_=xrp[:, :, :, tlen:tlen + LH]), XRP.b, STL.b)
                tk.dma(SP, slconvT[l, j0 * 128:(j0 + 2) * 128].rearrange("(j p) b r -> p j (b r)", p=128),
                       STL.f().rearrange("p (j q) -> p j q", j=2), STL.b, [], is_output=True)
            pbx, pbxb = bank()
            for s in range(2):
                j = j0 + s
                tk.group(PE, [(lambda e, k=k, s=s, j=j: e.matmul(pbx[:, s * n:(s + 1) * n], lhsT=DGRt[rl][:, j * LK + k, :], rhs=xrpb[:, s, :, k:k + tlen],
                                                               start=(k == 0), stop=(k == LK - 1))) for k in range(LK)],
                         [DGRb[rl]] + XRPB.b, pbxb)
            for s in range(2):
                j = j0 + s
                tk.op(ACT, lambda e, s=s, j=j, pbx=pbx: e.activation(out=xc[:, j, :], in_=pbx[:, s * n:(s + 1) * n], func=AF.Identity,
                                                                    bias=P7[:, j, R_LCB:R_LCB + 1], scale=1.0), pbxb + [P7B], XC.b)
        tk.op(ACT, lambda e: e.activation(out=xcb, in_=xc, func=AF.Copy), XC.b, XCB.b)
        blk_of = {}
        for bi, (mo, kc) in enumerate(GBLK):
            blk_of.setdefault(mo, []).append((bi, kc))
        sets = []
        for si in range(2):
            base = 17 + 8 * si
            sets.append((TV(base, 2 * n), TV(base, 2 * n, eoff=2 * n), TV(base + 4, 2 * n), TV(base + 6, 2 * n)))
        for bt in range(3):
            m0 = 2 * bt
            RGv, IGv, A2v, HHv = sets[bt % 2]
            rg = RGv.f().rearrange("p (j t) -> p j t", j=2); ig = IGv.f().rearrange("p (j t) -> p j t", j=2)
            hhv = HHv.f().rearrange("p (j t) -> p j t", j=2)
            for q in range(2):
                mo = m0 + q
                pb, pbb = bank()
                lst = blk_of[mo]
                tk.group(PE, [(lambda e, bi=bi, kc=kc, i=i: e.matmul(pb[:, 0:n], lhsT=WAt[:, bi, :], rhs=xcb[:, kc, :], start=(i == 0), stop=(i == len(lst) - 1)))
                              for i, (bi, kc) in enumerate(lst)], [WAb] + XCB.b, pbb)
                tk.group(PE, [(lambda e, bi=bi, kc=kc, i=i: e.matmul(pb[:, n:2 * n], lhsT=WXt[:, bi, :], rhs=xcb[:, kc, :], start=(i == 0), stop=(i == len(lst) - 1)))
                              for i, (bi, kc) in enumerate(lst)], [WXb] + XCB.b, pbb)
                tk.op(ACT, lambda e, mo=mo, q=q, pb=pb: e.activation(out=rg[:, q, :], in_=pb[:, 0:n], func=AF.Exp, bias=NEGt[:, mo, 0:1], scale=-1.0), pbb + [NEGb], RGv.b)
                tk.op(ACT, lambda e, mo=mo, q=q, pb=pb: e.activation(out=ig[:, q, :], in_=pb[:, n:2 * n], func=AF.Exp, bias=NEGt[:, mo, 1:2], scale=-1.0), pbb + [NEGb], IGv.b)
            RI = TV(17 + 8 * (bt % 2), 4 * n)
            tk.op(ACT, lambda e, RI=RI: e.activation(out=RI.f(), in_=RI.f(), func=AF.Ln, bias=ONEt[:], scale=1.0), RI.b + [CONSTb], RI.b)
            tk.op(ACT, lambda e, RI=RI: e.activation(out=RI.f(), in_=RI.f(), func=AF.Exp, scale=-1.0), RI.b, RI.b)
            for q in range(2):
                mo = m0 + q
                tk.op(ACT, lambda e, mo=mo, q=q: e.activation(out=rg[:, q, :], in_=rg[:, q, :], func=AF.Exp, scale=CVt[:, mo:mo + 1]), RGv.b + [CVb], RGv.b)
            tk.op(DVE, lambda e: e.tensor_tensor(out=A2v.f(), in0=RGv.f(), in1=RGv.f(), op=ALU.mult), RGv.b, A2v.b)
            tk.op(ACT, lambda e: e.activation(out=A2v.f(), in_=A2v.f(), func=AF.Ln, bias=ONEt[:], scale=-1.0), A2v.b + [CONSTb], A2v.b)
            tk.op(ACT, lambda e: e.activation(out=A2v.f(), in_=A2v.f(), func=AF.Exp, scale=0.5), A2v.b, A2v.b)
            tk.op(DVE, lambda e, m0=m0: e.tensor_tensor(out=ig, in0=xc[:, m0:m0 + 2, :], in1=ig, op=ALU.mult), IGv.b + XC.b, IGv.b)
            tk.op(DVE, lambda e: e.tensor_tensor(out=IGv.f(), in0=IGv.f(), in1=A2v.f(), op=ALU.mult), IGv.b + A2v.b, IGv.b)
            for q in range(2):
                mo = m0 + q
                aq = rg[:, q, :]; bq = ig[:, q, :]; hq = hhv[:, q, :]
                if isp:
                    if g.first:
                        init = 0.0; rdi = []
                    else:
                        init = HSTt[:, mo:mo + 1]; rdi = [HSTb[mo]]
                else:
                    aa3 = v3(aq); bx3 = v3(bq)
                    tk.op(DVE, lambda e, mo=mo, aa3=aa3: e.tensor_tensor(out=T0.f(), in0=aa3[:, :, 0], in1=H0t[:, mo, :], op=ALU.mult), RGv.b + H0v.b, T0.b)
                    tk.op(DVE, lambda e, bx3=bx3: e.tensor_tensor(out=bx3[:, :, 0], in0=bx3[:, :, 0], in1=T0.f(), op=ALU.add), IGv.b + T0.b, IGv.b)
                    tk.op(DVE, lambda e, aa3=aa3: e.memset(aa3[:, :, 0], 0.0), RGv.b, RGv.b)
                    init = 0.0; rdi = []
                tk.op(DVE, lambda e, init=init, aq=aq, bq=bq, hq=hq: e.tensor_tensor_scan(out=hq, data0=aq, data1=bq, initial=init, op0=ALU.mult, op1=ALU.add),
                      RGv.b + IGv.b + rdi, HHv.b)
                if isp:
                    tk.op(POOL, lambda e, mo=mo, hq=hq: e.tensor_copy(out=HSTt[:, mo:mo + 1], in_=hq[:, n - 1:n]), HHv.b, [HSTb[mo]])
                else:
                    tk.op(POOL, lambda e, mo=mo, hq=hq: e.tensor_copy(out=SHOt[:, mo, :], in_=v3(hq)[:, :, tlen - 1]), HHv.b, SHOv.b)
            sgs = SG2s[bt]
            tk.op(DVE, lambda e, m0=m0, sgs=sgs: e.tensor_tensor(out=CATt[:, 6 + m0:6 + m0 + 2, 0:n], in0=hhv, in1=sgs.f().rearrange("p (j t) -> p j t", j=2), op=ALU.mult),
                  HHv.b + sgs.b, CATb[6 + m0:6 + m0 + 2])
        if isp:
            if g.last:
                tk.dma(SP, phT[l].rearrange("(j p) o -> p (j o)", p=128), HSTt[:], HSTb, [], is_output=True, slow=True)
        else:
            tk.dma(SP, shT[l].rearrange("(j p) b -> p j b", p=128), SHOt, SHOv.b, [], is_output=True)

        if after_lru is not None:
            after_lru()
        ck(6)
        QT = TV(B0 + 0, 2 * n); SGQ = TV(B0 + 2, 4 * n)
        HB = max(1, min(NH, 512 // (2 * n)))
        PT = [TV(B0 + 6, HB * n), TV(B0 + 7, HB * n)]
        RD = TV(B0 + 8, HB * n); OT = TV(B0 + 9, HB * n)
        qt = QT.h().rearrange("p (h t) -> p h t", h=NH)
        sgq = SGQ.f().rearrange("p (h t) -> p h t", h=NH)
        for jp in range(2):
            pbQ, pbQb = proj_pair("q", 2 * jp, n)
            tk.op(ACT, lambda e, jp=jp: e.activation(out=qt[:, 2 * jp:2 * jp + 2, :], in_=pbQ[:, 0:2 * n].rearrange("p (h t) -> p h t", h=2), func=AF.Copy),
                  pbQb, QT.b)
            pbG, pbGb = proj_pair("gq", 2 * jp, n)
            g3 = pbG[:, 0:2 * n].rearrange("p (h t) -> p h t", h=2)
            sigmoid_act(sgq[:, 2 * jp:2 * jp + 2, :], SGQ.b, g3, pbGb)
            tk.op(DVE, lambda e, jp=jp, g3=g3: e.tensor_tensor(out=sgq[:, 2 * jp:2 * jp + 2, :], in0=sgq[:, 2 * jp:2 * jp + 2, :], in1=g3, op=ALU.mult),
                  SGQ.b + pbGb, SGQ.b)
        sc = 1.0 / math.sqrt(128.0)
        if not isp:
            kring = []
            vring = []
            for i in range(4):
                kv_ = TV(15 + 2 * i, NH * NMEM // 2)
                kring.append((kv_.h().rearrange("p (h m) -> p h m", h=NH), kv_.b))
                vv_ = TV(23 + 2 * i, 2 * MW // 2)
                vring.append((vv_.h().rearrange("p (c d) -> p c d", c=2), vv_.b))
            NR = len(kring)
        for hg in range(NH // HB):
            h0 = hg * HB
            pbS, pbSb = bank()
            pbO, pbOb = bank()
            sS = pbS[:, 0:HB * 2 * n].rearrange("p (h c t) -> p h c t", h=HB, c=2)
            sO = pbO[:, 0:HB * 2 * n].rearrange("p (h c t) -> p h c t", h=HB, c=2)
            pt = PT[hg % 2]
            ptv = pt.h().rearrange("p (h c t) -> p h c t", h=HB, c=2)
            if isp:
                for hh in range(HB):
                    h = h0 + hh
                    for mc in range(2):
                        tk.group(PE, [lambda e, hh=hh, h=h, mc=mc: e.matmul(sS[:, hh, mc, 0:n], lhsT=KTPt[:, h, mc * 128:(mc + 1) * 128], rhs=qt[:, h, 0:n],
                                                                          start=True, stop=True)], [KTPb] + QT.b, pbSb)
            else:
                for b in range(NB):
                    kap, kb_ = kring[b % NR]
                    tk.dma(POOL, kap, ckT[l, b].rearrange("(h d) m -> d h m", d=128), [], kb_)
                    c0, c1 = b * TS, (b + 1) * TS
                    fns = []
                    for hh in range(HB):
                        h = h0 + hh
                        for mc in range(2):
                            fns.append(lambda e, hh=hh, h=h, mc=mc, kap=kap, c0=c0, c1=c1: e.matmul(sS[:, hh, mc, c0:c1], lhsT=kap[:, h, mc * 128:(mc + 1) * 128],
                                                                                                  rhs=qt[:, h, c0:c1], start=True, stop=True))
                    tk.group(PE, fns, kb_ + QT.b, pbSb)
            tk.op(ACT, lambda e: e.activation(out=ptv, in_=sS, func=AF.Exp, scale=sc), pbSb, pt.b)
            for hh in range(HB):
                tk.group(PE, [(lambda e, hh=hh, mc=mc: e.matmul(sO[:, hh, 0, :], lhsT=ONESt[:], rhs=ptv[:, hh, mc, :], start=(mc == 0), stop=(mc == 1)))
                              for mc in range(2)], pt.b + [ONESb], pbOb)
            if isp:
                for hh in range(HB):
                    h = h0 + hh
                    tk.group(PE, [(lambda e, hh=hh, h=h, mc=mc: e.matmul(sO[:, hh, 1, 0:n], lhsT=VPt[:, mc, h * 128:(h + 1) * 128], rhs=ptv[:, hh, mc, 0:n],
                                                                       start=(mc == 0), stop=(mc == 1))) for mc in range(2)], [VPb] + pt.b, pbOb)
            else:
                for b in range(NB):
                    vap, vb_ = vring[b % NR]
                    tk.dma(POOL, vap, cv[l, b].rearrange("(c p) d -> p c d", p=128), [], vb_)
                    c0, c1 = b * TS, (b + 1) * TS
                    for hh in range(HB):
                        h = h0 + hh
                        tk.group(PE, [(lambda e, hh=hh, h=h, mc=mc, vap=vap, c0=c0, c1=c1: e.matmul(sO[:, hh, 1, c0:c1], lhsT=vap[:, mc, h * 128:(h + 1) * 128],
                                                                                                  rhs=ptv[:, hh, mc, c0:c1], start=(mc == 0), stop=(mc == 1)))
                                      for mc in range(2)], vb_ + pt.b, pbOb)
            rd = RD.f().rearrange("p (h t) -> p h t", h=HB)
            ot = OT.f().rearrange("p (h t) -> p h t", h=HB)
            tk.op(DVE, lambda e: e.reciprocal(out=rd, in_=sO[:, :, 0, :]), pbOb, RD.b)
            tk.op(DVE, lambda e: e.tensor_tensor(out=ot, in0=sO[:, :, 1, :], in1=rd, op=ALU.mult), pbOb + RD.b, OT.b)
            tk.op(DVE, lambda e, h0=h0: e.tensor_tensor(out=CATt[:, 12 + h0:12 + h0 + HB, 0:n], in0=ot, in1=sgq[:, h0:h0 + HB, :], op=ALU.mult),
                  OT.b + SGQ.b, CATb[12 + h0:12 + h0 + HB])

    def stage_C(l, g):
        pi = l % 2
        n = g.n
        xi = g.xi
        X = XBt[xi]
        O32 = TV(25, 8 * SL); OSQ = TV(21, 4 * SL); R2 = TV(20, n)
        o32 = O32.f().rearrange("p (k t) -> p k t", k=KD)
        osq = OSQ.h().rearrange("p (k t) -> p k t", k=KD)
        for dp in range(4):
            pb, pbb = bank()
            for s in range(2):
                d = 2 * dp + s
                fns = []
                for kc in range(16):
                    si, kl = (0, kc) if kc < 6 else ((1, kc - 6) if kc < 12 else (2, kc - 12))
                    fns.append(lambda e, kc=kc, si=si, kl=kl, d=d, s=s: e.matmul(pb[:, s * n:(s + 1) * n], lhsT=WOt[si][:, kl, d * 128:(d + 1) * 128],
                                                                               rhs=CATt[:, kc, 0:n], start=(kc == 0), stop=(kc == 15)))
                tk.group(PE, fns, WOb + CATb, pbb)
            pv = pb[:, 0:2 * n].rearrange("p (s t) -> p s t", s=2)
            tk.op(ACT, lambda e, dp=dp, pv=pv: e.activation(out=o32[:, 2 * dp:2 * dp + 2, 0:n], in_=pv, func=AF.Copy), pbb, O32.b)
            tk.op(ACT, lambda e, dp=dp, pv=pv: e.activation(out=osq[:, 2 * dp:2 * dp + 2, 0:n], in_=pv, func=AF.Square), pbb, OSQ.b)
        pb, pbb = bank()
        tk.group(PE, [(lambda e, kc=kc: e.matmul(pb[:, 0:n], lhsT=ONESt[:], rhs=osq[:, kc, 0:n], start=(kc == 0), stop=(kc == KD - 1)))
                      for kc in range(KD)], OSQ.b + [ONESb], pbb)
        rsqrt_act(R2, pb[:, 0:n], pbb, 1.0 / D)
        tk.op(DVE, lambda e: e.tensor_tensor(out=o32[:, :, 0:n], in0=o32[:, :, 0:n], in1=R2.f().unsqueeze(1).broadcast_to([128, KD, n]), op=ALU.mult),
              O32.b + R2.b, O32.b)
        for d in range(KD):
            tk.op(DVE, lambda e, d=d: e.scalar_tensor_tensor(out=X[:, d, 0:n], in0=o32[:, d, 0:n], scalar=P10t[pi][:, d, 1:2], in1=X[:, d, 0:n],
                                                           op0=ALU.mult, op1=ALU.add), O32.b + [XBb[xi], P10b[pi]], [XBb[xi]])
        if l == L - 1:
            dst = (ypT[:, g.c0:g.c0 + n] if g.kind == "p" else ysT[:, 0:n])
            tk.dma(SP, dst.rearrange("(k p) t -> p k t", p=128), X[:, :, 0:n], [XBb[xi]], [], is_output=True)
        else:
            tk.dma(SP, xscr[l % 2][:, g.c0:g.c0 + n].rearrange("(k p) t -> p k t", p=128), X[:, :, 0:n], [XBb[xi]], [scrb[l % 2][g.idx]])

    def ck(level):
        if cfg.upto < level:
            raise StopEmit()

    def emit_all():
        load_params(0)
        load_weights(0)
        ck(1)
        diag_chunks(0, range(6))
        order = [groups[-1]] + groups[:-1]

        def reload(l, names):
            for nm_ in names:
                if nm_ == "WA":
                    tk.dma(POOL, WAt[:], wab[l], [], [WAb])
                elif nm_ == "WX":
                    tk.dma(POOL, WXt[:], wxb[l], [], [WXb])
                elif nm_ == "WO":
                    r0 = 0
                    for i, n_ in enumerate((6, 6, 4)):
                        tk.dma(POOL, WOt[i][:], w_out[l, r0:r0 + n_ * 128, :].rearrange("(k p) c -> p k c", p=128), [], [WOb[i]])
                        r0 += n_ * 128
                else:
                    (nm, c0, ncs) = [x for x in SEGS if x[0] == nm_][0]
                    tk.dma(POOL, Wt[nm][:], w_in[l, :, c0:c0 + ncs * 128].rearrange("(k p) c -> p k c", p=128), [], [Wb[nm]])

        for l in range(L):
            layer_consts(l)
            if l + 1 < L:
                load_params(l + 1)
            ck(2)
            mem_phase(l)
            if l > 0:
                reload(l, ["gc", "xr", "gr", "WA", "WX", "q", "gq"])
            ck(3)
            stage_A(l, order[0])
            for gi, g in enumerate(order):
                ck(4)
                ac = None; al = None
                last = (gi == NG - 1 and l + 1 < L)
                ngen = min(3, NG - 1)
                per = (6 + ngen - 1) // ngen
                dg = None
                if l + 1 < L and 1 <= gi <= ngen:
                    dg = (lambda gi=gi: diag_chunks(l + 1, range(per * (gi - 1), min(6, per * gi))))
                if last:
                    def ac(dg=dg):
                        if dg is not None:
                            dg()
                        reload(l + 1, ["b", "a"])
                    al = None
                else:
                    ac = dg
                stage_B(l, g, ck, after_conv=ac, after_lru=al)
                if gi == 0 and l > 0:
                    reload(l, ["WO"])
                if gi + 1 < NG:
                    stage_A(l, order[gi + 1])
                ck(8)
                stage_C(l, g)

    try:
        emit_all()
    except StopEmit:
        pass
    tk.finish()
    return nc


def host_inputs(inp, cfg, ncores):
    L, SEQ, NB = cfg.L, cfg.SEQ, cfg.NB
    f = lambda a: np.ascontiguousarray(np.asarray(a, dtype=np.float32))
    p768 = np.concatenate([
        np.asarray(inp["conv_w"])[:L], np.asarray(inp["conv_b"])[:L, None], np.asarray(inp["conv_ln_g"])[:L, None],
        np.asarray(inp["conv_ln_b"])[:L, None], np.asarray(inp["lru_conv_w"])[:L], np.asarray(inp["lru_conv_b"])[:L, None],
        np.asarray(inp["lru_ba"])[:L, None], np.asarray(inp["lru_bx"])[:L, None], np.asarray(inp["lru_lambda"])[:L, None]], axis=1)
    assert p768.shape[1] == NP7
    p768 = f(p768.transpose(0, 2, 1))
    p1024 = f(np.stack([np.asarray(inp["norm_pre_g"])[:L], np.asarray(inp["norm_post_g"])[:L], np.asarray(inp["mem_norm_g"])[:L]], axis=2))

    def blocks(w):
        w = np.asarray(w)[:L]
        bd = np.zeros((L, LW, LW), np.float32)
        for h in range(8):
            bd[:, 96 * h:96 * h + 96, 96 * h:96 * h + 96] = w[:, h]
        out = np.zeros((L, 128, NBLK, 128), np.float32)
        for bi, (mo, kc) in enumerate(GBLK):
            out[:, :, bi, :] = bd[:, kc * 128:(kc + 1) * 128, mo * 128:(mo + 1) * 128]
        return out

    shared = {
        "w_in": f(np.asarray(inp["w_in"])[:L]), "w_out": f(np.asarray(inp["w_out"])[:L]),
        "wk": f(np.asarray(inp["w_mem_k"])[:L]), "wv": f(np.asarray(inp["w_mem_v"])[:L]),
        "wab": blocks(inp["lru_wa"]), "wxb": blocks(inp["lru_wx"]), "p768": p768, "p1024": p1024,
        "ident": np.eye(128, dtype=np.float32),
    }
    xp = np.asarray(inp["x_prompt"]); xs = np.asarray(inp["x_sample"]); mem = np.asarray(inp["mem_prompt"])
    cc = np.asarray(inp["cache_conv"]); cl = np.asarray(inp["cache_lru_conv"]); h0 = np.asarray(inp["state_lru_h"])
    ck = np.asarray(inp["cache_mem_k"]); cvv = np.asarray(inp["cache_mem_v"])
    maps = []
    for i in range(ncores):
        sl = slice(i * NB, (i + 1) * NB)
        m = dict(shared)
        m["xpT"] = f(xp[i, :SEQ].T)
        m["xsT"] = f(xs[sl].reshape(NB * TS, D).T)
        m["memT"] = f(mem[i].T)
        m["cconvT"] = f(cc[:L, sl].transpose(0, 3, 1, 2))
        m["clruT"] = f(cl[:L, sl].transpose(0, 3, 1, 2))
        m["h0T"] = f(h0[:L, sl].transpose(0, 2, 1))
        m["ckT"] = f(ck[:L, sl].reshape(L, NB, NMEM, MW).transpose(0, 1, 3, 2))
        m["cv"] = f(cvv[:L, sl].reshape(L, NB, NMEM, MW))
        maps.append(m)
    return maps


def host_outputs(results, cfg, ncores):
    L, SEQ, NB = cfg.L, cfg.SEQ, cfg.NB
    yp = np.stack([r["ypT"].T for r in results])
    ys = np.concatenate([r["ysT"].T.reshape(NB, TS, D) for r in results])
    pconv = np.stack([r["pconvT"].transpose(0, 2, 1) for r in results], axis=1)
    plconv = np.stack([r["plconvT"].transpose(0, 2, 1) for r in results], axis=1)
    ph = np.stack([r["phT"][:, :, 0] for r in results], axis=1)
    pmk = np.stack([r["pmkT"].transpose(0, 2, 1).reshape(L, NMEM, NH, 128) for r in results], axis=1)
    pmv = np.stack([r["pmv"].reshape(L, NMEM, NH, 128) for r in results], axis=1)
    sconv = np.concatenate([r["sconvT"].transpose(0, 2, 3, 1) for r in results], axis=1)
    slconv = np.concatenate([r["slconvT"].transpose(0, 2, 3, 1) for r in results], axis=1)
    sh = np.concatenate([r["shT"].transpose(0, 2, 1) for r in results], axis=1)
    outs = (yp, ys, pconv, plconv, ph, pmk, pmv, sconv, slconv, sh)
    return tuple(np.ascontiguousarray(o.astype(np.float32)) for o in outs)


_CACHE = {}


def kernel(**inputs):
    cfg = Cfg()
    ncores = 8
    if "nc" not in _CACHE:
        _CACHE["nc"] = build(cfg)
    nc = _CACHE["nc"]
    maps = host_inputs(inputs, cfg, ncores)
    res = run_bass_kernel_spmd(nc, maps, core_ids=list(range(ncores)))
    return host_outputs(res.results, cfg, ncores)
```

```python
import math
import numpy as np
import concourse.bass as bass
import concourse.mybir as mybir
from concourse.bass_utils import run_bass_kernel_spmd

F32 = mybir.dt.float32
BF16 = mybir.dt.bfloat16
AF = mybir.ActivationFunctionType
ALU = mybir.AluOpType

D = 1024
KD = 8
CW = 768
LW = 768
MW = 512
NH = 4
NMEM = 256
CK = 31
HK = CK - 1
LK = 4
LH = LK - 1
INW = 4864
EPS = 1e-6
SEGS = [("a", 0, 6), ("b", 768, 6), ("gc", 1536, 6), ("xr", 2304, 6), ("gr", 3072, 6), ("q", 3840, 4), ("gq", 4352, 4)]
NP7 = 42
R_CW, R_CB, R_LNG, R_LNB, R_LCW, R_LCB, R_BA, R_BX, R_LAM = 0, 31, 32, 33, 34, 38, 39, 40, 41
TS = 4


def gate_blocks():
    out = []
    for mo in range(6):
        hlo = (128 * mo) // 96
        hhi = (128 * mo + 127) // 96
        klo = (96 * hlo) // 128
        khi = (96 * hhi + 95) // 128
        for kc in range(klo, khi + 1):
            out.append((mo, kc))
    return out


GBLK = gate_blocks()
NBLK = len(GBLK)


class StopEmit(Exception):
    pass


class Buf:
    __slots__ = ("w", "r", "name", "excl")

    def __init__(self, name="", excl=False):
        self.w = None
        self.r = []
        self.name = name
        self.excl = excl


class Eng:
    def __init__(self, nc, e, name):
        self.e = e
        self.name = name
        self.sem = nc.alloc_semaphore("es_" + name)
        self.cnt = 0
        self.seen = {}


class Tracker:
    def __init__(self, nc, ndma=56):
        self.nc = nc
        self.pe = Eng(nc, nc.tensor, "pe")
        self.act = Eng(nc, nc.scalar, "act")
        self.dve = Eng(nc, nc.vector, "dve")
        self.pool = Eng(nc, nc.gpsimd, "pool")
        self.sp = Eng(nc, nc.sync, "sp")
        self.dpools = {}
        for nm, cnt in (("sp", ndma // 2), ("pool", ndma // 2)):
            self.dpools[nm] = {"sems": [nc.alloc_semaphore(f"ds_{nm}{i}") for i in range(cnt)], "cnt": [0] * cnt, "next": 0}
        self.out_events = []
        import os
        self.nops = 0
        self.maxops = int(os.environ.get("STOPN", "100000000"))
        self.log = []

    def _waits(self, E, reads, writes):
        self.nops += 1
        if self.nops > self.maxops:
            raise StopEmit()
        need = {}

        def add(ev, raw):
            sem, val, eng = ev
            if eng is E and E is not self.pool:
                if not raw or E is self.pe:
                    return
            k = id(sem)
            if k not in need or need[k][1] < val:
                need[k] = (sem, val)

        for b in reads:
            if b.w is not None:
                add(b.w, True)
            if b.excl:
                for ev in b.r:
                    add(ev, False)
        for b in writes:
            if b.w is not None:
                add(b.w, False)
            for ev in b.r:
                add(ev, False)
        for k, (sem, val) in need.items():
            if E.seen.get(k, 0) < val:
                E.e.wait_ge(sem, val)
                E.seen[k] = val

    def _commit(self, ev, reads, writes):
        for b in writes:
            b.w = ev
            b.r = []
        for b in reads:
            b.r = [x for x in b.r if x[0] is not ev[0]]
            b.r.append(ev)

    def op(self, E, fn, reads=(), writes=()):
        self._waits(E, reads, writes)
        ins = fn(E.e)
        E.cnt += 1
        ins.then_inc(E.sem, 1)
        self._commit((E.sem, E.cnt, E), reads, writes)

    def group(self, E, fns, reads=(), writes=()):
        self._waits(E, reads, writes)
        ins = None
        for fn in fns:
            ins = fn(E.e)
        E.cnt += 1
        ins.then_inc(E.sem, 1)
        self._commit((E.sem, E.cnt, E), reads, writes)

    def dma(self, Q, out, in_, reads=(), writes=(), is_output=False, slow=False):
        self._waits(Q, reads, writes)
        dp = self.dpools[Q.name]
        i = dp["next"]
        dp["next"] = (i + 1) % len(dp["sems"])
        if dp["cnt"][i] > 0 and Q.seen.get(id(dp["sems"][i]), 0) < dp["cnt"][i]:
            Q.e.wait_ge(dp["sems"][i], dp["cnt"][i])
            Q.seen[id(dp["sems"][i])] = dp["cnt"][i]
        if slow:
            Q.e.dma_start(out=out, in_=in_, allow_slow_non_contiguous=True).then_inc(dp["sems"][i], 16)
        else:
            Q.e.dma_start(out=out, in_=in_).then_inc(dp["sems"][i], 16)
        dp["cnt"][i] += 16
        ev = (dp["sems"][i], dp["cnt"][i], None)
        self._commit(ev, reads, writes)
        if is_output:
            self.out_events.append(ev)

    def finish(self):
        E = self.sp
        for dp in self.dpools.values():
            for sem, val in zip(dp["sems"], dp["cnt"]):
                if val > 0:
                    E.e.wait_ge(sem, val)
        for G in (self.pe, self.act, self.dve, self.pool):
            if G.cnt > 0:
                E.e.wait_ge(G.sem, G.cnt)


class Cfg:
    def __init__(self, L=4, SEQ=2048, NB=16, T=256, NPE=0, upto=99):
        self.L, self.SEQ, self.NB, self.T, self.NPE = L, SEQ, NB, T, NPE
        self.upto = upto
        self.ndma = 56
        self.NS = NB * TS
        assert SEQ % T == 0 and T >= HK and self.NS <= T


def build(cfg):
    L, SEQ, NB, T = cfg.L, cfg.SEQ, cfg.NB, cfg.T
    NS = cfg.NS
    SL = T
    nc = bass.Bass("TRN2", target_bir_lowering=False)

    def din(name, shape):
        return nc.dram_tensor(name, list(shape), F32, kind="ExternalInput").ap()

    def dout(name, shape):
        return nc.dram_tensor(name, list(shape), F32, kind="ExternalOutput").ap()

    xpT = din("xpT", [D, SEQ]); xsT = din("xsT", [D, NS]); memT = din("memT", [D, NMEM])
    cconvT = din("cconvT", [L, CW, NB, HK]); clruT = din("clruT", [L, LW, NB, LH]); h0T = din("h0T", [L, LW, NB])
    ckT = din("ckT", [L, NB, MW, NMEM]); cv = din("cv", [L, NB, NMEM, MW])
    w_in = din("w_in", [L, D, INW]); w_out = din("w_out", [L, 2 * D, D])
    wk = din("wk", [L, D, MW]); wv = din("wv", [L, D, MW])
    wab = din("wab", [L, 128, NBLK, 128]); wxb = din("wxb", [L, 128, NBLK, 128])
    p768 = din("p768", [L, CW, NP7]); p1024 = din("p1024", [L, D, 3])
    ident_d = din("ident", [128, 128])
    DGd = nc.dram_tensor("dgscr", [L, 6, 128, CK * 128], BF16, kind="Internal").ap()
    DGb = [[Buf(f"dg{l}_{j}") for j in range(6)] for l in range(L)]
    DGLd = nc.dram_tensor("dglscr", [L, 128, 6 * LK * 128], BF16, kind="Internal").ap()
    DGLb = [Buf(f"dgl{l}") for l in range(L)]

    ypT = dout("ypT", [D, SEQ]); ysT = dout("ysT", [D, NS])
    pconvT = dout("pconvT", [L, CW, HK]); plconvT = dout("plconvT", [L, LW, LH]); phT = dout("phT", [L, LW, 1])
    pmkT = dout("pmkT", [L, MW, NMEM]); pmv = dout("pmv", [L, NMEM, MW])
    sconvT = dout("sconvT", [L, CW, NB, HK]); slconvT = dout("slconvT", [L, LW, NB, LH]); shT = dout("shT", [L, LW, NB])
    xscr = [nc.dram_tensor(f"xscr{i}", [D, SEQ + NS], F32, kind="Internal").ap() for i in range(2)]

    tk = Tracker(nc, ndma=cfg.ndma)
    PE, ACT, DVE, POOL, SP = tk.pe, tk.act, tk.dve, tk.pool, tk.sp

    def sb(name, shape, dt):
        return nc.alloc_sbuf_tensor(name, list(shape), dt)

    Wt = {}; Wb = {}
    for (nm, c0, ncs) in SEGS:
        Wt[nm] = sb("W_" + nm, [128, KD, ncs * 128], BF16); Wb[nm] = Buf("W_" + nm)
    WOt = [sb(f"WO{i}", [128, n_, D], BF16) for i, n_ in enumerate((6, 6, 4))]
    WOb = [Buf(f"WO{i}") for i in range(3)]
    WAt = sb("WA", [128, NBLK, 128], BF16); WAb = Buf("WA")
    WXt = sb("WX", [128, NBLK, 128], BF16); WXb = Buf("WX")
    P7t = [sb(f"P7_{i}", [128, 6, NP7], F32) for i in range(2)]; P7b = [Buf(), Buf()]
    P10t = [sb(f"P10_{i}", [128, KD, 3], F32) for i in range(2)]; P10b = [Buf(), Buf()]
    CVt = sb("CV", [128, 6], F32); CVb = Buf("CV")
    CVtmp = sb("CVtmp", [128, 6], F32); CVtmpb = Buf("CVtmp")
    NEGt = sb("NEGP", [128, 6, 4], F32); NEGb = Buf("NEGP")
    ONESt = sb("ONES", [128, 128], BF16); ONESb = Buf("ONES")
    EPSt = sb("EPSc", [128, 1], F32); ONEt = sb("ONEc", [128, 1], F32); CONSTb = Buf("const")
    KTPt = sb("KTP", [128, NH, NMEM], BF16); KTPb = Buf("KTP")
    VPt = sb("VP", [128, 2, MW], BF16); VPb = Buf("VP")
    HISTCt = sb("HISTC", [128, 6, HK], F32); HISTCb = [Buf() for _ in range(3)]
    HISTLt = sb("HISTL", [128, 6, LH], F32); HISTLb = [Buf() for _ in range(3)]
    HSTt = sb("HST", [128, 6], F32); HSTb = [Buf() for _ in range(6)]
    XBt = [sb(f"XB{i}", [128, KD, T], F32) for i in range(2)]; XBb = [Buf(), Buf()]
    XNt = sb("XN", [128, KD, T], BF16); XNb = Buf("XN")
    CATt = sb("CAT", [128, 16, T], BF16); CATb = [Buf(f"cat{i}") for i in range(16)]
    IDENTt = sb("IDENT", [128, 128], BF16); IDENTb = Buf("IDENT")
    DGRt = [sb(f"DGR{i}", [128, CK, 128], BF16) for i in range(2)]; DGRb = [Buf("dgr0"), Buf("dgr1")]
    dgr_state = {"i": 0}
    NSLOT = 41
    TMt = sb("TM", [128, NSLOT * SL], F32)
    TMb = [Buf(f"tm{i}") for i in range(NSLOT)]
    PBt = [nc.alloc_psum_tensor(f"PB{i}", [128, 512], F32) for i in range(8)]
    PBb = [Buf(f"pb{i}", excl=True) for i in range(8)]
    pstate = {"i": 0}

    def bank():
        i = pstate["i"]
        pstate["i"] = (i + 1) % 7
        return PBt[i], [PBb[i]]

    class TV:
        def __init__(self, s0, nel_f32, eoff=0):
            self.e0 = s0 * SL + eoff
            self.nel = nel_f32
            sa = self.e0 // SL
            s1 = (self.e0 + nel_f32 + SL - 1) // SL
            assert s1 <= NSLOT, (s0, nel_f32)
            self.b = TMb[sa:s1]

        def f(self):
            return TMt[:, self.e0:self.e0 + self.nel]

        def h(self):
            return TMt[:, self.e0:self.e0 + self.nel].bitcast(BF16)

    tk.op(POOL, lambda e: e.memset(ONESt[:], 1.0), [], [ONESb])
    tk.op(POOL, lambda e: e.memset(EPSt[:], EPS), [], [CONSTb])
    tk.op(POOL, lambda e: e.memset(ONEt[:], 1.0), [], [CONSTb])
    tk.dma(POOL, IDENTt[:], ident_d, [], [IDENTb])

    def load_weights(l):
        for (nm, c0, ncs) in SEGS:
            tk.dma(POOL, Wt[nm][:], w_in[l, :, c0:c0 + ncs * 128].rearrange("(k p) c -> p k c", p=128), [], [Wb[nm]])
        r0 = 0
        for i, n_ in enumerate((6, 6, 4)):
            tk.dma(POOL, WOt[i][:], w_out[l, r0:r0 + n_ * 128, :].rearrange("(k p) c -> p k c", p=128), [], [WOb[i]])
            r0 += n_ * 128
        tk.dma(POOL, WAt[:], wab[l], [], [WAb])
        tk.dma(POOL, WXt[:], wxb[l], [], [WXb])

    def load_params(l):
        pi = l % 2
        tk.dma(SP, P7t[pi][:], p768[l].rearrange("(j p) r -> p j r", p=128), [], [P7b[pi]])
        tk.dma(SP, P10t[pi][:], p1024[l].rearrange("(j p) r -> p j r", p=128), [], [P10b[pi]])

    def layer_consts(l):
        pi = l % 2
        P7 = P7t[pi]
        tk.op(ACT, lambda e: e.activation(out=CVtmp[:], in_=P7[:, :, R_LAM], func=AF.Exp, scale=-1.0), [P7b[pi]], [CVtmpb])
        tk.op(ACT, lambda e: e.activation(out=CVtmp[:], in_=CVtmp[:], func=AF.Ln, bias=ONEt[:], scale=1.0), [CVtmpb, CONSTb], [CVtmpb])
        tk.op(DVE, lambda e: e.tensor_scalar(out=CVt[:], in0=CVtmp[:], scalar1=-8.0, scalar2=None, op0=ALU.mult), [CVtmpb], [CVb])
        tk.op(DVE, lambda e: e.tensor_scalar(out=NEGt[:, :, 0:2], in0=P7[:, :, R_BA:R_BA + 2], scalar1=-1.0, scalar2=None, op0=ALU.mult), [P7b[pi]], [NEGb])
        tk.op(DVE, lambda e: e.tensor_scalar(out=NEGt[:, :, 2:4], in0=P7[:, :, R_LNG:R_LNG + 2], scalar1=-1.0, scalar2=None, op0=ALU.mult), [P7b[pi]], [NEGb])

    def diag_chunks(l, js):
        pi = l % 2
        for j in js:
            r = dgr_state["i"]; dgr_state["i"] = 1 - r
            tk.op(POOL, lambda e, j=j, r=r: e.tensor_tensor(out=DGRt[r][:], in0=IDENTt[:].unsqueeze(1).broadcast_to([128, CK, 128]),
                                                          in1=P7t[pi][:, j, R_CW:R_CW + CK].unsqueeze(2).broadcast_to([128, CK, 128]), op=ALU.mult),
                  [IDENTb, P7b[pi]], [DGRb[r]])
            tk.dma(SP, DGd[l, j], DGRt[r][:].rearrange("p k c -> p (k c)"), [DGRb[r]], [DGb[l][j]])
        if 5 in js:
            r = dgr_state["i"]; dgr_state["i"] = 1 - r
            for j in range(6):
                tk.op(POOL, lambda e, j=j, r=r: e.tensor_tensor(out=DGRt[r][:, j * LK:(j + 1) * LK, :], in0=IDENTt[:].unsqueeze(1).broadcast_to([128, LK, 128]),
                                                              in1=P7t[pi][:, j, R_LCW:R_LCW + LK].unsqueeze(2).broadcast_to([128, LK, 128]), op=ALU.mult),
                      [IDENTb, P7b[pi]], [DGRb[r]])
            tk.dma(SP, DGLd[l], DGRt[r][:, 0:6 * LK, :].rearrange("p k c -> p (k c)"), [DGRb[r]], [DGLb[l]])

    def mem_phase(l):
        pi = l % 2
        P10 = P10t[pi]
        MEM = TV(0, 8 * SL); MSQ = TV(8, 4 * SL); MN = TV(8, 4 * SL); RM = TV(12, NMEM)
        WKv = TV(13, 8 * SL); WVv = TV(21, 8 * SL); OF = [TV(29, 2 * SL), TV(31, 2 * SL)]
        assert NMEM == SL
        memf = MEM.f().rearrange("p (k m) -> p k m", k=KD)
        msq = MSQ.h().rearrange("p (k m) -> p k m", k=KD)
        mn = MN.h().rearrange("p (k m) -> p k m", k=KD)
        wkv = WKv.h().rearrange("p (k c) -> p k c", k=KD)
        wvv = WVv.h().rearrange("p (k c) -> p k c", k=KD)
        tk.dma(SP, memf, memT.rearrange("(k p) m -> p k m", p=128), [], MEM.b)
        tk.dma(POOL, wkv, wk[l].rearrange("(k p) c -> p k c", p=128), [], WKv.b)
        tk.dma(POOL, wvv, wv[l].rearrange("(k p) c -> p k c", p=128), [], WVv.b)
        tk.op(ACT, lambda e: e.activation(out=msq, in_=memf, func=AF.Square), MEM.b, MSQ.b)
        pb, pbb = bank()
        tk.group(PE, [(lambda e, kc=kc: e.matmul(pb[:, 0:NMEM], lhsT=ONESt[:], rhs=msq[:, kc, :], start=(kc == 0), stop=(kc == KD - 1)))
                      for kc in range(KD)], MSQ.b + [ONESb], pbb)
        tk.op(ACT, lambda e: e.activation(out=RM.f(), in_=pb[:, 0:NMEM], func=AF.Ln, bias=EPSt[:], scale=1.0 / D), pbb + [CONSTb], RM.b)
        tk.op(ACT, lambda e: e.activation(out=RM.f(), in_=RM.f(), func=AF.Exp, scale=-0.5), RM.b, RM.b)
        for kc in range(KD):
            tk.op(DVE, lambda e, kc=kc: e.scalar_tensor_tensor(out=mn[:, kc, :], in0=memf[:, kc, :], scalar=P10[:, kc, 2:3], in1=RM.f(),
                                                             op0=ALU.mult, op1=ALU.mult), MEM.b + RM.b + [P10b[pi]], MN.b)
        for hp in range(2):
            pb, pbb = bank()
            for s in range(2):
                h = 2 * hp + s
                tk.group(PE, [(lambda e, kc=kc, h=h, s=s: e.matmul(pb[:, s * NMEM:(s + 1) * NMEM], lhsT=wkv[:, kc, h * 128:(h + 1) * 128],
                                                                rhs=mn[:, kc, :], start=(kc == 0), stop=(kc == KD - 1))) for kc in range(KD)],
                         WKv.b + MN.b, pbb)
            tk.op(ACT, lambda e, hp=hp: e.activation(out=KTPt[:, 2 * hp:2 * hp + 2, :], in_=pb[:, :].rearrange("p (s m) -> p s m", s=2), func=AF.Copy),
                  pbb, [KTPb])
            of = OF[hp % 2]
            import os
            dbg = os.environ.get("DBG", "")
            if "A" in dbg:
                tk.op(ACT, lambda e: e.activation(out=of.f(), in_=pb[:, :], func=AF.Copy), pbb, of.b)
            elif "S" in dbg:
                tk.op(DVE, lambda e: e.tensor_copy(out=of.f()[:, 0:256], in_=RM.f()), pbb + RM.b, of.b)
            elif "N" in dbg:
                tk.op(DVE, lambda e: e.tensor_copy(out=of.f(), in_=pb[:, :]), [], of.b)
            elif "T" in dbg:
                tk.op(DVE, lambda e: e.tensor_scalar(out=of.f(), in0=pb[:, :], scalar1=1.0, scalar2=None, op0=ALU.mult), pbb, of.b)
            elif "H" in dbg:
                tk.op(DVE, lambda e: e.tensor_copy(out=of.f()[:, 0:256], in_=pb[:, 0:256]), pbb, of.b)
                tk.op(DVE, lambda e: e.tensor_copy(out=of.f()[:, 256:512], in_=pb[:, 256:512]), pbb, of.b)
            else:
                tk.op(DVE, lambda e: e.tensor_copy(out=of.f(), in_=pb[:, :]), pbb, of.b)
            tk.dma(SP, pmkT[l, hp * 256:(hp + 1) * 256, :].rearrange("(s p) m -> p s m", p=128), of.f().rearrange("p (s m) -> p s m", s=2),
                   of.b, [], is_output=True)
        for mc in range(2):
            pb, pbb = bank()
            tk.group(PE, [(lambda e, kc=kc, mc=mc: e.matmul(pb[:, :], lhsT=mn[:, kc, mc * 128:(mc + 1) * 128], rhs=wvv[:, kc, :],
                                                          start=(kc == 0), stop=(kc == KD - 1))) for kc in range(KD)], WVv.b + MN.b, pbb)
            tk.op(ACT, lambda e, mc=mc: e.activation(out=VPt[:, mc, :], in_=pb[:, :], func=AF.Copy), pbb, [VPb])
            of = OF[mc % 2]
            tk.op(DVE, lambda e: e.tensor_copy(out=of.f(), in_=pb[:, :]), pbb, of.b)
            tk.dma(SP, pmv[l, mc * 128:(mc + 1) * 128, :], of.f(), of.b, [], is_output=True)

    class Grp:
        pass

    def make_groups():
        gs = []
        for ti in range(SEQ // T):
            g = Grp(); g.kind = "p"; g.n = T; g.nseq = 1; g.tlen = T; g.c0 = ti * T
            g.first = (ti == 0); g.last = (ti == SEQ // T - 1); g.idx = ti
            gs.append(g)
        g = Grp(); g.kind = "s"; g.n = NS; g.nseq = NB; g.tlen = TS; g.c0 = SEQ; g.first = True; g.last = True; g.idx = SEQ // T
        gs.append(g)
        return gs

    groups = make_groups()
    NG = len(groups)
    scrb = [[Buf() for _ in range(NG)] for _ in range(2)]
    xslot = {"i": 0}

    def proj_pair(seg, j0, n, npair=2):
        pb, pbb = bank()
        for s in range(npair):
            j = j0 + s
            tk.group(PE, [(lambda e, kc=kc, j=j, s=s: e.matmul(pb[:, s * n:(s + 1) * n], lhsT=Wt[seg][:, kc, j * 128:(j + 1) * 128],
                                                             rhs=XNt[:, kc, 0:n], start=(kc == 0), stop=(kc == KD - 1))) for kc in range(KD)],
                     [Wb[seg], XNb], pbb)
        return pb, pbb

    def rsqrt_act(out_tv, in_ap, rd, scale):
        tk.op(ACT, lambda e: e.activation(out=out_tv.f(), in_=in_ap, func=AF.Ln, bias=EPSt[:], scale=scale), rd + [CONSTb], out_tv.b)
        tk.op(ACT, lambda e: e.activation(out=out_tv.f(), in_=out_tv.f(), func=AF.Exp, scale=-0.5), out_tv.b, out_tv.b)

    def sigmoid_act(out_ap, out_b, in_ap, rd, nscale=-1.0, nbias=None):
        if nbias is None:
            tk.op(ACT, lambda e: e.activation(out=out_ap, in_=in_ap, func=AF.Exp, scale=nscale), rd, out_b)
        else:
            tk.op(ACT, lambda e: e.activation(out=out_ap, in_=in_ap, func=AF.Exp, bias=nbias, scale=nscale), rd, out_b)
        tk.op(ACT, lambda e: e.activation(out=out_ap, in_=out_ap, func=AF.Ln, bias=ONEt[:], scale=1.0), out_b + [CONSTb], out_b)
        tk.op(ACT, lambda e: e.activation(out=out_ap, in_=out_ap, func=AF.Exp, scale=-1.0), out_b, out_b)

    def stage_A(l, g):
        pi = l % 2
        n = g.n
        xi = xslot["i"]; xslot["i"] = 1 - xi
        g.xi = xi
        X = XBt[xi]
        if l == 0:
            src = (xpT[:, g.c0:g.c0 + n] if g.kind == "p" else xsT[:, 0:n])
            rd = []
        else:
            src = xscr[(l - 1) % 2][:, g.c0:g.c0 + n]
            rd = [scrb[(l - 1) % 2][g.idx]]
        tk.dma(SP, X[:, :, 0:n], src.rearrange("(k p) t -> p k t", p=128), rd, [XBb[xi]])
        SQ = TV(0, 4 * SL); RT = TV(4, n)
        sq = SQ.h().rearrange("p (k t) -> p k t", k=KD)
        tk.op(ACT, lambda e: e.activation(out=sq[:, :, 0:n], in_=X[:, :, 0:n], func=AF.Square), [XBb[xi]], SQ.b)
        pb, pbb = bank()
        tk.group(PE, [(lambda e, kc=kc: e.matmul(pb[:, 0:n], lhsT=ONESt[:], rhs=sq[:, kc, 0:n], start=(kc == 0), stop=(kc == KD - 1)))
                      for kc in range(KD)], SQ.b + [ONESb], pbb)
        rsqrt_act(RT, pb[:, 0:n], pbb, 1.0 / D)
        for kc in range(KD):
            tk.op(DVE, lambda e, kc=kc: e.scalar_tensor_tensor(out=XNt[:, kc, 0:n], in0=X[:, kc, 0:n], scalar=P10t[pi][:, kc, 0:1], in1=RT.f(),
                                                             op0=ALU.mult, op1=ALU.mult), [XBb[xi], P10b[pi]] + RT.b, [XNb])

    def stage_B(l, g, ck=lambda lv: None, after_conv=None, after_lru=None):
        pi = l % 2
        P7 = P7t[pi]; P7B = P7b[pi]
        n, nseq, tlen = g.n, g.nseq, g.tlen
        isp = (g.kind == "p")

        def v3(ap2):
            return ap2.rearrange("p (s t) -> p s t", s=nseq)

        ulen = HK + tlen
        SIGs = [TV(5 + 2 * i, 2 * n) for i in range(3)]
        UPBs = [TV(11 + 3 * i, nseq * ulen) for i in range(3)]
        UP = TV(20, 2 * nseq * ulen)
        CS = TV(25, 6 * n)
        CSB = TV(31, n); CSQ = TV(32, n)
        MEAN = TV(5, n); MSQv = TV(6, n); VAR = TV(7, n)
        cs = CS.f().rearrange("p (j t) -> p j t", j=6)
        up = UP.f().rearrange("p (j s u) -> p j s u", j=2, s=nseq)
        csb = CSB.h().rearrange("p (j t) -> p j t", j=2)
        csq = CSQ.h().rearrange("p (j t) -> p j t", j=2)
        upbs = [u_.h().rearrange("p (j s u) -> p j s u", j=2, s=nseq) for u_ in UPBs]
        for jp in range(3):
            j0 = 2 * jp
            SIG = SIGs[jp]; UPB = UPBs[jp]; upb = upbs[jp]
            pbB, pbBb = proj_pair("b", j0, n)
            sigmoid_act(SIG.f(), SIG.b, pbB[:, 0:2 * n], pbBb)
            pbA, pbAb = proj_pair("a", j0, n)
            if isp:
                if g.first:
                    tk.op(POOL, lambda e: e.memset(up[:, :, :, 0:HK], 0.0), [], UP.b)
                else:
                    tk.op(POOL, lambda e: e.tensor_copy(out=up[:, :, 0, 0:HK], in_=HISTCt[:, j0:j0 + 2, :]), [HISTCb[jp]], UP.b)
            else:
                STG = TV(27, 2 * NB * HK)
                stg = STG.f().rearrange("p (j s r) -> p j s r", j=2, s=nseq)
                tk.dma(SP, STG.f().rearrange("p (j q) -> p j q", j=2),
                       cconvT[l, j0 * 128:(j0 + 2) * 128].rearrange("(j p) b r -> p j (b r)", p=128), [], STG.b)
                tk.op(POOL, lambda e, stg=stg: e.tensor_copy(out=up[:, :, :, 0:HK], in_=stg), STG.b, UP.b)
            tk.op(POOL, lambda e, upb=upb: e.tensor_copy(out=upb[:, :, :, 0:HK], in_=up[:, :, :, 0:HK]), UP.b, UPB.b)
            a4 = pbA[:, 0:2 * n].rearrange("p (j s t) -> p j s t", j=2, s=nseq)
            s4 = SIG.f().rearrange("p (j s t) -> p j s t", j=2, s=nseq)
            tk.op(DVE, lambda e, a4=a4, s4=s4: e.tensor_tensor(out=up[:, :, :, HK:HK + tlen], in0=a4, in1=s4, op=ALU.mult), pbAb + SIG.b, UP.b)
            tk.op(DVE, lambda e, a4=a4, s4=s4, upb=upb: e.tensor_tensor(out=upb[:, :, :, HK:HK + tlen], in0=a4, in1=s4, op=ALU.mult), pbAb + SIG.b, UPB.b)
            if isp:
                if g.last:
                    tk.dma(SP, pconvT[l, j0 * 128:(j0 + 2) * 128, :].rearrange("(j p) r -> p j r", p=128), up[:, :, 0, tlen:tlen + HK], UP.b, [], is_output=True)
                else:
                    tk.op(POOL, lambda e: e.tensor_copy(out=HISTCt[:, j0:j0 + 2, :], in_=up[:, :, 0, tlen:tlen + HK]), UP.b, [HISTCb[jp]])
            else:
                tk.op(POOL, lambda e, stg=stg: e.tensor_copy(out=stg, in_=up[:, :, :, tlen:tlen + HK]), UP.b, STG.b)
                tk.dma(SP, sconvT[l, j0 * 128:(j0 + 2) * 128].rearrange("(j p) b r -> p j (b r)", p=128),
                       STG.f().rearrange("p (j q) -> p j q", j=2), STG.b, [], is_output=True)
        if isp:
            SGcs = [TV(0, 2 * n), TV(2, 2 * n), TV(23, 2 * n)]
        else:
            SGcs = [TV(0, 2 * n), TV(2, 2 * n), TV(4, 2 * n)]
        for jp in range(3):
            pbG, pbGb = proj_pair("gc", 2 * jp, n)
            sigmoid_act(SGcs[jp].f(), SGcs[jp].b, pbG[:, 0:2 * n], pbGb)
            tk.op(DVE, lambda e, jp=jp, pbG=pbG: e.tensor_tensor(out=SGcs[jp].f(), in0=SGcs[jp].f(), in1=pbG[:, 0:2 * n], op=ALU.mult),
                  SGcs[jp].b + pbGb, SGcs[jp].b)
        SG2s = [TV(33, 2 * n), TV(35, 2 * n), TV(37, 2 * n)]
        for jp in range(3):
            pbG, pbGb = proj_pair("gr", 2 * jp, n)
            sigmoid_act(SG2s[jp].f(), SG2s[jp].b, pbG[:, 0:2 * n], pbGb)
            tk.op(DVE, lambda e, jp=jp, pbG=pbG: e.tensor_tensor(out=SG2s[jp].f(), in0=SG2s[jp].f(), in1=pbG[:, 0:2 * n], op=ALU.mult),
                  SG2s[jp].b + pbGb, SG2s[jp].b)
        pst, pstb = PBt[7], [PBb[7]]
        for jp in range(3):
            j0 = 2 * jp
            upb = upbs[jp]; UPB = UPBs[jp]
            pbc, pbcb = bank()
            for s in range(2):
                j = j0 + s
                r = dgr_state["i"]; dgr_state["i"] = 1 - r
                tk.dma(SP, DGRt[r][:].rearrange("p k c -> p (k c)"), DGd[l, j], [DGb[l][j]], [DGRb[r]])
                tk.group(PE, [(lambda e, k=k, s=s, r=r, upb=upb: e.matmul(pbc[:, s * n:(s + 1) * n], lhsT=DGRt[r][:, k, :], rhs=upb[:, s, :, k:k + tlen],
                                                                        start=(k == 0), stop=(k == CK - 1))) for k in range(CK)],
                         [DGRb[r]] + UPB.b, pbcb)
            for s in range(2):
                j = j0 + s
                tk.op(ACT, lambda e, s=s, j=j: e.activation(out=cs[:, j, :], in_=pbc[:, s * n:(s + 1) * n], func=AF.Identity,
                                                           bias=P7[:, j, R_CB:R_CB + 1], scale=1.0), pbcb + [P7B], CS.b)
            tk.op(ACT, lambda e, j0=j0: e.activation(out=csb, in_=cs[:, j0:j0 + 2, :], func=AF.Copy), CS.b, CSB.b)
            tk.op(ACT, lambda e, j0=j0: e.activation(out=csq, in_=cs[:, j0:j0 + 2, :], func=AF.Square), CS.b, CSQ.b)
            tk.group(PE, [(lambda e, s=s: e.matmul(pst[:, 0:n], lhsT=ONESt[:], rhs=csb[:, s, :], start=(jp == 0 and s == 0), stop=(jp == 2 and s == 1),
                                                   skip_group_check=True)) for s in range(2)], CSB.b + [ONESb], pstb)
            tk.group(PE, [(lambda e, s=s: e.matmul(pst[:, n:2 * n], lhsT=ONESt[:], rhs=csq[:, s, :], start=False, stop=(jp == 2 and s == 1),
                                                   skip_group_check=True)) for s in range(2)], CSQ.b + [ONESb], pstb)
        if after_conv is not None:
            after_conv()
        tk.op(DVE, lambda e: e.tensor_scalar(out=MEAN.f(), in0=pst[:, 0:n], scalar1=1.0 / CW, scalar2=None, op0=ALU.mult), pstb, MEAN.b)
        tk.op(DVE, lambda e: e.tensor_tensor(out=MSQv.f(), in0=MEAN.f(), in1=MEAN.f(), op=ALU.mult), MEAN.b, MSQv.b)
        tk.op(DVE, lambda e: e.scalar_tensor_tensor(out=VAR.f(), in0=pst[:, n:2 * n], scalar=1.0 / CW, in1=MSQv.f(), op0=ALU.mult, op1=ALU.subtract),
              pstb + MSQv.b, VAR.b)
        rsqrt_act(VAR, VAR.f(), VAR.b, 1.0)
        TT = TV(8, 6 * n); ZZ = TV(14, 6 * n)
        tt = TT.f().rearrange("p (j t) -> p j t", j=6)
        zz = ZZ.f().rearrange("p (j t) -> p j t", j=6)
        mean_b = MEAN.f().unsqueeze(1).broadcast_to([128, 6, n])
        rs_b = VAR.f().unsqueeze(1).broadcast_to([128, 6, n])
        tk.op(DVE, lambda e: e.tensor_tensor(out=tt, in0=cs, in1=mean_b, op=ALU.subtract), CS.b + MEAN.b, TT.b)
        tk.op(DVE, lambda e: e.tensor_tensor(out=tt, in0=tt, in1=rs_b, op=ALU.mult), TT.b + VAR.b, TT.b)
        for j in range(6):
            tk.op(DVE, lambda e, j=j: e.tensor_scalar(out=zz[:, j, :], in0=tt[:, j, :], scalar1=P7[:, j, R_LNG:R_LNG + 1], scalar2=P7[:, j, R_LNB:R_LNB + 1],
                                                     op0=ALU.mult, op1=ALU.add), TT.b + [P7B], ZZ.b)
        EE = TV(25, 6 * n)
        ee = EE.f().rearrange("p (j t) -> p j t", j=6)
        for j in range(6):
            tk.op(ACT, lambda e, j=j: e.activation(out=ee[:, j, :], in_=tt[:, j, :], func=AF.Exp, bias=NEGt[:, j, 3:4], scale=NEGt[:, j, 2:3]),
                  TT.b + [NEGb], EE.b)
        tk.op(ACT, lambda e: e.activation(out=EE.f(), in_=EE.f(), func=AF.Ln, bias=ONEt[:], scale=1.0), EE.b + [CONSTb], EE.b)
        tk.op(ACT, lambda e: e.activation(out=EE.f(), in_=EE.f(), func=AF.Exp, scale=-1.0), EE.b, EE.b)
        tk.op(DVE, lambda e: e.tensor_tensor(out=ZZ.f(), in0=ZZ.f(), in1=EE.f(), op=ALU.mult), ZZ.b + EE.b, ZZ.b)
        for jp in range(3):
            j0 = 2 * jp
            tk.op(DVE, lambda e, j0=j0, jp=jp: e.tensor_tensor(out=CATt[:, j0:j0 + 2, 0:n], in0=zz[:, j0:j0 + 2, :],
                                                              in1=SGcs[jp].f().rearrange("p (j t) -> p j t", j=2), op=ALU.mult),
                  ZZ.b + SGcs[jp].b, CATb[j0:j0 + 2])

        ck(5)
        B0 = 5
        xlen = LH + tlen
        XRP = TV(B0 + 0, 2 * nseq * xlen)
        XC = TV(B0 + 3, 6 * n); XCB = TV(B0 + 9, 3 * n)
        T0 = TV(4, nseq)
        H0v = TV(3, 6 * NB); SHOv = TV(7, 6 * NB)
        H0t = H0v.f().rearrange("p (j b) -> p j b", j=6); SHOt = SHOv.f().rearrange("p (j b) -> p j b", j=6)
        H0b = None; SHOb = None
        xc = XC.f().rearrange("p (j t) -> p j t", j=6)
        xcb = XCB.h().rearrange("p (j t) -> p j t", j=6)
        xrp = XRP.f().rearrange("p (j s u) -> p j s u", j=2, s=nseq)
        if not isp:
            tk.dma(SP, H0t, h0T[l].rearrange("(j p) b -> p j b", p=128), [], H0v.b)
        XRPB = TV(29, nseq * xlen)
        xrpb = XRPB.h().rearrange("p (j s u) -> p j s u", j=2, s=nseq)
        rl = dgr_state["i"]; dgr_state["i"] = 1 - rl
        tk.dma(SP, DGRt[rl][:, 0:6 * LK, :].rearrange("p k c -> p (k c)"), DGLd[l], [DGLb[l]], [DGRb[rl]])
        for jp in range(3):
            j0 = 2 * jp
            pbX, pbXb = proj_pair("xr", j0, n)
            if isp:
                if g.first:
                    tk.op(POOL, lambda e: e.memset(xrp[:, :, :, 0:LH], 0.0), [], XRP.b)
                else:
                    tk.op(POOL, lambda e: e.tensor_copy(out=xrp[:, :, 0, 0:LH], in_=HISTLt[:, j0:j0 + 2, :]), [HISTLb[jp]], XRP.b)
            else:
                STL = TV(7, 2 * NB * LH)
                stl = STL.f().rearrange("p (j s r) -> p j s r", j=2, s=nseq)
                tk.dma(SP, STL.f().rearrange("p (j q) -> p j q", j=2),
                       clruT[l, j0 * 128:(j0 + 2) * 128].rearrange("(j p) b r -> p j (b r)", p=128), [], STL.b)
                tk.op(POOL, lambda e, stl=stl: e.tensor_copy(out=xrp[:, :, :, 0:LH], in_=stl), STL.b, XRP.b)
            tk.op(POOL, lambda e: e.tensor_copy(out=xrpb[:, :, :, 0:LH], in_=xrp[:, :, :, 0:LH]), XRP.b, XRPB.b)
            x4 = pbX[:, 0:2 * n].rearrange("p (j s t) -> p j s t", j=2, s=nseq)
            tk.op(ACT, lambda e, x4=x4: e.activation(out=xrp[:, :, :, LH:LH + tlen], in_=x4, func=AF.Copy), pbXb, XRP.b)
            tk.op(ACT, lambda e, x4=x4: e.activation(out=xrpb[:, :, :, LH:LH + tlen], in_=x4, func=AF.Copy), pbXb, XRPB.b)
            if isp:
                if g.last:
                    tk.dma(SP, plconvT[l, j0 * 128:(j0 + 2) * 128, :].rearrange("(j p) r -> p j r", p=128), xrp[:, :, 0, tlen:tlen + LH], XRP.b, [], is_output=True)
                else:
                    tk.op(POOL, lambda e: e.tensor_copy(out=HISTLt[:, j0:j0 + 2, :], in_=xrp[:, :, 0, tlen:tlen + LH]), XRP.b, [HISTLb[jp]])
            else:
                tk.op(POOL, lambda e, stl=stl: e.tensor_copy(out=stl, in_=xrp[:, :, :, tlen:tlen + LH]), XRP.b, STL.b)
                tk.dma(SP, slconvT[l, j0 * 128:(j0 + 2) * 128].rearrange("(j p) b r -> p j (b r)", p=128),
                       STL.f().rearrange("p (j q) -> p j q", j=2), STL.b, [], is_output=True)
            pbx, pbxb = bank()
            for s in range(2):
                j = j0 + s
                tk.group(PE, [(lambda e, k=k, s=s, j=j: e.matmul(pbx[:, s * n:(s + 1) * n], lhsT=DGRt[rl][:, j * LK + k, :], rhs=xrpb[:, s, :, k:k + tlen],
                                                               start=(k == 0), stop=(k == LK - 1))) for k in range(LK)],
                         [DGRb[rl]] + XRPB.b, pbxb)
            for s in range(2):
                j = j0 + s
                tk.op(ACT, lambda e, s=s, j=j, pbx=pbx: e.activation(out=xc[:, j, :], in_=pbx[:, s * n:(s + 1) * n], func=AF.Identity,
                                                                    bias=P7[:, j, R_LCB:R_LCB + 1], scale=1.0), pbxb + [P7B], XC.b)
        tk.op(ACT, lambda e: e.activation(out=xcb, in_=xc, func=AF.Copy), XC.b, XCB.b)
        blk_of = {}
        for bi, (mo, kc) in enumerate(GBLK):
            blk_of.setdefault(mo, []).append((bi, kc))
        sets = []
        for si in range(2):
            base = 17 + 8 * si
            sets.append((TV(base, 2 * n), TV(base, 2 * n, eoff=2 * n), TV(base + 4, 2 * n), TV(base + 6, 2 * n)))
        for bt in range(3):
            m0 = 2 * bt
            RGv, IGv, A2v, HHv = sets[bt % 2]
            rg = RGv.f().rearrange("p (j t) -> p j t", j=2); ig = IGv.f().rearrange("p (j t) -> p j t", j=2)
            hhv = HHv.f().rearrange("p (j t) -> p j t", j=2)
            for q in range(2):
                mo = m0 + q
                pb, pbb = bank()
                lst = blk_of[mo]
                tk.group(PE, [(lambda e, bi=bi, kc=kc, i=i: e.matmul(pb[:, 0:n], lhsT=WAt[:, bi, :], rhs=xcb[:, kc, :], start=(i == 0), stop=(i == len(lst) - 1)))
                              for i, (bi, kc) in enumerate(lst)], [WAb] + XCB.b, pbb)
                tk.group(PE, [(lambda e, bi=bi, kc=kc, i=i: e.matmul(pb[:, n:2 * n], lhsT=WXt[:, bi, :], rhs=xcb[:, kc, :], start=(i == 0), stop=(i == len(lst) - 1)))
                              for i, (bi, kc) in enumerate(lst)], [WXb] + XCB.b, pbb)
                tk.op(ACT, lambda e, mo=mo, q=q, pb=pb: e.activation(out=rg[:, q, :], in_=pb[:, 0:n], func=AF.Exp, bias=NEGt[:, mo, 0:1], scale=-1.0), pbb + [NEGb], RGv.b)
                tk.op(ACT, lambda e, mo=mo, q=q, pb=pb: e.activation(out=ig[:, q, :], in_=pb[:, n:2 * n], func=AF.Exp, bias=NEGt[:, mo, 1:2], scale=-1.0), pbb + [NEGb], IGv.b)
            RI = TV(17 + 8 * (bt % 2), 4 * n)
            tk.op(ACT, lambda e, RI=RI: e.activation(out=RI.f(), in_=RI.f(), func=AF.Ln, bias=ONEt[:], scale=1.0), RI.b + [CONSTb], RI.b)
            tk.op(ACT, lambda e, RI=RI: e.activation(out=RI.f(), in_=RI.f(), func=AF.Exp, scale=-1.0), RI.b, RI.b)
            for q in range(2):
                mo = m0 + q
                tk.op(ACT, lambda e, mo=mo, q=q: e.activation(out=rg[:, q, :], in_=rg[:, q, :], func=AF.Exp, scale=CVt[:, mo:mo + 1]), RGv.b + [CVb], RGv.b)
            tk.op(DVE, lambda e: e.tensor_tensor(out=A2v.f(), in0=RGv.f(), in1=RGv.f(), op=ALU.mult), RGv.b, A2v.b)
            tk.op(ACT, lambda e: e.activation(out=A2v.f(), in_=A2v.f(), func=AF.Ln, bias=ONEt[:], scale=-1.0), A2v.b + [CONSTb], A2v.b)
            tk.op(ACT, lambda e: e.activation(out=A2v.f(), in_=A2v.f(), func=AF.Exp, scale=0.5), A2v.b, A2v.b)
            tk.op(DVE, lambda e, m0=m0: e.tensor_tensor(out=ig, in0=xc[:, m0:m0 + 2, :], in1=ig, op=ALU.mult), IGv.b + XC.b, IGv.b)
            tk.op(DVE, lambda e: e.tensor_tensor(out=IGv.f(), in0=IGv.f(), in1=A2v.f(), op=ALU.mult), IGv.b + A2v.b, IGv.b)
            for q in range(2):
                mo = m0 + q
                aq = rg[:, q, :]; bq = ig[:, q, :]; hq = hhv[:, q, :]
                if isp:
                    if g.first:
                        init = 0.0; rdi = []
                    else:
                        init = HSTt[:, mo:mo + 1]; rdi = [HSTb[mo]]
                else:
                    aa3 = v3(aq); bx3 = v3(bq)
                    tk.op(DVE, lambda e, mo=mo, aa3=aa3: e.tensor_tensor(out=T0.f(), in0=aa3[:, :, 0], in1=H0t[:, mo, :], op=ALU.mult), RGv.b + H0v.b, T0.b)
                    tk.op(DVE, lambda e, bx3=bx3: e.tensor_tensor(out=bx3[:, :, 0], in0=bx3[:, :, 0], in1=T0.f(), op=ALU.add), IGv.b + T0.b, IGv.b)
                    tk.op(DVE, lambda e, aa3=aa3: e.memset(aa3[:, :, 0], 0.0), RGv.b, RGv.b)
                    init = 0.0; rdi = []
                tk.op(DVE, lambda e, init=init, aq=aq, bq=bq, hq=hq: e.tensor_tensor_scan(out=hq, data0=aq, data1=bq, initial=init, op0=ALU.mult, op1=ALU.add),
                      RGv.b + IGv.b + rdi, HHv.b)
                if isp:
                    tk.op(POOL, lambda e, mo=mo, hq=hq: e.tensor_copy(out=HSTt[:, mo:mo + 1], in_=hq[:, n - 1:n]), HHv.b, [HSTb[mo]])
                else:
                    tk.op(POOL, lambda e, mo=mo, hq=hq: e.tensor_copy(out=SHOt[:, mo, :], in_=v3(hq)[:, :, tlen - 1]), HHv.b, SHOv.b)
            sgs = SG2s[bt]
            tk.op(DVE, lambda e, m0=m0, sgs=sgs: e.tensor_tensor(out=CATt[:, 6 + m0:6 + m0 + 2, 0:n], in0=hhv, in1=sgs.f().rearrange("p (j t) -> p j t", j=2), op=ALU.mult),
                  HHv.b + sgs.b, CATb[6 + m0:6 + m0 + 2])
        if isp:
            if g.last:
                tk.dma(SP, phT[l].rearrange("(j p) o -> p (j o)", p=128), HSTt[:], HSTb, [], is_output=True, slow=True)
        else:
            tk.dma(SP, shT[l].rearrange("(j p) b -> p j b", p=128), SHOt, SHOv.b, [], is_output=True)

        if after_lru is not None:
            after_lru()
        ck(6)
        QT = TV(B0 + 0, 2 * n); SGQ = TV(B0 + 2, 4 * n)
        HB = max(1, min(NH, 512 // (2 * n)))
        PT = [TV(B0 + 6, HB * n), TV(B0 + 7, HB * n)]
        RD = TV(B0 + 8, HB * n); OT = TV(B0 + 9, HB * n)
        qt = QT.h().rearrange("p (h t) -> p h t", h=NH)
        sgq = SGQ.f().rearrange("p (h t) -> p h t", h=NH)
        for jp in range(2):
            pbQ, pbQb = proj_pair("q", 2 * jp, n)
            tk.op(ACT, lambda e, jp=jp: e.activation(out=qt[:, 2 * jp:2 * jp + 2, :], in_=pbQ[:, 0:2 * n].rearrange("p (h t) -> p h t", h=2), func=AF.Copy),
                  pbQb, QT.b)
            pbG, pbGb = proj_pair("gq", 2 * jp, n)
            g3 = pbG[:, 0:2 * n].rearrange("p (h t) -> p h t", h=2)
            sigmoid_act(sgq[:, 2 * jp:2 * jp + 2, :], SGQ.b, g3, pbGb)
            tk.op(DVE, lambda e, jp=jp, g3=g3: e.tensor_tensor(out=sgq[:, 2 * jp:2 * jp + 2, :], in0=sgq[:, 2 * jp:2 * jp + 2, :], in1=g3, op=ALU.mult),
                  SGQ.b + pbGb, SGQ.b)
        sc = 1.0 / math.sqrt(128.0)
        if not isp:
            kring = []
            vring = []
            for i in range(4):
                kv_ = TV(15 + 2 * i, NH * NMEM // 2)
                kring.append((kv_.h().rearrange("p (h m) -> p h m", h=NH), kv_.b))
                vv_ = TV(23 + 2 * i, 2 * MW // 2)
                vring.append((vv_.h().rearrange("p (c d) -> p c d", c=2), vv_.b))
            NR = len(kring)
        for hg in range(NH // HB):
            h0 = hg * HB
            pbS, pbSb = bank()
            pbO, pbOb = bank()
            sS = pbS[:, 0:HB * 2 * n].rearrange("p (h c t) -> p h c t", h=HB, c=2)
            sO = pbO[:, 0:HB * 2 * n].rearrange("p (h c t) -> p h c t", h=HB, c=2)
            pt = PT[hg % 2]
            ptv = pt.h().rearrange("p (h c t) -> p h c t", h=HB, c=2)
            if isp:
                for hh in range(HB):
                    h = h0 + hh
                    for mc in range(2):
                        tk.group(PE, [lambda e, hh=hh, h=h, mc=mc: e.matmul(sS[:, hh, mc, 0:n], lhsT=KTPt[:, h, mc * 128:(mc + 1) * 128], rhs=qt[:, h, 0:n],
                                                                          start=True, stop=True)], [KTPb] + QT.b, pbSb)
            else:
                for b in range(NB):
                    kap, kb_ = kring[b % NR]
                    tk.dma(POOL, kap, ckT[l, b].rearrange("(h d) m -> d h m", d=128), [], kb_)
                    c0, c1 = b * TS, (b + 1) * TS
                    fns = []
                    for hh in range(HB):
                        h = h0 + hh
                        for mc in range(2):
                            fns.append(lambda e, hh=hh, h=h, mc=mc, kap=kap, c0=c0, c1=c1: e.matmul(sS[:, hh, mc, c0:c1], lhsT=kap[:, h, mc * 128:(mc + 1) * 128],
                                                                                                  rhs=qt[:, h, c0:c1], start=True, stop=True))
                    tk.group(PE, fns, kb_ + QT.b, pbSb)
            tk.op(ACT, lambda e: e.activation(out=ptv, in_=sS, func=AF.Exp, scale=sc), pbSb, pt.b)
            for hh in range(HB):
                tk.group(PE, [(lambda e, hh=hh, mc=mc: e.matmul(sO[:, hh, 0, :], lhsT=ONESt[:], rhs=ptv[:, hh, mc, :], start=(mc == 0), stop=(mc == 1)))
                              for mc in range(2)], pt.b + [ONESb], pbOb)
            if isp:
                for hh in range(HB):
                    h = h0 + hh
                    tk.group(PE, [(lambda e, hh=hh, h=h, mc=mc: e.matmul(sO[:, hh, 1, 0:n], lhsT=VPt[:, mc, h * 128:(h + 1) * 128], rhs=ptv[:, hh, mc, 0:n],
                                                                       start=(mc == 0), stop=(mc == 1))) for mc in range(2)], [VPb] + pt.b, pbOb)
            else:
                for b in range(NB):
                    vap, vb_ = vring[b % NR]
                    tk.dma(POOL, vap, cv[l, b].rearrange("(c p) d -> p c d", p=128), [], vb_)
                    c0, c1 = b * TS, (b + 1) * TS
                    for hh in range(HB):
                        h = h0 + hh
                        tk.group(PE, [(lambda e, hh=hh, h=h, mc=mc, vap=vap, c0=c0, c1=c1: e.matmul(sO[:, hh, 1, c0:c1], lhsT=vap[:, mc, h * 128:(h + 1) * 128],
                                                                                                  rhs=ptv[:, hh, mc, c0:c1], start=(mc == 0), stop=(mc == 1)))
                                      for mc in range(2)], vb_ + pt.b, pbOb)
            rd = RD.f().rearrange("p (h t) -> p h t", h=HB)
            ot = OT.f().rearrange("p (h t) -> p h t", h=HB)
            tk.op(DVE, lambda e: e.reciprocal(out=rd, in_=sO[:, :, 0, :]), pbOb, RD.b)
            tk.op(DVE, lambda e: e.tensor_tensor(out=ot, in0=sO[:, :, 1, :], in1=rd, op=ALU.mult), pbOb + RD.b, OT.b)
            tk.op(DVE, lambda e, h0=h0: e.tensor_tensor(out=CATt[:, 12 + h0:12 + h0 + HB, 0:n], in0=ot, in1=sgq[:, h0:h0 + HB, :], op=ALU.mult),
                  OT.b + SGQ.b, CATb[12 + h0:12 + h0 + HB])

    def stage_C(l, g):
        pi = l % 2
        n = g.n
        xi = g.xi
        X = XBt[xi]
        O32 = TV(25, 8 * SL); OSQ = TV(21, 4 * SL); R2 = TV(20, n)
        o32 = O32.f().rearrange("p (k t) -> p k t", k=KD)
        osq = OSQ.h().rearrange("p (k t) -> p k t", k=KD)
        for dp in range(4):
            pb, pbb = bank()
            for s in range(2):
                d = 2 * dp + s
                fns = []
                for kc in range(16):
                    si, kl = (0, kc) if kc < 6 else ((1, kc - 6) if kc < 12 else (2, kc - 12))
                    fns.append(lambda e, kc=kc, si=si, kl=kl, d=d, s=s: e.matmul(pb[:, s * n:(s + 1) * n], lhsT=WOt[si][:, kl, d * 128:(d + 1) * 128],
                                                                               rhs=CATt[:, kc, 0:n], start=(kc == 0), stop=(kc == 15)))
                tk.group(PE, fns, WOb + CATb, pbb)
            pv = pb[:, 0:2 * n].rearrange("p (s t) -> p s t", s=2)
            tk.op(ACT, lambda e, dp=dp, pv=pv: e.activation(out=o32[:, 2 * dp:2 * dp + 2, 0:n], in_=pv, func=AF.Copy), pbb, O32.b)
            tk.op(ACT, lambda e, dp=dp, pv=pv: e.activation(out=osq[:, 2 * dp:2 * dp + 2, 0:n], in_=pv, func=AF.Square), pbb, OSQ.b)
        pb, pbb = bank()
        tk.group(PE, [(lambda e, kc=kc: e.matmul(pb[:, 0:n], lhsT=ONESt[:], rhs=osq[:, kc, 0:n], start=(kc == 0), stop=(kc == KD - 1)))
                      for kc in range(KD)], OSQ.b + [ONESb], pbb)
        rsqrt_act(R2, pb[:, 0:n], pbb, 1.0 / D)
        tk.op(DVE, lambda e: e.tensor_tensor(out=o32[:, :, 0:n], in0=o32[:, :, 0:n], in1=R2.f().unsqueeze(1).broadcast_to([128, KD, n]), op=ALU.mult),
              O32.b + R2.b, O32.b)
        for d in range(KD):
            tk.op(DVE, lambda e, d=d: e.scalar_tensor_tensor(out=X[:, d, 0:n], in0=o32[:, d, 0:n], scalar=P10t[pi][:, d, 1:2], in1=X[:, d, 0:n],
                                                           op0=ALU.mult, op1=ALU.add), O32.b + [XBb[xi], P10b[pi]], [XBb[xi]])
        if l == L - 1:
            dst = (ypT[:, g.c0:g.c0 + n] if g.kind == "p" else ysT[:, 0:n])
            tk.dma(SP, dst.rearrange("(k p) t -> p k t", p=128), X[:, :, 0:n], [XBb[xi]], [], is_output=True)
        else:
            tk.dma(SP, xscr[l % 2][:, g.c0:g.c0 + n].rearrange("(k p) t -> p k t", p=128), X[:, :, 0:n], [XBb[xi]], [scrb[l % 2][g.idx]])

    def ck(level):
        if cfg.upto < level:
            raise StopEmit()

    def emit_all():
        load_params(0)
        load_weights(0)
        ck(1)
        diag_chunks(0, range(6))
        order = [groups[-1]] + groups[:-1]

        def reload(l, names):
            for nm_ in names:
                if nm_ == "WA":
                    tk.dma(POOL, WAt[:], wab[l], [], [WAb])
                elif nm_ == "WX":
                    tk.dma(POOL, WXt[:], wxb[l], [], [WXb])
                elif nm_ == "WO":
                    r0 = 0
                    for i, n_ in enumerate((6, 6, 4)):
                        tk.dma(POOL, WOt[i][:], w_out[l, r0:r0 + n_ * 128, :].rearrange("(k p) c -> p k c", p=128), [], [WOb[i]])
                        r0 += n_ * 128
                else:
                    (nm, c0, ncs) = [x for x in SEGS if x[0] == nm_][0]
                    tk.dma(POOL, Wt[nm][:], w_in[l, :, c0:c0 + ncs * 128].rearrange("(k p) c -> p k c", p=128), [], [Wb[nm]])

        for l in range(L):
            layer_consts(l)
            if l + 1 < L:
                load_params(l + 1)
            ck(2)
            mem_phase(l)
            if l > 0:
                reload(l, ["gc", "xr", "gr", "WA", "WX", "q", "gq"])
            ck(3)
            stage_A(l, order[0])
            for gi, g in enumerate(order):
                ck(4)
                ac = None; al = None
                last = (gi == NG - 1 and l + 1 < L)
                ngen = min(3, NG - 1)
                per = (6 + ngen - 1) // ngen
                dg = None
                if l + 1 < L and 1 <= gi <= ngen:
                    dg = (lambda gi=gi: diag_chunks(l + 1, range(per * (gi - 1), min(6, per * gi))))
                if last:
                    def ac(dg=dg):
                        if dg is not None:
                            dg()
                        reload(l + 1, ["b", "a"])
                    al = None
                else:
                    ac = None
                    al = dg
                stage_B(l, g, ck, after_conv=ac, after_lru=al)
                if gi == 0 and l > 0:
                    reload(l, ["WO"])
                if gi + 1 < NG:
                    stage_A(l, order[gi + 1])
                ck(8)
                stage_C(l, g)

    try:
        emit_all()
    except StopEmit:
        pass
    tk.finish()
    return nc


def host_inputs(inp, cfg, ncores):
    L, SEQ, NB = cfg.L, cfg.SEQ, cfg.NB
    f = lambda a: np.ascontiguousarray(np.asarray(a, dtype=np.float32))
    p768 = np.concatenate([
        np.asarray(inp["conv_w"])[:L], np.asarray(inp["conv_b"])[:L, None], np.asarray(inp["conv_ln_g"])[:L, None],
        np.asarray(inp["conv_ln_b"])[:L, None], np.asarray(inp["lru_conv_w"])[:L], np.asarray(inp["lru_conv_b"])[:L, None],
        np.asarray(inp["lru_ba"])[:L, None], np.asarray(inp["lru_bx"])[:L, None], np.asarray(inp["lru_lambda"])[:L, None]], axis=1)
    assert p768.shape[1] == NP7
    p768 = f(p768.transpose(0, 2, 1))
    p1024 = f(np.stack([np.asarray(inp["norm_pre_g"])[:L], np.asarray(inp["norm_post_g"])[:L], np.asarray(inp["mem_norm_g"])[:L]], axis=2))

    def blocks(w):
        w = np.asarray(w)[:L]
        bd = np.zeros((L, LW, LW), np.float32)
        for h in range(8):
            bd[:, 96 * h:96 * h + 96, 96 * h:96 * h + 96] = w[:, h]
        out = np.zeros((L, 128, NBLK, 128), np.float32)
        for bi, (mo, kc) in enumerate(GBLK):
            out[:, :, bi, :] = bd[:, kc * 128:(kc + 1) * 128, mo * 128:(mo + 1) * 128]
        return out

    shared = {
        "w_in": f(np.asarray(inp["w_in"])[:L]), "w_out": f(np.asarray(inp["w_out"])[:L]),
        "wk": f(np.asarray(inp["w_mem_k"])[:L]), "wv": f(np.asarray(inp["w_mem_v"])[:L]),
        "wab": blocks(inp["lru_wa"]), "wxb": blocks(inp["lru_wx"]), "p768": p768, "p1024": p1024,
        "ident": np.eye(128, dtype=np.float32),
    }
    xp = np.asarray(inp["x_prompt"]); xs = np.asarray(inp["x_sample"]); mem = np.asarray(inp["mem_prompt"])
    cc = np.asarray(inp["cache_conv"]); cl = np.asarray(inp["cache_lru_conv"]); h0 = np.asarray(inp["state_lru_h"])
    ck = np.asarray(inp["cache_mem_k"]); cvv = np.asarray(inp["cache_mem_v"])
    maps = []
    for i in range(ncores):
        sl = slice(i * NB, (i + 1) * NB)
        m = dict(shared)
        m["xpT"] = f(xp[i, :SEQ].T)
        m["xsT"] = f(xs[sl].reshape(NB * TS, D).T)
        m["memT"] = f(mem[i].T)
        m["cconvT"] = f(cc[:L, sl].transpose(0, 3, 1, 2))
        m["clruT"] = f(cl[:L, sl].transpose(0, 3, 1, 2))
        m["h0T"] = f(h0[:L, sl].transpose(0, 2, 1))
        m["ckT"] = f(ck[:L, sl].reshape(L, NB, NMEM, MW).transpose(0, 1, 3, 2))
        m["cv"] = f(cvv[:L, sl].reshape(L, NB, NMEM, MW))
        maps.append(m)
    return maps


def host_outputs(results, cfg, ncores):
    L, SEQ, NB = cfg.L, cfg.SEQ, cfg.NB
    yp = np.stack([r["ypT"].T for r in results])
    ys = np.concatenate([r["ysT"].T.reshape(NB, TS, D) for r in results])
    pconv = np.stack([r["pconvT"].transpose(0, 2, 1) for r in results], axis=1)
    plconv = np.stack([r["plconvT"].transpose(0, 2, 1) for r in results], axis=1)
    ph = np.stack([r["phT"][:, :, 0] for r in results], axis=1)
    pmk = np.stack([r["pmkT"].transpose(0, 2, 1).reshape(L, NMEM, NH, 128) for r in results], axis=1)
    pmv = np.stack([r["pmv"].reshape(L, NMEM, NH, 128) for r in results], axis=1)
    sconv = np.concatenate([r["sconvT"].transpose(0, 2, 3, 1) for r in results], axis=1)
    slconv = np.concatenate([r["slconvT"].transpose(0, 2, 3, 1) for r in results], axis=1)
    sh = np.concatenate([r["shT"].transpose(0, 2, 1) for r in results], axis=1)
    outs = (yp, ys, pconv, plconv, ph, pmk, pmv, sconv, slconv, sh)
    return tuple(np.ascontiguousarray(o.astype(np.float32)) for o in outs)


_CACHE = {}


def kernel(**inputs):
    cfg = Cfg()
    ncores = 8
    if "nc" not in _CACHE:
        _CACHE["nc"] = build(cfg)
    nc = _CACHE["nc"]
    maps = host_inputs(inputs, cfg, ncores)
    res = run_bass_kernel_spmd(nc, maps, core_ids=list(range(ncores)))
    return host_outputs(res.results, cfg, ncores)
```
